# Optimizing a Trainium2 kernel written in Bass

```python
import math
import jax, jax.numpy as jnp
from jax import lax
import numpy as np

D_MODEL = 1024
BATCH = 32
SEQ = 2048
DEPTH = 2
DEC_BATCH = 32
DEC_SEQ = 64
PAST_LEN = 4096

CHUNK = 64
Q_BLOCK = 128
ROPE_THETA = 10000.0
EPS = 1e-6
NEG_INF = -1e30
HEAD_DIM = D_MODEL // 16
D_FF = 11 * D_MODEL // 4
H_A = 4
Q_LORA = 4 * HEAD_DIM
KV_LORA = 2 * HEAD_DIM
NOPE_A = HEAD_DIM
ROPE_A = HEAD_DIM // 2
V_A = HEAD_DIM
H_B = 4
DH_B = HEAD_DIM
H_IDX = 8
D_IDX = HEAD_DIM
TOPK_MAX = 256
H_C = 4
DH_C = HEAD_DIM
H_D = 4
DQK_D = HEAD_DIM // 2
DV_D = HEAD_DIM
MIX_WIDTH = H_A * V_A + H_B * DH_B + H_C * DH_C + H_D * DV_D
COL_SIZES = (Q_LORA, KV_LORA, ROPE_A,
             H_B * DH_B, DH_B, DH_B, H_IDX * D_IDX, D_IDX, H_IDX,
             H_C * DH_C, H_C * DH_C, H_C * DH_C,
             H_D * 2 * DQK_D, H_D * 2 * DQK_D, H_D * DV_D)
IN_COLS = sum(COL_SIZES)

kernel_name = 'hybrid_streaming_encoder_step'


def _rmsnorm(x, g):
    xf = x.astype(jnp.float32)
    y = xf * lax.rsqrt(jnp.mean(xf * xf, axis=-1, keepdims=True) + EPS)
    return (y * g.astype(jnp.float32)).astype(x.dtype)


def _rope(x, pos):
    d = x.shape[-1]
    half = d // 2
    freqs = ROPE_THETA ** (-jnp.arange(half, dtype=jnp.float32) * 2.0 / d)
    ang = pos.astype(jnp.float32)[:, None] * freqs[None, :]
    shp = (1, pos.shape[0]) + (1,) * (x.ndim - 3) + (half,)
    cos = jnp.cos(ang).reshape(shp)
    sin = jnp.sin(ang).reshape(shp)
    xf = x.astype(jnp.float32)
    x1, x2 = xf[..., :half], xf[..., half:]
    return jnp.concatenate([x1 * cos - x2 * sin, x2 * cos + x1 * sin], axis=-1).astype(x.dtype)


def _swiglu(x, wg, wu, wd):
    return (jax.nn.silu(x @ wg) * (x @ wu)) @ wd


def _chunk_mask(q_pos, k_pos):
    return (k_pos[None, :] // CHUNK) <= (q_pos[:, None] // CHUNK)


def _with_past(past, new):
    return new if past is None else jnp.concatenate([past.astype(new.dtype), new], axis=1)


def _over_query_blocks(fn, q_pos, *q_arrays):
    T = q_pos.shape[0]
    if T <= Q_BLOCK:
        return fn(q_pos, *q_arrays)
    nb = T // Q_BLOCK
    split = lambda a: jnp.moveaxis(a.reshape((a.shape[0], nb, Q_BLOCK) + a.shape[2:]), 1, 0)
    xs = (q_pos.reshape(nb, Q_BLOCK),) + tuple(split(a) for a in q_arrays)
    out = lax.map(lambda args: fn(*args), xs)
    out = jnp.moveaxis(out, 0, 1)
    return out.reshape((out.shape[0], T) + out.shape[3:])


def _softmax_attend(q, k, v, mask):
    s = jnp.einsum('bqhd,bkhd->bhqk', q, k).astype(jnp.float32) * (q.shape[-1] ** -0.5)
    p = jax.nn.softmax(jnp.where(mask[None, None], s, NEG_INF), axis=-1).astype(v.dtype)
    return jnp.einsum('bhqk,bkhd->bqhd', p, v)


def _mla(cq, ckv, kr_raw, past, q_pos, k_pos, q_norm, w_qb, kv_norm, w_kvb):
    B, T = cq.shape[:2]
    q = (_rmsnorm(cq, q_norm) @ w_qb).reshape(B, T, H_A, NOPE_A + ROPE_A)
    q = jnp.concatenate([q[..., :NOPE_A], _rope(q[..., NOPE_A:], q_pos)], axis=-1)
    new_rows = jnp.concatenate([_rmsnorm(ckv, kv_norm), _rope(kr_raw, q_pos)], axis=-1)
    rows = _with_past(past, new_rows)
    L = rows.shape[1]
    kv = (rows[..., :KV_LORA] @ w_kvb).reshape(B, L, H_A, NOPE_A + V_A)
    k = jnp.concatenate([kv[..., :NOPE_A],
                         jnp.broadcast_to(rows[..., None, KV_LORA:], (B, L, H_A, ROPE_A))], axis=-1)
    v = kv[..., NOPE_A:]
    out = _over_query_blocks(lambda qp, qb: _softmax_attend(qb, k, v, _chunk_mask(qp, k_pos)), q_pos, q)
    return out.reshape(B, T, H_A * V_A), new_rows


def _dsa(q, k, v, qi, ki, wi, past, q_pos, k_pos):
    B, T = q.shape[:2]
    q = _rope(q.reshape(B, T, H_B, DH_B), q_pos)
    qi = _rope(qi.reshape(B, T, H_IDX, D_IDX), q_pos)
    wi = wi * (H_IDX ** -0.5)
    new_rows = jnp.concatenate([_rope(k, q_pos), v, _rope(ki, q_pos)], axis=-1)
    rows = _with_past(past, new_rows)
    L = rows.shape[1]
    k_all = rows[..., :DH_B]
    v_all = rows[..., DH_B:2 * DH_B]
    ki_all = rows[..., 2 * DH_B:]
    n_sel = min(TOPK_MAX, L // 4)
    gather = jax.vmap(lambda r, i: r[i])

    def block(qp, qb, qib, wib):
        mask = _chunk_mask(qp, k_pos)
        score = jax.nn.relu(jnp.einsum('bqhd,bkd->bqhk', qib, ki_all).astype(jnp.float32))
        score = jnp.einsum('bqhk,bqh->bqk', score, wib.astype(jnp.float32))
        score = jnp.where(mask[None], score, -jnp.inf)
        _, idx = lax.top_k(score, n_sel)
        valid = (k_pos[idx] // CHUNK) <= (qp[None, :, None] // CHUNK)
        ks = gather(k_all, idx)
        vs = gather(v_all, idx)
        s = jnp.einsum('bqhd,bqkd->bhqk', qb, ks).astype(jnp.float32) * (DH_B ** -0.5)
        p = jax.nn.softmax(jnp.where(valid[:, None], s, NEG_INF), axis=-1).astype(vs.dtype)
        return jnp.einsum('bhqk,bqkd->bqhd', p, vs)

    out = _over_query_blocks(block, q_pos, q, qi, wi)
    return out.reshape(B, T, H_B * DH_B), new_rows


def _stick_breaking(q, k, v, past, q_pos, k_pos):
    B, T = q.shape[:2]
    q = q.reshape(B, T, H_C, DH_C)
    new_rows = jnp.concatenate([k.reshape(B, T, H_C, DH_C), v.reshape(B, T, H_C, DH_C)], axis=-1)
    rows = _with_past(past, new_rows)
    k_all = rows[..., :DH_C]
    v_all = rows[..., DH_C:]

    def block(qp, qb):
        z = jnp.einsum('bqhd,bkhd->bhqk', qb, k_all).astype(jnp.float32) * (DH_C ** -0.5)
        mask = (k_pos[None, :] < qp[:, None])[None, None]
        log_keep = jnp.where(mask, jax.nn.log_sigmoid(-z), 0.0)
        log_after = lax.cumsum(log_keep, axis=3, reverse=True) - log_keep
        a = jnp.where(mask, jnp.exp(jax.nn.log_sigmoid(z) + log_after), 0.0)
        return jnp.einsum('bhqk,bkhd->bqhd', a.astype(v_all.dtype), v_all)

    out = _over_query_blocks(block, q_pos, q)
    return out.reshape(B, T, H_C * DH_C), new_rows


def _diff_attn(q, k, v, past, q_pos, k_pos, lam, sub_norm, lam_init):
    B, T = q.shape[:2]
    q = _rope(q.reshape(B, T, H_D, 2, DQK_D), q_pos)
    k = _rope(k.reshape(B, T, H_D, 2, DQK_D), q_pos).reshape(B, T, H_D, 2 * DQK_D)
    new_rows = jnp.concatenate([k, v.reshape(B, T, H_D, DV_D)], axis=-1)
    rows = _with_past(past, new_rows)
    L = rows.shape[1]
    k_all = rows[..., :2 * DQK_D].reshape(B, L, H_D, 2, DQK_D)
    v_all = rows[..., 2 * DQK_D:]
    lf = lam.astype(jnp.float32)
    lam_full = jnp.exp(jnp.sum(lf[0] * lf[1])) - jnp.exp(jnp.sum(lf[2] * lf[3])) + lam_init

    def block(qp, qb):
        s = jnp.einsum('bqhmd,bkhmd->bhmqk', qb, k_all).astype(jnp.float32) * (DQK_D ** -0.5)
        p = jax.nn.softmax(jnp.where(_chunk_mask(qp, k_pos)[None, None, None], s, NEG_INF), axis=-1)
        w = p[:, :, 0] - lam_full * p[:, :, 1]
        return jnp.einsum('bhqk,bkhd->bqhd', w.astype(v_all.dtype), v_all)

    out = _over_query_blocks(block, q_pos, q)
    out = _rmsnorm(out, sub_norm) * (1.0 - lam_init)
    return out.reshape(B, T, H_D * DV_D), new_rows


def _trunk(x, caches, q_pos, k_pos, params):
    (n1, g1, u1, d1, nm, w_in, qn, wqb, kvn, wkvb, lam, dn, wo, n2, g2, u2, d2, fnorm) = params
    split_at = np.cumsum(COL_SIZES)[:-1].tolist()
    rows_a, rows_b, rows_c, rows_d = [], [], [], []
    for l in range(DEPTH):
        if caches is None:
            pa = pb = pc = pd = None
        else:
            pa, pb, pc, pd = caches[0][l], caches[1][l], caches[2][l], caches[3][l]
        lam_init = 0.8 - 0.6 * math.exp(-0.3 * l)
        x = x + 0.5 * _swiglu(_rmsnorm(x, n1[l]), g1[l], u1[l], d1[l])
        (cq, ckv, kr, qb, kb, vb, qib, kib, wib, qc, kc, vc, qd, kd, vd) = jnp.split(
            _rmsnorm(x, nm[l]) @ w_in[l], split_at, axis=-1)
        oa, ra = _mla(cq, ckv, kr, pa, q_pos, k_pos, qn[l], wqb[l], kvn[l], wkvb[l])
        ob, rb = _dsa(qb, kb, vb, qib, kib, wib, pb, q_pos, k_pos)
        oc, rc = _stick_breaking(qc, kc, vc, pc, q_pos, k_pos)
        od, rd = _diff_attn(qd, kd, vd, pd, q_pos, k_pos, lam[l], dn[l], lam_init)
        x = x + jnp.concatenate([oa, ob, oc, od], axis=-1) @ wo[l]
        x = x + 0.5 * _swiglu(_rmsnorm(x, n2[l]), g2[l], u2[l], d2[l])
        rows_a.append(ra)
        rows_b.append(rb)
        rows_c.append(rc)
        rows_d.append(rd)
    return (_rmsnorm(x, fnorm), jnp.stack(rows_a), jnp.stack(rows_b), jnp.stack(rows_c), jnp.stack(rows_d))


def setup_inputs(seed: int = 0) -> dict:
    key = jax.random.key(seed)
    ks = iter(jax.random.split(key, 32))
    nrm = lambda shape, scale=1.0: scale * jax.random.normal(next(ks), shape, jnp.float32)
    gain = lambda shape: 1.0 + 0.01 * jax.random.normal(next(ks), shape, jnp.float32)
    return {
        'x_prompt': nrm((BATCH, SEQ, D_MODEL)),
        'x_sample': nrm((DEC_BATCH, DEC_SEQ, D_MODEL)),
        'cache_mla': nrm((DEPTH, DEC_BATCH, PAST_LEN, KV_LORA + ROPE_A)),
        'cache_dsa': nrm((DEPTH, DEC_BATCH, PAST_LEN, 2 * DH_B + D_IDX)),
        'cache_sb': nrm((DEPTH, DEC_BATCH, PAST_LEN, H_C, 2 * DH_C)),
        'cache_diff': nrm((DEPTH, DEC_BATCH, PAST_LEN, H_D, 2 * DQK_D + DV_D)),
        'norm_ff1': gain((DEPTH, D_MODEL)),
        'w_ff1_gate': nrm((DEPTH, D_MODEL, D_FF), D_MODEL ** -0.5),
        'w_ff1_up': nrm((DEPTH, D_MODEL, D_FF), D_MODEL ** -0.5),
        'w_ff1_down': nrm((DEPTH, D_FF, D_MODEL), D_FF ** -0.5),
        'norm_mix': gain((DEPTH, D_MODEL)),
        'w_in': nrm((DEPTH, D_MODEL, IN_COLS), D_MODEL ** -0.5),
        'mla_q_norm': gain((DEPTH, Q_LORA)),
        'mla_w_qb': nrm((DEPTH, Q_LORA, H_A * (NOPE_A + ROPE_A)), Q_LORA ** -0.5),
        'mla_kv_norm': gain((DEPTH, KV_LORA)),
        'mla_w_kvb': nrm((DEPTH, KV_LORA, H_A * (NOPE_A + V_A)), KV_LORA ** -0.5),
        'diff_lambda': nrm((DEPTH, 4, DQK_D), 0.1),
        'diff_norm': gain((DEPTH, DV_D)),
        'w_out': nrm((DEPTH, MIX_WIDTH, D_MODEL), MIX_WIDTH ** -0.5),
        'norm_ff2': gain((DEPTH, D_MODEL)),
        'w_ff2_gate': nrm((DEPTH, D_MODEL, D_FF), D_MODEL ** -0.5),
        'w_ff2_up': nrm((DEPTH, D_MODEL, D_FF), D_MODEL ** -0.5),
        'w_ff2_down': nrm((DEPTH, D_FF, D_MODEL), D_FF ** -0.5),
        'final_norm': gain((D_MODEL,)),
    }


def reference(x_prompt, x_sample, cache_mla, cache_dsa, cache_sb, cache_diff,
              norm_ff1, w_ff1_gate, w_ff1_up, w_ff1_down, norm_mix, w_in,
              mla_q_norm, mla_w_qb, mla_kv_norm, mla_w_kvb, diff_lambda, diff_norm, w_out,
              norm_ff2, w_ff2_gate, w_ff2_up, w_ff2_down, final_norm):
    params = (norm_ff1, w_ff1_gate, w_ff1_up, w_ff1_down, norm_mix, w_in,
              mla_q_norm, mla_w_qb, mla_kv_norm, mla_w_kvb, diff_lambda, diff_norm, w_out,
              norm_ff2, w_ff2_gate, w_ff2_up, w_ff2_down, final_norm)
    p_pos = jnp.arange(x_prompt.shape[1], dtype=jnp.int32)
    y_prompt, mla_p, dsa_p, sb_p, diff_p = _trunk(x_prompt, None, p_pos, p_pos, params)
    past_len = cache_mla.shape[2]
    t_new = x_sample.shape[1]
    s_qpos = past_len + jnp.arange(t_new, dtype=jnp.int32)
    s_kpos = jnp.arange(past_len + t_new, dtype=jnp.int32)
    y_sample, mla_s, dsa_s, sb_s, diff_s = _trunk(
        x_sample, (cache_mla, cache_dsa, cache_sb, cache_diff), s_qpos, s_kpos, params)
    return (y_prompt, y_sample, mla_p, dsa_p, sb_p, diff_p, mla_s, dsa_s, sb_s, diff_s)
```

```python
import math
from contextlib import ExitStack
import numpy as np
import concourse.bass as bass
import concourse.mybir as mybir
from concourse.bass_utils import run_bass_kernel_spmd

F32 = mybir.dt.float32
BF16 = mybir.dt.bfloat16
AF = mybir.ActivationFunctionType
ALU = mybir.AluOpType
AX = mybir.AxisListType

D = 1024
DFF = 2816
HD = 64
CHUNK = 64
EPS = 1e-6
THETA = 10000.0
IN_COLS = 2920
NEGB = 30000.0
C_CQ, C_CKV, C_KR = 0, 256, 384
C_QB, C_KB, C_VB, C_QIB, C_KIB, C_WIB = 416, 672, 736, 800, 1312, 1376
C_QC, C_KC, C_VC = 1384, 1640, 1896
C_QD, C_KD, C_VD = 2152, 2408, 2664


class T:
    __slots__ = ("w", "r", "excl")

    def __init__(self, excl=False):
        self.w = None
        self.r = {}
        self.excl = excl


class Ring(list):
    pass


class Eng:
    def __init__(self, name, be, sem, is_pe=False):
        self.name = name
        self.be = be
        self.sem = sem
        self.cnt = 0
        self.clock = {}
        self.is_pe = is_pe


class K:
    def __init__(self, nc, stack, n_dma_sems=48):
        self.nc = nc
        self.es = {}
        for name, be in (("pe", nc.tensor), ("act", nc.scalar), ("dve", nc.vector),
                         ("pool", nc.gpsimd), ("sp", nc.sync)):
            sem = stack.enter_context(nc.semaphore("s_" + name))
            self.es[name] = Eng(name, be, sem, is_pe=(name == "pe"))
        self.pe, self.act, self.dve, self.pool, self.sp = (
            self.es[n] for n in ("pe", "act", "dve", "pool", "sp"))
        self.dsems = []
        for i in range(n_dma_sems):
            sem = stack.enter_context(nc.semaphore("s_dma%d" % i))
            self.dsems.append([("dma", i), sem, 0])
        self.dnext = 0
        self.dnext2 = 0
        self.semof = {e.name: e.sem for e in self.es.values()}
        for d in self.dsems:
            self.semof[d[0]] = d[1]
        self.nins = 0
        self.nwaits = 0

    def _need(self, eng, deps):
        for key, c in deps.items():
            if key == eng.name and eng.is_pe:
                continue
            if c > eng.clock.get(key, 0):
                eng.be.wait_ge(self.semof[key], c)
                eng.clock[key] = c
                self.nwaits += 1

    @staticmethod
    def _collect(reads, writes):
        deps = {}
        for t in reads:
            if t.w is not None:
                kk, c = t.w
                if c > deps.get(kk, 0):
                    deps[kk] = c
        for t in writes:
            if t.w is not None:
                kk, c = t.w
                if c > deps.get(kk, 0):
                    deps[kk] = c
            for kk, c in t.r.items():
                if c > deps.get(kk, 0):
                    deps[kk] = c
        return deps

    def op(self, eng, fn, reads=(), writes=()):
        ex = [t for t in reads if t.excl]
        if ex:
            reads = [t for t in reads if not t.excl]
            writes = list(writes) + ex
        deps = self._collect(reads, writes)
        self._need(eng, deps)
        ins = fn(eng.be)
        eng.cnt += 1
        ins.then_inc(eng.sem, 1)
        self.nins += 1
        me = (eng.name, eng.cnt)
        for t in writes:
            t.w = me
            t.r = {}
        for t in reads:
            if t.r.get(eng.name, 0) < eng.cnt:
                t.r[eng.name] = eng.cnt
        return ins

    def dma(self, eng, out, in_, reads=(), writes=()):
        half = len(self.dsems) // 2
        if eng.name == "pool":
            d = self.dsems[self.dnext]
            self.dnext = (self.dnext + 1) % half
        else:
            d = self.dsems[half + self.dnext2]
            self.dnext2 = (self.dnext2 + 1) % (len(self.dsems) - half)
        deps = self._collect(reads, writes)
        if d[2] > 0:
            deps[d[0]] = max(deps.get(d[0], 0), d[2])
        self._need(eng, deps)
        ins = eng.be.dma_start(out=out, in_=in_)
        d[2] += 16
        ins.then_inc(d[1], 16)
        self.nins += 1
        me = (d[0], d[2])
        for t in writes:
            t.w = me
            t.r = {}
        for t in reads:
            t.r[d[0]] = d[2]
        return ins

    def barrier(self):
        allv = {e.name: e.cnt for e in self.es.values() if e.cnt > 0}
        for d in self.dsems:
            if d[2] > 0:
                allv[d[0]] = d[2]
        for e in self.es.values():
            for key, c in allv.items():
                if key == e.name:
                    continue
                if c > e.clock.get(key, 0):
                    e.be.wait_ge(self.semof[key], c)
                    e.clock[key] = c
                    self.nwaits += 1

    def finish(self):
        self.barrier()


class Cfg:
    def __init__(self, NPS=4, S=2048, NSS=4, TS=64, PAST=4096, DEPTH=2, mixers=("mla", "dsa", "sb", "diff"),
                 ffn=True, run_sample=True):
        self.NPS, self.S, self.NSS, self.TS, self.PAST, self.DEPTH = NPS, S, NSS, TS, PAST, DEPTH
        self.mixers = mixers
        self.ffn = ffn
        self.run_sample = run_sample


def rope_tables(pos, d):
    half = d // 2
    freqs = (np.float32(THETA) ** (-np.arange(half, dtype=np.float32) * np.float32(2.0) / np.float32(d))).astype(np.float32)
    ang = pos.astype(np.float32)[:, None] * freqs[None, :]
    return np.cos(ang).astype(np.float32), np.sin(ang).astype(np.float32)


def host_consts(cfg):
    c = {}
    c["ident"] = np.eye(128, dtype=np.float32)
    c["bigi"] = np.tile(np.eye(128, dtype=np.float32) * NEGB, (1, 4))
    q = np.arange(128)[:, None]
    kk = np.arange(128)[None, :]
    c["m_chunk"] = np.where((kk // CHUNK) <= (q // CHUNK), 0.0, -1.0).astype(np.float32)
    c["m_strict"] = np.where(kk < q, 0.0, -1.0).astype(np.float32)
    c["tri"] = (q >= kk).astype(np.float32)
    c["ones"] = np.ones((128, 128), np.float32)
    def tabs(pos):
        c64, s64 = rope_tables(pos, 64)
        c32, s32 = rope_tables(pos, 32)
        return np.concatenate([c64, c64, -s64, s64, c32, c32, -s32, s32], axis=1).astype(np.float32)
    S = cfg.S
    tab = tabs(np.arange(S))
    c["rope_p"] = np.ascontiguousarray(tab.reshape(S // 128, 128, 192).transpose(1, 0, 2))
    tab2 = np.zeros((128, 1, 192), np.float32)
    tab2[:cfg.TS, 0] = tabs(cfg.PAST + np.arange(cfg.TS))
    c["rope_s"] = tab2
    c["zero"] = np.zeros((1, 512), np.float32)
    c["pow2"] = np.tile((0.5 ** np.arange(1, 33, dtype=np.float32))[None, :], (128, 1)).astype(np.float32)
    return c


class Builder:
    def __init__(self, cfg):
        self.cfg = cfg
        self.nc = bass.Bass("TRN2", target_bir_lowering=False)
        self.din = {}
        self.dout = {}

    def declare(self):
        cfg, nc = self.cfg, self.nc
        L = cfg.DEPTH

        def I(name, shape):
            self.din[name] = nc.dram_tensor(name, list(shape), F32, kind="ExternalInput").ap()

        def O(name, shape):
            self.dout[name] = nc.dram_tensor(name, list(shape), F32, kind="ExternalOutput").ap()

        I("xp", (cfg.NPS, cfg.S, D))
        I("xs", (cfg.NSS, cfg.TS, D))
        I("c_mla", (L, cfg.NSS, cfg.PAST, 160))
        I("c_dsa", (L, cfg.NSS, cfg.PAST, 192))
        I("c_sb", (L, cfg.NSS, cfg.PAST, 512))
        I("c_diff", (L, cfg.NSS, cfg.PAST, 512))
        for nm in ("n1", "nm", "n2"):
            I(nm, (L, 128, 8))
        I("fnorm", (D,))
        for nm in ("wg1", "wu1", "wg2", "wu2"):
            I(nm, (L, D, DFF))
        for nm in ("wd1", "wd2"):
            I(nm, (L, DFF, D))
        I("w_in", (L, D, IN_COLS))
        I("qn", (L, 256))
        I("wqb", (L, 256, 384))
        I("kvn", (L, 128))
        I("wkvb", (L, 128, 512))
        I("lam", (L, 128))
        I("dn", (L, 64))
        I("wo", (L, D, D))
        for nm, a in host_consts(cfg).items():
            I("k_" + nm, a.shape)
        O("yp", (cfg.NPS, cfg.S, D))
        O("ys", (cfg.NSS, cfg.TS, D))
        O("mla_p", (L, cfg.NPS, cfg.S, 160))
        O("dsa_p", (L, cfg.NPS, cfg.S, 192))
        O("sb_p", (L, cfg.NPS, cfg.S, 512))
        O("diff_p", (L, cfg.NPS, cfg.S, 512))
        O("mla_s", (L, cfg.NSS, cfg.TS, 160))
        O("dsa_s", (L, cfg.NSS, cfg.TS, 192))
        O("sb_s", (L, cfg.NSS, cfg.TS, 512))
        O("diff_s", (L, cfg.NSS, cfg.TS, 512))

    def sb(self, stack, name, shape, dtype):
        self.uid = getattr(self, "uid", 0) + 1
        return stack.enter_context(self.nc.sbuf_tensor("%s_%d" % (name, self.uid), list(shape), dtype))

    def PE(self, fn, r=(), w=()):
        return self.k.op(self.k.pe, fn, r, w)

    def ACT(self, fn, r=(), w=()):
        return self.k.op(self.k.act, fn, r, w)

    def DVE(self, fn, r=(), w=()):
        return self.k.op(self.k.dve, fn, r, w)

    def POOL(self, fn, r=(), w=()):
        return self.k.op(self.k.pool, fn, r, w)

    def psum(self):
        i = self.ps_next
        self.ps_next = (self.ps_next + 1) % len(self.ps_rot)
        return self.ps_rot[i]

    def ring(self, lst):
        idx = lst.idx
        lst.idx = (idx + 1) % len(lst)
        return lst[idx]

    def mkring(self, stack, name, shape, dtype, n):
        r = Ring((self.sb(stack, "%s%d" % (name, i), shape, dtype), T()) for i in range(n))
        r.idx = 0
        return r

    def build(self):
        cfg, nc = self.cfg, self.nc
        self.declare()
        with ExitStack() as gs:
            self.k = K(nc, gs)
            k = self.k
            self.rings = {}
            self.banks = []
            for i in range(8):
                pt = gs.enter_context(nc.psum_tensor("psb%d" % i, [128, 512], F32))
                self.banks.append((pt, T(excl=True)))
            self.ps_rot = self.banks[0:5]
            self.ps_next = 0
            self.ps_acc = self.banks[5:8]
            self.load_consts(gs)
            for n in range(cfg.NPS):
                self.run_job("p", n)
                k.barrier()
            if cfg.NSS > 0 and cfg.run_sample:
                self.run_job("s", 0)
                k.barrier()
            k.finish()
            self.stats = (k.nins, k.nwaits)
        return nc

    def load_consts(self, gs):
        cfg, k = self.cfg, self.k
        din = self.din
        C = {}
        TC = T()
        self.TC = TC

        def cst(name, shape, dtype, src):
            t = self.sb(gs, "c_" + name, shape, dtype)
            k.dma(k.pool, t[:], src, writes=[TC])
            C[name] = t
            return t

        cst("ident", (128, 128), BF16, din["k_ident"][:, :])
        cst("bigi", (128, 512), BF16, din["k_bigi"][:, :])
        cst("m_chunk", (128, 128), BF16, din["k_m_chunk"][:, :])
        cst("m_strict", (128, 128), BF16, din["k_m_strict"][:, :])
        cst("tri", (128, 128), BF16, din["k_tri"][:, :])
        cst("ones", (128, 128), BF16, din["k_ones"][:, :])
        cst("pow2", (128, 32), F32, din["k_pow2"][:, :])
        cst("zero", (1, 512), BF16, din["k_zero"][:, :])
        L = cfg.DEPTH
        for nm in ("n1", "nm", "n2"):
            cst(nm, (128, L, 8), F32, din[nm].rearrange("l p c -> p l c"))
        cst("qn", (128, L, 256), F32, din["qn"].partition_broadcast(128))
        cst("kvn", (128, L, 128), F32, din["kvn"].partition_broadcast(128))
        cst("lam", (128, L, 128), F32, din["lam"].partition_broadcast(128))
        cst("dn", (128, L, 64), F32, din["dn"].partition_broadcast(128))
        self.C = C

    def run_job(self, kind, n):
        cfg, k, nc = self.cfg, self.k, self.nc
        if kind == "p":
            NT, TP = cfg.S // 128, 128
        else:
            NT, TP = cfg.NSS, cfg.TS
        self.kind, self.NT, self.TP, self.jn = kind, NT, TP, n
        ntok = NT * TP
        self.ntok = ntok
        tpg = 512 // TP
        self.groups = [(g0, min(tpg, NT - g0)) for g0 in range(0, NT, tpg)]
        with ExitStack() as js:
            self.X = self.sb(js, "X", (128, NT, D), F32)
            self.TX = [T() for _ in range(NT)]
            self.XNT = self.sb(js, "XNT", (128, 8, ntok), BF16)
            self.TXNT = [T() for _ in self.groups]
            self.small = self.mkring(js, "small", (128, 8), F32, 6)
            for t in range(NT):
                if kind == "p":
                    src = self.din["xp"][n, t * 128:(t + 1) * 128, :]
                else:
                    src = self.din["xs"][t, :, :]
                k.dma(k.sp, self.X[:TP, t, :], src, writes=[self.TX[t]])
            for l in range(cfg.DEPTH):
                if cfg.ffn:
                    self.norm_transpose(self.C["n1"][:, l, :])
                    self.ffn(l, "1")
                    k.barrier()
                if cfg.mixers:
                    self.norm_transpose(self.C["nm"][:, l, :])
                    self.mixer(l)
                    k.barrier()
                if cfg.ffn:
                    self.norm_transpose(self.C["n2"][:, l, :])
                    self.ffn(l, "2")
                    k.barrier()
            self.final_norm()

    def rstd_of(self, ss_ap, TP, Tss, n_feat):
        sm, Tsm = self.ring(self.small)
        self.DVE(lambda e: e.tensor_scalar(out=sm[:TP, 0:1], in0=ss_ap, scalar1=1.0 / n_feat, scalar2=EPS,
                                           op0=ALU.mult, op1=ALU.add), [Tss], [Tsm])
        self.ACT(lambda e: e.activation(out=sm[:TP, 1:2], in_=sm[:TP, 0:1], func=AF.Ln), [Tsm], [Tsm])
        self.ACT(lambda e: e.activation(out=sm[:TP, 2:3], in_=sm[:TP, 1:2], func=AF.Exp, scale=-0.5), [Tsm], [Tsm])
        return sm[:TP, 2:3], Tsm

    def norm_transpose(self, G):
        TP, NT = self.TP, self.NT
        C = self.C
        tpg = 512 // TP
        ns_ = ExitStack()
        xsr = self.mkring(ns_, "xs", (128, D), BF16, 2)
        junk = self.mkring(ns_, "junk", (128, D), BF16, 2)
        for t in range(NT):
            g = t // tpg
            sm, Tsm = self.ring(self.small)
            jk, Tjk = self.ring(junk)
            self.ACT(lambda e: e.activation(out=jk[:TP, :], in_=self.X[:TP, t, :], func=AF.Square,
                                            accum_out=sm[:TP, 3:4]), [self.TX[t]], [Tjk, Tsm])
            rs, Trs = self.rstd_of(sm[:TP, 3:4], TP, Tsm, D)
            xs, Txs = self.ring(xsr)
            self.DVE(lambda e: e.tensor_scalar(out=xs[:TP, :], in0=self.X[:TP, t, :], scalar1=rs, scalar2=None,
                                               op0=ALU.mult), [self.TX[t], Trs], [Txs])
            pb, Tpb = self.psum()
            pv = pb[:].bitcast(BF16).rearrange("p (c t) -> p c t", c=8)
            for c in range(8):
                self.PE(lambda e: e.transpose(pv[:, c, :TP], xs[:TP, c * 128:(c + 1) * 128], C["ident"][:TP, :TP]),
                        [Txs, self.TC], [Tpb])
            self.DVE(lambda e: e.tensor_tensor(out=self.XNT[:, :, t * TP:(t + 1) * TP], in0=pv[:, :, :TP],
                                               in1=G.unsqueeze(2).to_broadcast([128, 8, TP]), op=ALU.mult),
                     [Tpb, self.TC], [self.TXNT[g]])
        self.k.barrier()
        ns_.close()

    def ffn(self, l, which):
        cfg, k = self.cfg, self.k
        TP, NT = self.TP, self.NT
        wg = self.din["wg" + which][l].rearrange("(k p) f -> p k f", p=128)
        wu = self.din["wu" + which][l].rearrange("(k p) f -> p k f", p=128)
        wd = self.din["wd" + which][l].rearrange("(c p) d -> p c d", p=128)
        NCH = DFF // 128
        CG = 2
        cgs = [(c0, min(CG, NCH - c0)) for c0 in range(0, NCH, CG)]
        with ExitStack() as fs:
            WG = self.mkring(fs, "WG", (128, 8, CG * 128), BF16, 3)
            WU = self.mkring(fs, "WU", (128, 8, CG * 128), BF16, 3)
            WD = self.mkring(fs, "WD", (128, CG, D), BF16, 3)
            HT = self.mkring(fs, "HT", (128, CG, self.ntok), BF16, 2)
            SG = self.mkring(fs, "SG", (128, 512), BF16, 3)

            def gate_up(gi):
                c0, nch = cgs[gi]
                wgt, Twg = WG[gi % 3]
                wut, Twu = WU[gi % 3]
                wdt, Twd = WD[gi % 3]
                ht, Tht = HT[gi % 2]
                k.dma(k.pool, wgt[:, :, :nch * 128], wg[:, :, c0 * 128:(c0 + nch) * 128], writes=[Twg])
                k.dma(k.pool, wut[:, :, :nch * 128], wu[:, :, c0 * 128:(c0 + nch) * 128], writes=[Twu])
                k.dma(k.pool, wdt[:, :nch, :], wd[:, c0:c0 + nch, :], writes=[Twd])
                for g, (t0, nt) in enumerate(self.groups):
                    col0, ncol = t0 * TP, nt * TP
                    for ci in range(nch):
                        pg, Tpg = self.psum()
                        for kk in range(8):
                            self.PE(lambda e: e.matmul(pg[:, :ncol], wgt[:, kk, ci * 128:(ci + 1) * 128],
                                                       self.XNT[:, kk, col0:col0 + ncol], start=(kk == 0), stop=(kk == 7)),
                                    [Twg, self.TXNT[g]], [Tpg])
                        pu, Tpu = self.psum()
                        for kk in range(8):
                            self.PE(lambda e: e.matmul(pu[:, :ncol], wut[:, kk, ci * 128:(ci + 1) * 128],
                                                       self.XNT[:, kk, col0:col0 + ncol], start=(kk == 0), stop=(kk == 7)),
                                    [Twu, self.TXNT[g]], [Tpu])
                        sg, Tsg = self.ring(SG)
                        self.ACT(lambda e: e.activation(out=sg[:, :ncol], in_=pg[:, :ncol], func=AF.Silu), [Tpg], [Tsg])
                        self.DVE(lambda e: e.tensor_tensor(out=ht[:, ci, col0:col0 + ncol], in0=sg[:, :ncol],
                                                           in1=pu[:, :ncol], op=ALU.mult), [Tsg, Tpu], [Tht])

            def down(gi):
                c0, nch = cgs[gi]
                wdt, Twd = WD[gi % 3]
                ht, Tht = HT[gi % 2]
                for t in range(NT):
                    for h in range(2):
                        pd, Tpd = self.psum()
                        for ci in range(nch):
                            self.PE(lambda e: e.matmul(pd[:TP, :], ht[:, ci, t * TP:(t + 1) * TP],
                                                       wdt[:, ci, h * 512:(h + 1) * 512], start=(ci == 0), stop=(ci == nch - 1)),
                                    [Tht, Twd], [Tpd])
                        xs = self.X[:TP, t, h * 512:(h + 1) * 512]
                        self.DVE(lambda e: e.scalar_tensor_tensor(out=xs, in0=pd[:TP, :], scalar=0.5, in1=xs,
                                                                  op0=ALU.mult, op1=ALU.add), [Tpd, self.TX[t]], [self.TX[t]])

            ng = len(cgs)
            gate_up(0)
            for gi in range(ng):
                if gi + 1 < ng:
                    gate_up(gi + 1)
                down(gi)

    def final_norm(self):
        k = self.k
        TP, NT = self.TP, self.NT
        with ExitStack() as fs:
            YT = self.mkring(fs, "YT", (128, D), F32, 2)
            junk = self.mkring(fs, "junk", (128, D), BF16, 2)
            GF = self.sb(fs, "gf", (128, D), F32)
            TGF = T()
            k.dma(k.pool, GF[:], self.din["fnorm"].partition_broadcast(128), writes=[TGF])
            for t in range(NT):
                sm, Tsm = self.ring(self.small)
                jk, Tjk = self.ring(junk)
                self.ACT(lambda e: e.activation(out=jk[:TP, :], in_=self.X[:TP, t, :], func=AF.Square,
                                                accum_out=sm[:TP, 3:4]), [self.TX[t]], [Tjk, Tsm])
                rs, Trs = self.rstd_of(sm[:TP, 3:4], TP, Tsm, D)
                yt, Tyt = self.ring(YT)
                self.DVE(lambda e: e.scalar_tensor_tensor(out=yt[:TP, :], in0=self.X[:TP, t, :], scalar=rs,
                                                          in1=GF[:TP, :], op0=ALU.mult, op1=ALU.mult),
                         [self.TX[t], Trs, TGF], [Tyt])
                if self.kind == "p":
                    dst = self.dout["yp"][self.jn, t * 128:(t + 1) * 128, :]
                else:
                    dst = self.dout["ys"][t, :, :]
                k.dma(k.sp, dst, yt[:TP, :], reads=[Tyt])
            k.barrier()

    MIX = {
        "mla": (0, 416, [(0, 416)], 0, 4, 4),
        "dsa": (416, 968, [(0, 384), (384, 512), (896, 72)], 256, 2, 12),
        "sb": (1384, 768, [(0, 512), (512, 256)], 512, 4, 4),
        "diff": (2152, 768, [(0, 512), (512, 256)], 768, 4, 4),
    }

    def mixer(self, l):
        for m in self.cfg.mixers:
            with ExitStack() as ms:
                self.mixer_pass(l, m, ms)
            self.k.barrier()

    def acc_bank(self):
        i = getattr(self, "acc_next", 0)
        self.acc_next = (i + 1) % len(self.ps_acc)
        return self.ps_acc[i]

    def mixer_pass(self, l, m, ms):
        cfg, k = self.cfg, self.k
        TP, NT, kind = self.TP, self.NT, self.kind
        c0, ncols, banks, orow0, nks, nqs = self.MIX[m]
        self.m, self.l = m, l
        if kind == "p":
            NK = cfg.S
        else:
            NK = cfg.PAST + cfg.TS
        self.NK = NK
        NKT = (NK + 127) // 128
        self.NKT = NKT
        self.nsel = min(256, NK // 4)
        rsrc = self.din["k_rope_p"] if kind == "p" else self.din["k_rope_s"]
        ntab = (cfg.S // 128) if kind == "p" else 1
        self.tab = None
        if m == "dsa":
            self.tab = self.sb(ms, "tab", (128, ntab, 128), F32)
            k.dma(k.pool, self.tab[:], rsrc[:, :, 0:128], writes=[self.TC])
        elif m in ("mla", "diff"):
            self.tab = self.sb(ms, "tab", (128, ntab, 64), F32)
            k.dma(k.pool, self.tab[:], rsrc[:, :, 128:192], writes=[self.TC])
        self.WIN = self.sb(ms, "WIN", (128, 8, ncols), BF16)
        self.TWIN = T()
        k.dma(k.pool, self.WIN[:], self.din["w_in"][l].rearrange("(k p) c -> p k c", p=128)[:, :, c0:c0 + ncols],
              writes=[self.TWIN])
        self.WO = self.sb(ms, "WO", (128, 2, D), BF16)
        self.TWO = T()
        k.dma(k.pool, self.WO[:], self.din["wo"][l, orow0:orow0 + 256, :].rearrange("(k p) d -> p k d", p=128),
              writes=[self.TWO])
        self.KT = self.sb(ms, "KT", (128, nks, NK), BF16)
        self.TK = [T() for _ in range(NKT)]
        nv = 1 if m == "dsa" else 4
        self.V = self.sb(ms, "V", (128, NKT, nv, 65), BF16)
        self.TV = [T() for _ in range(NKT)]
        self.QT = self.sb(ms, "QT", (128, nqs, 512), BF16)
        self.TQ = T()
        ntl = NT if kind == "s" else 4
        self.O = self.sb(ms, "O", (128, ntl, 256), BF16)
        self.TO = [T() for _ in range(ntl)]
        self.OT = self.sb(ms, "OT", (128, 2, 512), BF16)
        self.TOT = T()
        self.ROWS = self.mkring(ms, "ROWS", (128, 512), F32, 2)
        self.STG = self.mkring(ms, "STG", (128, 1024), BF16, 2)
        self.RT = self.mkring(ms, "RT", (128, 512), F32, 4)
        self.PT = self.mkring(ms, "PT", (128, 512), BF16, 3)
        self.NRM = self.mkring(ms, "NRM", (128, 8), F32, 4)
        TVall = self.TV
        self.DVE(lambda e: e.memset(self.V[:, :, :, 64:65], 1.0), [], TVall)
        if m == "mla":
            self.WQB = self.sb(ms, "WQB", (128, 2, 384), BF16)
            self.WKVB = self.sb(ms, "WKVB", (128, 512), BF16)
            self.TWM = T()
            k.dma(k.pool, self.WQB[:], self.din["wqb"][l].rearrange("(k p) c -> p k c", p=128), writes=[self.TWM])
            k.dma(k.pool, self.WKVB[:], self.din["wkvb"][l], writes=[self.TWM])
            self.CQT = self.mkring(ms, "CQT", (128, 2, 128), BF16, 2)
            self.LT = self.mkring(ms, "LT", (128, 128), BF16, 2)
        if m == "dsa":
            self.IB = self.sb(ms, "IB", (128, NK), F32)
            self.TIB = T()
            self.MB = self.sb(ms, "MB", (128, NK), BF16)
            self.TMB = T()
            self.JK = self.sb(ms, "JK", (128, NK), BF16)
            self.TJK = T()
            self.WAB = self.sb(ms, "WAB", (128, ntl, 8), F32)
            self.WSG = self.sb(ms, "WSG", (128, ntl, 8), F32)
            self.TWAB = [T() for _ in range(ntl)]
            self.RB = self.mkring(ms, "RB", (128, 512), F32, 2)
            self.BIS = self.mkring(ms, "BIS", (128, 48), F32, 2)
        if m == "sb":
            W = 256 if kind == "p" else 4 * TP
            self.SBW = W
            self.SP = self.sb(ms, "SP", (128, NKT, W), BF16)
            self.ES = self.sb(ms, "ES", (128, NKT, W), BF16)
            self.TSP = [T() for _ in range(NKT)]
        if m == "diff":
            self.OD = self.mkring(ms, "OD", (128, 8, 64), F32, 2)
            self.LAMF = self.sb(ms, "LAMF", (128, 8), F32)
            self.TLAM = T()
            self.diff_lambda(l)
        import os
        DBG = int(os.environ.get("MIXSTOP", "9"))
        if DBG <= 1:
            return
        if kind == "p":
            for g, (t0, nt) in enumerate(self.groups):
                for t in range(t0, t0 + nt):
                    self.proj_tile(t, qslot=t - t0, kt=t, kcol=t * 128, oslot=t - t0)
                if DBG <= 2:
                    continue
                self.attend_group(t0, nt)
                if DBG <= 3:
                    continue
                self.out_proj(list(range(t0, t0 + nt)), list(range(nt)))
        else:
            for s in range(NT):
                self.load_past(s)
                self.proj_tile(s, qslot=0, kt=NKT - 1, kcol=cfg.PAST, oslot=s)
                self.attend_sample(s)
            self.out_proj(list(range(NT)), list(range(NT)))

    def transposes_to(self, srcs, dst_ap, rows, nparts, reads, writes, dt=BF16):
        C = self.C
        pb, Tpb = self.psum()
        n = len(srcs)
        pv = pb[:].bitcast(BF16)[:, 0:n * 128].rearrange("p (c t) -> p c t", c=n)
        for i, sap in enumerate(srcs):
            self.PE(lambda e: e.transpose(pv[:nparts, i, :rows], sap, C["ident"][:rows, :rows]), list(reads) + [self.TC], [Tpb])
        self.evac_copy(dst_ap, pv[:nparts, :, :rows], [Tpb], writes)

    def evac_copy(self, dst, src, reads, writes):
        i = getattr(self, "_ec", 0)
        self._ec = i + 1
        import os
        if i % 2 == 0 or os.environ.get("ALLDVE") == "1":
            self.DVE(lambda e: e.tensor_copy(out=dst, in_=src), reads, writes)
        else:
            self.ACT(lambda e: e.activation(out=dst, in_=src, func=AF.Copy), reads, writes)

    def rope(self, src, dst, H, d, tslot, TP, reads, writes):
        half = d // 2
        toff = 0
        tab = self.tab
        CC = tab[:TP, tslot, toff:toff + d]
        SS = tab[:TP, tslot, toff + d:toff + 2 * d]
        a, Ta = self.ring(self.RT)
        b, Tb = self.ring(self.RT)
        av = a[:TP, 0:H * d].rearrange("p (h d) -> p h d", h=H)
        bv = b[:TP, 0:H * d].rearrange("p (h d) -> p h d", h=H)
        self.DVE(lambda e: e.tensor_tensor(out=av, in0=src, in1=CC.unsqueeze(1).to_broadcast([TP, H, d]), op=ALU.mult),
                 list(reads) + [self.TC], [Ta])
        self.DVE(lambda e: e.tensor_tensor(out=bv[:, :, 0:half], in0=src[:, :, half:d],
                                           in1=SS[:, 0:half].unsqueeze(1).to_broadcast([TP, H, half]), op=ALU.mult),
                 list(reads) + [self.TC], [Tb])
        self.DVE(lambda e: e.tensor_tensor(out=bv[:, :, half:d], in0=src[:, :, 0:half],
                                           in1=SS[:, half:d].unsqueeze(1).to_broadcast([TP, H, half]), op=ALU.mult),
                 list(reads) + [self.TC], [Tb])
        self.DVE(lambda e: e.tensor_tensor(out=dst, in0=av, in1=bv, op=ALU.add), [Ta, Tb], writes)

    def project(self, t):
        TP = self.TP
        g = t // (512 // TP)
        outs = []
        for (b0, bn) in self.MIX[self.m][2]:
            pb, Tpb = self.psum()
            for kk in range(8):
                self.PE(lambda e: e.matmul(pb[:TP, :bn], self.XNT[:, kk, t * TP:(t + 1) * TP], self.WIN[:, kk, b0:b0 + bn],
                                           start=(kk == 0), stop=(kk == 7)), [self.TXNT[g], self.TWIN], [Tpb])
            outs.append((pb, Tpb))
        return outs

    def rows_out(self, rows, Trows, t, width):
        k = self.k
        name = {"mla": "mla", "dsa": "dsa", "sb": "sb", "diff": "diff"}[self.m]
        if self.kind == "p":
            dst = self.dout[name + "_p"][self.l, self.jn, t * 128:(t + 1) * 128, :]
        else:
            dst = self.dout[name + "_s"][self.l, t, :, :]
        k.dma(k.sp, dst, rows[:self.TP, 0:width], reads=[Trows])

    def proj_tile(self, t, qslot, kt, kcol, oslot):
        getattr(self, "proj_" + self.m)(t, qslot, kt, kcol, oslot)

    def mla_keys(self, lat, krope, nk, kt, kcol, reads):
        lt, Tlt = self.ring(self.LT)
        self.transposes_to([lat], lt[:, 0:nk].unsqueeze(1), nk, 128, reads, [Tlt])
        pkv, Tpkv = self.psum()
        self.PE(lambda e: e.matmul(pkv[:nk, :], lt[:, :nk], self.WKVB[:, :], start=True, stop=True), [Tlt, self.TWM], [Tpkv])
        st, Tst = self.ring(self.STG)
        kst = st[:nk, 0:384].rearrange("p (h d) -> p h d", h=4)
        pv = pkv[:nk, :].rearrange("p (h d) -> p h d", h=4)
        self.evac_copy(kst[:, :, 0:64], pv[:, :, 0:64], [Tpkv], [Tst])
        self.evac_copy(self.V[:nk, kt, :, 0:64], pv[:, :, 64:128], [Tpkv], [self.TV[kt]])
        self.evac_copy(kst[:, :, 64:96], krope.unsqueeze(1).to_broadcast([nk, 4, 32]), reads, [Tst])
        self.transposes_to([kst[:, h, :] for h in range(4)], self.KT[0:96, 0:4, kcol:kcol + nk], nk, 96, [Tst], [self.TK[kt]])

    def proj_mla(self, t, qslot, kt, kcol, oslot):
        TP, l, C = self.TP, self.l, self.C
        (pj, Tpj), = self.project(t)
        rows, Trows = self.ring(self.ROWS)
        st, Tst = self.ring(self.STG)
        jk, Tjk = self.ring(self.RT)
        nr, Tnr = self.ring(self.NRM)
        self.ACT(lambda e: e.activation(out=jk[:TP, 0:256], in_=pj[:TP, 0:256], func=AF.Square, accum_out=nr[:TP, 0:1]),
                 [Tpj], [Tjk, Tnr])
        self.ACT(lambda e: e.activation(out=jk[:TP, 256:384], in_=pj[:TP, 256:384], func=AF.Square, accum_out=nr[:TP, 1:2]),
                 [Tpj], [Tjk, Tnr])
        rq, Trq = self.rstd_of(nr[:TP, 0:1], TP, Tnr, 256)
        rk, Trk = self.rstd_of(nr[:TP, 1:2], TP, Tnr, 128)
        self.DVE(lambda e: e.scalar_tensor_tensor(out=st[:TP, 0:256], in0=pj[:TP, 0:256], scalar=rq, in1=C["qn"][:TP, l, :],
                                                  op0=ALU.mult, op1=ALU.mult), [Tpj, Trq, self.TC], [Tst])
        self.DVE(lambda e: e.scalar_tensor_tensor(out=rows[:TP, 0:128], in0=pj[:TP, 256:384], scalar=rk, in1=C["kvn"][:TP, l, :],
                                                  op0=ALU.mult, op1=ALU.mult), [Tpj, Trk, self.TC], [Trows])
        self.evac_copy(st[:TP, 256:384], rows[:TP, 0:128], [Trows], [Tst])
        self.rope(pj[:TP, 384:416].unsqueeze(1), rows[:TP, 128:160].unsqueeze(1), 1, 32, self.tslot(t), TP, [Tpj], [Trows])
        self.evac_copy(st[:TP, 384:416], rows[:TP, 128:160], [Trows], [Tst])
        import os
        ML = int(os.environ.get("MLASTOP", "9"))
        if ML <= 1:
            return
        self.rows_out(rows, Trows, t, 160)
        if ML <= 2:
            return
        cq, Tcq = self.ring(self.CQT)
        self.transposes_to([st[:TP, 0:128], st[:TP, 128:256]], cq[:, :, :TP], TP, 128, [Tst], [Tcq])
        pq, Tpq = self.psum()
        for kc in range(2):
            self.PE(lambda e: e.matmul(pq[:TP, 0:384], cq[:, kc, :TP], self.WQB[:, kc, :], start=(kc == 0), stop=(kc == 1)),
                    [Tcq, self.TWM], [Tpq])
        if ML <= 4:
            return
        sq, Tsq = self.ring(self.STG)
        pqv = pq[:TP, 0:384].rearrange("p (h d) -> p h d", h=4)
        sqv = sq[:TP, 0:384].rearrange("p (h d) -> p h d", h=4)
        self.evac_copy(sqv[:, :, 0:64], pqv[:, :, 0:64], [Tpq], [Tsq])
        if ML <= 5:
            return
        self.rope(pqv[:, :, 64:96], sqv[:, :, 64:96], 4, 32, self.tslot(t), TP, [Tpq], [Tsq])
        if ML <= 6:
            return
        self.transposes_to([sqv[:, h, :] for h in range(4)], self.QT[0:96, 0:4, qslot * TP:(qslot + 1) * TP], TP, 96, [Tsq], [self.TQ])
        if ML <= 7:
            return
        self.mla_keys(st[:TP, 256:384], st[:TP, 384:416], TP, kt, kcol, [Tst])

    def tslot(self, t):
        return t if self.kind == "p" else 0

    def proj_dsa(self, t, qslot, kt, kcol, oslot):
        TP = self.TP
        (pa, Tpa), (pb, Tpb), (pc, Tpc) = self.project(t)
        rows, Trows = self.ring(self.ROWS)
        st, Tst = self.ring(self.STG)
        ts = self.tslot(t)
        self.rope(pa[:TP, 0:256].rearrange("p (h d) -> p h d", h=4), st[:TP, 0:256].rearrange("p (h d) -> p h d", h=4),
                  4, 64, ts, TP, [Tpa], [Tst])
        self.rope(pa[:TP, 256:320].unsqueeze(1), rows[:TP, 0:64].unsqueeze(1), 1, 64, ts, TP, [Tpa], [Trows])
        self.evac_copy(rows[:TP, 64:128], pa[:TP, 320:384], [Tpa], [Trows])
        self.evac_copy(self.V[:TP, kt, 0, 0:64], pa[:TP, 320:384], [Tpa], [self.TV[kt]])
        self.rope(pc[:TP, 0:64].unsqueeze(1), rows[:TP, 128:192].unsqueeze(1), 1, 64, ts, TP, [Tpc], [Trows])
        self.ACT(lambda e: e.activation(out=self.WAB[:TP, oslot, :], in_=pc[:TP, 64:72], func=AF.Abs, scale=8.0 ** -0.5),
                 [Tpc], [self.TWAB[oslot]])
        self.DVE(lambda e: e.tensor_scalar(out=self.WSG[:TP, oslot, :], in0=pc[:TP, 64:72], scalar1=0.0, scalar2=2.0,
                                           op0=ALU.is_ge, op1=ALU.mult), [Tpc], [self.TWAB[oslot]])
        self.DVE(lambda e: e.tensor_scalar(out=self.WSG[:TP, oslot, :], in0=self.WSG[:TP, oslot, :], scalar1=-1.0, scalar2=None,
                                           op0=ALU.add), [self.TWAB[oslot]], [self.TWAB[oslot]])
        self.rope(pb[:TP, 0:512].rearrange("p (h d) -> p h d", h=8), st[:TP, 256:768].rearrange("p (h d) -> p h d", h=8),
                  8, 64, ts, TP, [Tpb], [Tst])
        self.evac_copy(st[:TP, 768:832], rows[:TP, 0:64], [Trows], [Tst])
        self.evac_copy(st[:TP, 832:896], rows[:TP, 128:192], [Trows], [Tst])
        self.rows_out(rows, Trows, t, 192)
        qs = slice(qslot * TP, (qslot + 1) * TP)
        self.transposes_to([st[:TP, h * 64:(h + 1) * 64] for h in range(4)], self.QT[0:64, 0:4, qs], TP, 64, [Tst], [self.TQ])
        self.transposes_to([st[:TP, 256 + h * 64:256 + (h + 1) * 64] for h in range(8)], self.QT[0:64, 4:12, qs], TP, 64, [Tst], [self.TQ])
        self.transposes_to([st[:TP, 768:832], st[:TP, 832:896]], self.KT[0:64, 0:2, kcol:kcol + TP], TP, 64, [Tst], [self.TK[kt]])

    def proj_sb(self, t, qslot, kt, kcol, oslot):
        TP = self.TP
        (pa, Tpa), (pb, Tpb) = self.project(t)
        rows, Trows = self.ring(self.ROWS)
        st, Tst = self.ring(self.STG)
        rv = rows[:TP, :].rearrange("p (h d) -> p h d", h=4)
        self.evac_copy(st[:TP, 0:256], pa[:TP, 0:256], [Tpa], [Tst])
        self.evac_copy(rv[:, :, 0:64], pa[:TP, 256:512].rearrange("p (h d) -> p h d", h=4), [Tpa], [Trows])
        self.evac_copy(st[:TP, 256:512].rearrange("p (h d) -> p h d", h=4), pa[:TP, 256:512].rearrange("p (h d) -> p h d", h=4), [Tpa], [Tst])
        self.evac_copy(rv[:, :, 64:128], pb[:TP, 0:256].rearrange("p (h d) -> p h d", h=4), [Tpb], [Trows])
        self.evac_copy(self.V[:TP, kt, :, 0:64], pb[:TP, 0:256].rearrange("p (h d) -> p h d", h=4), [Tpb], [self.TV[kt]])
        self.rows_out(rows, Trows, t, 512)
        qs = slice(qslot * TP, (qslot + 1) * TP)
        self.transposes_to([st[:TP, h * 64:(h + 1) * 64] for h in range(4)], self.QT[0:64, 0:4, qs], TP, 64, [Tst], [self.TQ])
        self.transposes_to([st[:TP, 256 + h * 64:256 + (h + 1) * 64] for h in range(4)], self.KT[0:64, 0:4, kcol:kcol + TP], TP, 64, [Tst], [self.TK[kt]])

    def diff_lambda(self, l):
        lam = self.C["lam"]
        lam_init = 0.8 - 0.6 * math.exp(-0.3 * l)
        self.lam_init = lam_init
        LF, TL = self.LAMF, self.TLAM
        jk, Tjk = self.ring(self.RT)
        self.DVE(lambda e: e.scalar_tensor_tensor(out=jk[:, 0:32], in0=lam[:, l, 0:32], scalar=1.0, in1=lam[:, l, 32:64],
                                                  op0=ALU.mult, op1=ALU.mult, accum_out=LF[:, 0:1]), [self.TC], [Tjk, TL])
        self.DVE(lambda e: e.scalar_tensor_tensor(out=jk[:, 32:64], in0=lam[:, l, 64:96], scalar=1.0, in1=lam[:, l, 96:128],
                                                  op0=ALU.mult, op1=ALU.mult, accum_out=LF[:, 1:2]), [self.TC], [Tjk, TL])
        self.ACT(lambda e: e.activation(out=LF[:, 2:4], in_=LF[:, 0:2], func=AF.Exp), [TL], [TL])
        self.DVE(lambda e: e.scalar_tensor_tensor(out=LF[:, 4:5], in0=LF[:, 3:4], scalar=-lam_init, in1=LF[:, 2:3],
                                                  op0=ALU.add, op1=ALU.subtract), [TL], [TL])

    def proj_diff(self, t, qslot, kt, kcol, oslot):
        TP = self.TP
        (pa, Tpa), (pb, Tpb) = self.project(t)
        rows, Trows = self.ring(self.ROWS)
        st, Tst = self.ring(self.STG)
        ts = self.tslot(t)
        rv = rows[:TP, :].rearrange("p (h d) -> p h d", h=4)
        self.rope(pa[:TP, 0:256].rearrange("p (h d) -> p h d", h=8), st[:TP, 0:256].rearrange("p (h d) -> p h d", h=8),
                  8, 32, ts, TP, [Tpa], [Tst])
        kf, Tkf = self.ring(self.RT)
        self.rope(pa[:TP, 256:512].rearrange("p (h d) -> p h d", h=8), kf[:TP, 0:256].rearrange("p (h d) -> p h d", h=8),
                  8, 32, ts, TP, [Tpa], [Tkf])
        self.evac_copy(rv[:, :, 0:64], kf[:TP, 0:256].rearrange("p (h d) -> p h d", h=4), [Tkf], [Trows])
        self.evac_copy(st[:TP, 256:512], kf[:TP, 0:256], [Tkf], [Tst])
        self.evac_copy(rv[:, :, 64:128], pb[:TP, 0:256].rearrange("p (h d) -> p h d", h=4), [Tpb], [Trows])
        self.evac_copy(self.V[:TP, kt, :, 0:64], pb[:TP, 0:256].rearrange("p (h d) -> p h d", h=4), [Tpb], [self.TV[kt]])
        self.rows_out(rows, Trows, t, 512)
        qs = slice(qslot * TP, (qslot + 1) * TP)
        self.transposes_to([st[:TP, h * 64:(h + 1) * 64] for h in range(4)], self.QT[0:64, 0:4, qs], TP, 64, [Tst], [self.TQ])
        self.transposes_to([st[:TP, 256 + h * 64:256 + (h + 1) * 64] for h in range(4)], self.KT[0:64, 0:4, kcol:kcol + TP], TP, 64, [Tst], [self.TK[kt]])

    def load_past(self, s):
        cfg, k, m, l = self.cfg, self.k, self.m, self.l
        PAST = cfg.PAST
        npt = PAST // 128
        cache = self.din["c_" + m][l, s]
        cv = cache.rearrange("(t p) c -> p t c", p=128)
        with ExitStack() as cs:
            if m == "mla":
                CST = self.mkring(cs, "CST", (128, 4, 160), BF16, 2)
            elif m == "dsa":
                CST = self.mkring(cs, "CST", (128, 4, 2, 64), BF16, 2)
            else:
                CST = self.mkring(cs, "CST", (128, 4, 4, 64), BF16, 2)
            for b0 in range(0, npt, 4):
                nb = min(4, npt - b0)
                c, Tc = self.ring(CST)
                if m == "mla":
                    k.dma(k.pool, c[:, 0:nb, :], cv[:, b0:b0 + nb, :], writes=[Tc])
                    for i in range(nb):
                        self.mla_keys(c[:, i, 0:128], c[:, i, 128:160], 128, b0 + i, (b0 + i) * 128, [Tc])
                elif m == "dsa":
                    k.dma(k.pool, c[:, 0:nb, 0, :], cv[:, b0:b0 + nb, 0:64], writes=[Tc])
                    k.dma(k.pool, c[:, 0:nb, 1, :], cv[:, b0:b0 + nb, 128:192], writes=[Tc])
                    k.dma(k.pool, self.V[:, b0:b0 + nb, 0, 0:64], cv[:, b0:b0 + nb, 64:128], writes=self.TV[b0:b0 + nb])
                    for st_ in range(2):
                        self.transposes_to([c[:, i, st_, :] for i in range(nb)],
                                           self.KT[0:64, st_, b0 * 128:(b0 + nb) * 128].rearrange("p (c t) -> p c t", c=nb),
                                           128, 64, [Tc], self.TK[b0:b0 + nb])
                else:
                    c4 = cv.rearrange("p t (h d) -> p t h d", h=4)
                    for i in range(nb):
                        k.dma(k.pool, c[:, i, :, :], c4[:, b0 + i, :, 0:64], writes=[Tc])
                        k.dma(k.pool, self.V[:, b0 + i, :, 0:64], c4[:, b0 + i, :, 64:128], writes=[self.TV[b0 + i]])
                    for h in range(4):
                        self.transposes_to([c[:, i, h, :] for i in range(nb)],
                                           self.KT[0:64, h, b0 * 128:(b0 + nb) * 128].rearrange("p (c t) -> p c t", c=nb),
                                           128, 64, [Tc], self.TK[b0:b0 + nb])
            self.k.barrier()

    def kt_info(self, kt):
        kcol = kt * 128
        return kcol, min(128, self.NK - kcol)

    def attend_group(self, t0, nt):
        m = self.m
        kts = []
        for kt in range(t0 + nt):
            j0 = max(0, kt - t0)
            kts.append((kt, j0, (j0 if kt >= t0 else None)))
        if m == "mla":
            for h in range(4):
                self.softmax_unit([dict(ks=h, qs=h, vs=h, p0=0, dk=96)], nt, 128, kts, 96 ** -0.5,
                                  dest=self.O[:128, 0:nt, h * 64:(h + 1) * 64], Tdest=self.TO[0:nt], mask=self.C["m_chunk"])
        elif m == "diff":
            for h in range(4):
                od, Tod = self.ring(self.OD)
                for mm in range(2):
                    self.softmax_unit([dict(ks=h, qs=h, vs=h, p0=32 * mm, dk=32)], nt, 128, kts, 32 ** -0.5,
                                      dest=od[:128, mm * 4:mm * 4 + nt, :], Tdest=[Tod], mask=self.C["m_chunk"])
                self.diff_combine(od[:128, 0:nt, :], od[:128, 4:4 + nt, :], Tod, self.O[:128, 0:nt, h * 64:(h + 1) * 64],
                                  self.TO[0:nt], 128, nt)
        elif m == "sb":
            for hf in range(0, nt, 2):
                ns2 = min(2, nt - hf)
                kts2 = []
                for kt in range(t0 + hf + ns2):
                    j0 = max(0, kt - (t0 + hf))
                    kts2.append((kt, j0, (j0 if kt >= t0 + hf else None)))
                for h in range(4):
                    self.sb_unit([h], ns2, 128, kts2, dest=self.O[:128, hf:hf + ns2, h * 64:(h + 1) * 64],
                                 Tdest=self.TO[hf:hf + ns2], qoff=hf * 128)
        elif m == "dsa":
            for j in range(nt):
                t = t0 + j
                nkv = (t + 1) * 128
                self.dsa_select(j, j, 128, nkv, prompt=True)
                ktj = [(kt, 0, None) for kt in range(t + 1)]
                self.softmax_unit([dict(ks=0, qs=h, vs=0, p0=0, dk=64) for h in range(4)], 1, 128, ktj, 64 ** -0.5,
                                  dest=self.O[:128, j, :].rearrange("p (h d) -> p h d", h=4), Tdest=[self.TO[j]],
                                  mask=None, qoff=j * 128, shared_k=True, mb=True)

    def attend_sample(self, s):
        m, TP = self.m, self.TP
        kts = [(kt, 0, None) for kt in range(self.NKT)]
        if m == "mla":
            self.softmax_unit([dict(ks=h, qs=h, vs=h, p0=0, dk=96) for h in range(4)], 1, TP, kts, 96 ** -0.5,
                              dest=self.O[:TP, s, :].rearrange("p (h d) -> p h d", h=4), Tdest=[self.TO[s]], mask=None)
        elif m == "diff":
            od, Tod = self.ring(self.OD)
            for mm in range(2):
                self.softmax_unit([dict(ks=h, qs=h, vs=h, p0=32 * mm, dk=32) for h in range(4)], 1, TP, kts, 32 ** -0.5,
                                  dest=od[:TP, mm * 4:mm * 4 + 4, :], Tdest=[Tod], mask=None)
            self.diff_combine(od[:TP, 0:4, :], od[:TP, 4:8, :], Tod, self.O[:TP, s, :].rearrange("p (h d) -> p h d", h=4),
                              [self.TO[s]], TP, 4)
        elif m == "sb":
            kts2 = [(kt, 0, (0 if kt == self.NKT - 1 else None)) for kt in range(self.NKT)]
            self.sb_unit([0, 1, 2, 3], 1, TP, kts2, dest=self.O[:TP, s, :].rearrange("p (h d) -> p h d", h=4), Tdest=[self.TO[s]])
        elif m == "dsa":
            self.dsa_select(0, s, TP, self.NK, prompt=False)
            self.softmax_unit([dict(ks=0, qs=h, vs=0, p0=0, dk=64) for h in range(4)], 1, TP, kts, 64 ** -0.5,
                              dest=self.O[:TP, s, :].rearrange("p (h d) -> p h d", h=4), Tdest=[self.TO[s]],
                              mask=None, shared_k=True, mb=True)

    def zero_acc(self, acc, Tacc, rows, ncols):
        Z = self.C["zero"]
        self.PE(lambda e: e.matmul(acc[:rows, 0:ncols], Z[0:1, 0:rows], Z[0:1, 0:ncols], start=True, stop=False, skip_group_check=True),
                [self.TC], [Tacc])

    def softmax_unit(self, streams, nsub, TP, kts, scale, dest, Tdest, mask, qoff=0, shared_k=False, mb=False):
        ns = len(streams)
        W = nsub * TP
        acc, Tacc = self.acc_bank()
        n = ns * nsub
        self.zero_acc(acc, Tacc, TP, n * 65)
        accv = acc[:TP, 0:n * 65].rearrange("p (n d) -> p n d", n=n)
        for (kt, j0, dj) in kts:
            kcol, nk = self.kt_info(kt)
            sc, Tsc = self.psum()
            c0 = j0 * TP
            if shared_k:
                s0 = streams[0]
                self.PE(lambda e: e.matmul(sc[:nk, 0:ns * W].rearrange("p (s w) -> p s w", s=ns),
                                           self.KT[0:64, s0["ks"], kcol:kcol + nk],
                                           self.QT[0:64, 0:ns, qoff:qoff + W], start=True, stop=(not mb)),
                        [self.TK[kt], self.TQ], [Tsc])
                if mb:
                    bi = self.C["bigi"][:TP, :].rearrange("p (s w) -> p s w", s=4)[:, 0:ns, 0:TP]
                    self.PE(lambda e: e.matmul(sc[:nk, 0:ns * W].rearrange("p (s w) -> p s w", s=ns),
                                               self.MB[:TP, kcol:kcol + nk], bi, start=False, stop=True),
                            [self.TMB, self.TC], [Tsc])
            else:
                for i, s_ in enumerate(streams):
                    p0, dk = s_["p0"], s_["dk"]
                    hasmask = (dj is not None and mask is not None)
                    self.PE(lambda e: e.matmul(sc[:nk, i * W + c0:(i + 1) * W], self.KT[p0:p0 + dk, s_["ks"], kcol:kcol + nk],
                                               self.QT[p0:p0 + dk, s_["qs"], qoff + c0:qoff + W], start=True, stop=(not hasmask)),
                            [self.TK[kt], self.TQ], [Tsc])
                    if hasmask:
                        d0 = i * W + dj * TP
                        self.PE(lambda e: e.matmul(sc[:nk, d0:d0 + TP], mask[:TP, :nk], self.C["bigi"][:TP, 0:TP],
                                                   start=False, stop=True), [self.TC], [Tsc])
            pt, Tpt = self.ring(self.PT)
            if ns == 1:
                self.ACT(lambda e: e.activation(out=pt[:nk, c0:W], in_=sc[:nk, c0:W], func=AF.Exp, scale=scale), [Tsc], [Tpt])
            else:
                self.ACT(lambda e: e.activation(out=pt[:nk, 0:ns * W], in_=sc[:nk, 0:ns * W], func=AF.Exp, scale=scale), [Tsc], [Tpt])
            for i, s_ in enumerate(streams):
                for j in range(j0, nsub):
                    a0 = (i * nsub + j) * 65
                    self.PE(lambda e: e.matmul(acc[:TP, a0:a0 + 65], pt[:nk, i * W + j * TP:i * W + (j + 1) * TP],
                                               self.V[:nk, kt, s_["vs"], :], start=False, stop=True, skip_group_check=True),
                            [Tpt, self.TV[kt]], [Tacc])
        nr, Tnr = self.ring(self.NRM)
        rcp = nr[:TP, 0:n]
        self.DVE(lambda e: e.reciprocal(out=rcp, in_=accv[:, :, 64]), [Tacc], [Tnr])
        self.DVE(lambda e: e.tensor_tensor(out=dest, in0=accv[:, :, 0:64], in1=rcp.unsqueeze(2).to_broadcast([TP, n, 64]),
                                           op=ALU.mult), [Tacc, Tnr], Tdest)

    def diff_combine(self, o1, o2, Tod, dest, Tdest, TP, n):
        l = self.l
        a, Ta = self.ring(self.RT)
        b, Tb = self.ring(self.RT)
        av = a[:TP, 0:n * 64].rearrange("p (n d) -> p n d", n=n)
        bv = b[:TP, 0:n * 64].rearrange("p (n d) -> p n d", n=n)
        self.DVE(lambda e: e.scalar_tensor_tensor(out=av, in0=o2, scalar=self.LAMF[:TP, 4:5], in1=o1, op0=ALU.mult, op1=ALU.add),
                 [Tod, self.TLAM], [Ta])
        self.DVE(lambda e: e.tensor_tensor(out=bv, in0=av, in1=av, op=ALU.mult), [Ta], [Tb])
        nr, Tnr = self.ring(self.NRM)
        self.DVE(lambda e: e.tensor_reduce(out=nr[:TP, 0:n], in_=bv, axis=AX.X, op=ALU.add), [Tb], [Tnr])
        self.DVE(lambda e: e.tensor_scalar(out=nr[:TP, 0:n], in0=nr[:TP, 0:n], scalar1=1.0 / 64, scalar2=EPS, op0=ALU.mult, op1=ALU.add),
                 [Tnr], [Tnr])
        self.ACT(lambda e: e.activation(out=nr[:TP, 0:n], in_=nr[:TP, 0:n], func=AF.Ln), [Tnr], [Tnr])
        self.ACT(lambda e: e.activation(out=nr[:TP, 0:n], in_=nr[:TP, 0:n], func=AF.Exp, scale=-0.5), [Tnr], [Tnr])
        self.DVE(lambda e: e.tensor_tensor(out=bv, in0=av, in1=nr[:TP, 0:n].unsqueeze(2).to_broadcast([TP, n, 64]), op=ALU.mult),
                 [Ta, Tnr], [Tb])
        self.DVE(lambda e: e.scalar_tensor_tensor(out=dest, in0=bv, scalar=1.0 - self.lam_init,
                                                  in1=self.C["dn"][:TP, l, :].unsqueeze(1).to_broadcast([TP, n, 64]),
                                                  op0=ALU.mult, op1=ALU.mult), [Tb, self.TC], Tdest)

    def sb_unit(self, heads, nsub, TP, kts, dest, Tdest, qoff=0):
        ns = len(heads)
        W = nsub * TP
        NC = ns * W
        scale = 64 ** -0.5
        C = self.C
        SP, ES = self.SP, self.ES
        for (kt, j0, dj) in kts:
            kcol, nk = self.kt_info(kt)
            sc, Tsc = self.psum()
            c0 = j0 * TP
            for i, h in enumerate(heads):
                hasmask = dj is not None
                self.PE(lambda e: e.matmul(sc[:nk, i * W + c0:(i + 1) * W], self.KT[0:64, h, kcol:kcol + nk],
                                           self.QT[0:64, h, qoff + c0:qoff + W], start=True, stop=(not hasmask)), [self.TK[kt], self.TQ], [Tsc])
                if hasmask:
                    d0 = i * W + dj * TP
                    self.PE(lambda e: e.matmul(sc[:nk, d0:d0 + TP], C["m_strict"][:TP, :nk], C["bigi"][:TP, 0:TP],
                                               start=False, stop=True), [self.TC], [Tsc])
            lo = c0 if ns == 1 else 0
            self.ACT(lambda e: e.activation(out=ES[:nk, kt, lo:NC], in_=sc[:nk, lo:NC], func=AF.Exp, scale=scale), [Tsc], [self.TSP[kt]])
            self.ACT(lambda e: e.activation(out=SP[:nk, kt, lo:NC], in_=ES[:nk, kt, lo:NC], func=AF.Ln, bias=1.0),
                     [self.TSP[kt]], [self.TSP[kt]])
        acc, Tacc = self.acc_bank()
        n = ns * nsub
        self.zero_acc(acc, Tacc, TP, n * 64)
        accv = acc[:TP, 0:n * 64].rearrange("p (n d) -> p n d", n=n)
        for idx, (kt, j0, dj) in enumerate(kts):
            kcol, nk = self.kt_info(kt)
            c0 = j0 * TP
            lo = c0 if ns == 1 else 0
            ci, Tci = self.psum()
            later = kts[idx:]
            for li, (kt2, j02, dj2) in enumerate(later):
                kcol2, nk2 = self.kt_info(kt2)
                lo2 = (j02 * TP) if ns == 1 else 0
                lhs = C["tri"][:nk2, :nk] if li == 0 else C["ones"][:nk2, :nk]
                self.PE(lambda e: e.matmul(ci[:nk, lo2:NC], lhs, SP[:nk2, kt2, lo2:NC], start=(li == 0), stop=(li == len(later) - 1)),
                        [self.TSP[kt2], self.TC], [Tci])
            pt, Tpt = self.ring(self.PT)
            self.ACT(lambda e: e.activation(out=pt[:nk, lo:NC], in_=ci[:nk, lo:NC], func=AF.Exp, scale=-1.0), [Tci], [Tpt])
            self.DVE(lambda e: e.tensor_tensor(out=pt[:nk, lo:NC], in0=pt[:nk, lo:NC], in1=ES[:nk, kt, lo:NC], op=ALU.mult),
                     [Tpt, self.TSP[kt]], [Tpt])
            for i, h in enumerate(heads):
                for j in range(j0, nsub):
                    a0 = (i * nsub + j) * 64
                    self.PE(lambda e: e.matmul(acc[:TP, a0:a0 + 64], pt[:nk, i * W + j * TP:i * W + (j + 1) * TP],
                                               self.V[:nk, kt, h, 0:64], start=False, stop=True, skip_group_check=True), [Tpt, self.TV[kt]], [Tacc])
        self.evac_copy(dest, accv, [Tacc], Tdest)

    def dsa_select(self, qslot, wslot, TP, nkv, prompt):
        IB, MB, JK = self.IB, self.MB, self.JK
        qs = slice(qslot * TP, (qslot + 1) * TP)
        nsel = self.nsel
        for b0 in range(0, nkv, 512):
            nb = min(512, nkv - b0)
            for h in range(8):
                pi, Tpi = self.psum()
                kt_lo, kt_hi = b0 // 128, (b0 + nb + 127) // 128
                self.PE(lambda e: e.matmul(pi[:TP, 0:nb], self.QT[0:64, 4 + h, qs], self.KT[0:64, 1, b0:b0 + nb], start=True, stop=True),
                        [self.TQ] + self.TK[kt_lo:kt_hi], [Tpi])
                rb, Trb = self.ring(self.RB)
                self.ACT(lambda e: e.activation(out=rb[:TP, 0:nb], in_=pi[:TP, 0:nb], func=AF.Relu, scale=self.WAB[:TP, wslot, h:h + 1]),
                         [Tpi, self.TWAB[wslot]], [Trb])
                if h == 0:
                    self.DVE(lambda e: e.tensor_scalar(out=IB[:TP, b0:b0 + nb], in0=rb[:TP, 0:nb], scalar1=self.WSG[:TP, wslot, 0:1],
                                                       scalar2=None, op0=ALU.mult), [Trb, self.TWAB[wslot]], [self.TIB])
                else:
                    self.DVE(lambda e: e.scalar_tensor_tensor(out=IB[:TP, b0:b0 + nb], in0=rb[:TP, 0:nb], scalar=self.WSG[:TP, wslot, h:h + 1],
                                                              in1=IB[:TP, b0:b0 + nb], op0=ALU.mult, op1=ALU.add),
                             [Trb, self.TWAB[wslot], self.TIB], [self.TIB])
        bs, Tbs = self.ring(self.BIS)
        NIT = 26
        need_search = (nkv > nsel)
        if need_search:
            self.DVE(lambda e: e.tensor_reduce(out=bs[:TP, 0:1], in_=IB[:TP, 0:nkv], axis=AX.X, op=ALU.max), [self.TIB], [Tbs])
            self.DVE(lambda e: e.tensor_reduce(out=bs[:TP, 1:2], in_=IB[:TP, 0:nkv], axis=AX.X, op=ALU.min), [self.TIB], [Tbs])
        if prompt:
            self.DVE(lambda e: e.memset(IB[0:64, nkv - 64:nkv], -3.0e38), [self.TIB], [self.TIB])
        if not need_search:
            self.DVE(lambda e: e.memset(bs[:TP, 4:5], -1.0e38), [], [Tbs])
            thr = bs[:TP, 4:5]
        else:
            self.DVE(lambda e: e.tensor_tensor(out=bs[:TP, 2:3], in0=bs[:TP, 0:1], in1=bs[:TP, 1:2], op=ALU.subtract), [Tbs], [Tbs])
            self.DVE(lambda e: e.tensor_scalar(out=bs[:TP, 2:3], in0=bs[:TP, 2:3], scalar1=1.0 + 2.0 ** -10, scalar2=1e-30,
                                               op0=ALU.mult, op1=ALU.add), [Tbs], [Tbs])
            steps = bs[:TP, 8:8 + NIT + 1]
            self.DVE(lambda e: e.tensor_scalar(out=steps, in0=self.C["pow2"][:TP, 0:NIT + 1], scalar1=bs[:TP, 2:3], scalar2=None,
                                               op0=ALU.mult), [Tbs, self.TC], [Tbs])
            self.DVE(lambda e: e.tensor_tensor(out=bs[:TP, 4:5], in0=bs[:TP, 1:2], in1=bs[:TP, 8:9], op=ALU.add), [Tbs], [Tbs])
            cand = bs[:TP, 4:5]
            for it in range(1, NIT + 1):
                self.DVE(lambda e: e.tensor_scalar(out=JK[:TP, 0:nkv], in0=IB[:TP, 0:nkv], scalar1=cand, scalar2=0.0,
                                                   op0=ALU.is_ge, op1=ALU.add, accum_out=bs[:TP, 5:6]), [self.TIB, Tbs], [self.TJK, Tbs])
                self.DVE(lambda e: e.tensor_scalar(out=bs[:TP, 6:7], in0=bs[:TP, 5:6], scalar1=float(nsel) - 0.5,
                                                   scalar2=bs[:TP, 8 + it - 1:8 + it], op0=ALU.is_ge, op1=ALU.mult), [Tbs], [Tbs])
                nxt = it if it < NIT else it - 1
                self.DVE(lambda e: e.scalar_tensor_tensor(out=cand, in0=cand, scalar=bs[:TP, 8 + nxt:8 + nxt + 1], in1=bs[:TP, 6:7],
                                                          op0=ALU.subtract, op1=ALU.add), [Tbs], [Tbs])
            thr = cand
        self.DVE(lambda e: e.tensor_scalar(out=MB[:TP, 0:nkv], in0=IB[:TP, 0:nkv], scalar1=thr, scalar2=1.0,
                                           op0=ALU.is_ge, op1=ALU.subtract), [self.TIB, Tbs], [self.TMB])

    def out_proj(self, tiles, oslots):
        TP = self.TP
        for t, os_ in zip(tiles, oslots):
            col = os_ * TP
            self.transposes_to([self.O[:TP, os_, 0:128], self.O[:TP, os_, 128:256]], self.OT[:, :, col:col + TP], TP, 128,
                               [self.TO[os_]], [self.TOT])
            for h in range(2):
                po, Tpo = self.psum()
                for kc in range(2):
                    self.PE(lambda e: e.matmul(po[:TP, :], self.OT[:, kc, col:col + TP], self.WO[:, kc, h * 512:(h + 1) * 512],
                                               start=(kc == 0), stop=(kc == 1)), [self.TOT, self.TWO], [Tpo])
                xs = self.X[:TP, t, h * 512:(h + 1) * 512]
                self.DVE(lambda e: e.tensor_tensor(out=xs, in0=xs, in1=po[:TP, :], op=ALU.add), [Tpo, self.TX[t]], [self.TX[t]])


def shard_inputs(cfg, inputs, n_cores):
    consts = host_consts(cfg)
    per = []
    f = lambda a: np.ascontiguousarray(np.asarray(a, dtype=np.float32))
    L = cfg.DEPTH
    shared = {
        "fnorm": f(inputs["final_norm"]),
        "w_in": f(inputs["w_in"]), "qn": f(inputs["mla_q_norm"]), "wqb": f(inputs["mla_w_qb"]),
        "kvn": f(inputs["mla_kv_norm"]), "wkvb": f(inputs["mla_w_kvb"]),
        "lam": f(np.asarray(inputs["diff_lambda"]).reshape(L, 128)), "dn": f(inputs["diff_norm"]),
        "wo": f(inputs["w_out"]),
        "wg1": f(inputs["w_ff1_gate"]), "wu1": f(inputs["w_ff1_up"]), "wd1": f(inputs["w_ff1_down"]),
        "wg2": f(inputs["w_ff2_gate"]), "wu2": f(inputs["w_ff2_up"]), "wd2": f(inputs["w_ff2_down"]),
    }
    for nm, src in (("n1", "norm_ff1"), ("nm", "norm_mix"), ("n2", "norm_ff2")):
        shared[nm] = f(np.asarray(inputs[src]).reshape(L, 8, 128).transpose(0, 2, 1))
    for nm, a in consts.items():
        shared["k_" + nm] = f(a)
    for c in range(n_cores):
        d = dict(shared)
        ps = slice(c * cfg.NPS, (c + 1) * cfg.NPS)
        ss = slice(c * cfg.NSS, (c + 1) * cfg.NSS)
        d["xp"] = f(inputs["x_prompt"][ps])
        d["xs"] = f(inputs["x_sample"][ss])
        d["c_mla"] = f(inputs["cache_mla"][:, ss])
        d["c_dsa"] = f(inputs["cache_dsa"][:, ss])
        d["c_sb"] = f(np.asarray(inputs["cache_sb"])[:, ss].reshape(L, cfg.NSS, cfg.PAST, 512))
        d["c_diff"] = f(np.asarray(inputs["cache_diff"])[:, ss].reshape(L, cfg.NSS, cfg.PAST, 512))
        per.append(d)
    return per


def gather_outputs(cfg, results, n_cores):
    L = cfg.DEPTH
    cat = lambda name, ax: np.concatenate([np.asarray(r[name]) for r in results], axis=ax)
    yp = cat("yp", 0)
    ys = cat("ys", 0)
    outs = [yp, ys]
    B = cfg.NPS * n_cores
    Bs = cfg.NSS * n_cores
    outs.append(cat("mla_p", 1))
    outs.append(cat("dsa_p", 1))
    outs.append(cat("sb_p", 1).reshape(L, B, cfg.S, 4, 128))
    outs.append(cat("diff_p", 1).reshape(L, B, cfg.S, 4, 128))
    outs.append(cat("mla_s", 1))
    outs.append(cat("dsa_s", 1))
    outs.append(cat("sb_s", 1).reshape(L, Bs, cfg.TS, 4, 128))
    outs.append(cat("diff_s", 1).reshape(L, Bs, cfg.TS, 4, 128))
    return tuple(np.ascontiguousarray(o.astype(np.float32, copy=False)) for o in outs)


def run(cfg, inputs, n_cores=8):
    b = Builder(cfg)
    nc = b.build()
    in_maps = shard_inputs(cfg, inputs, n_cores)
    res = run_bass_kernel_spmd(nc, in_maps, core_ids=list(range(n_cores)))
    return gather_outputs(cfg, res.results, n_cores)


def kernel(**inputs):
    cfg = Cfg()
    return run(cfg, inputs, 8)
```

```python
import math
from contextlib import ExitStack
import numpy as np
import concourse.bass as bass
import concourse.mybir as mybir
from concourse.bass_utils import run_bass_kernel_spmd

F32 = mybir.dt.float32
BF16 = mybir.dt.bfloat16
AF = mybir.ActivationFunctionType
ALU = mybir.AluOpType
AX = mybir.AxisListType

D = 1024
DFF = 2816
HD = 64
CHUNK = 64
EPS = 1e-6
THETA = 10000.0
IN_COLS = 2920
NEGB = 30000.0
C_CQ, C_CKV, C_KR = 0, 256, 384
C_QB, C_KB, C_VB, C_QIB, C_KIB, C_WIB = 416, 672, 736, 800, 1312, 1376
C_QC, C_KC, C_VC = 1384, 1640, 1896
C_QD, C_KD, C_VD = 2152, 2408, 2664


class T:
    __slots__ = ("w", "r", "excl")

    def __init__(self, excl=False):
        self.w = None
        self.r = {}
        self.excl = excl


class Ring(list):
    pass


class Eng:
    def __init__(self, name, be, sem, is_pe=False):
        self.name = name
        self.be = be
        self.sem = sem
        self.cnt = 0
        self.clock = {}
        self.is_pe = is_pe


class K:
    def __init__(self, nc, stack, n_dma_sems=48):
        self.nc = nc
        self.es = {}
        for name, be in (("pe", nc.tensor), ("act", nc.scalar), ("dve", nc.vector),
                         ("pool", nc.gpsimd), ("sp", nc.sync)):
            sem = stack.enter_context(nc.semaphore("s_" + name))
            self.es[name] = Eng(name, be, sem, is_pe=(name == "pe"))
        self.pe, self.act, self.dve, self.pool, self.sp = (
            self.es[n] for n in ("pe", "act", "dve", "pool", "sp"))
        self.dsems = []
        for i in range(n_dma_sems):
            sem = stack.enter_context(nc.semaphore("s_dma%d" % i))
            self.dsems.append([("dma", i), sem, 0])
        self.dnext = 0
        self.dnext2 = 0
        self.semof = {e.name: e.sem for e in self.es.values()}
        for d in self.dsems:
            self.semof[d[0]] = d[1]
        self.nins = 0
        self.nwaits = 0

    def _need(self, eng, deps):
        for key, c in deps.items():
            if key == eng.name and eng.is_pe:
                continue
            if c > eng.clock.get(key, 0):
                if key in self.es:
                    assert c <= self.es[key].cnt, "dependency on an unsignalled instruction"
                eng.be.wait_ge(self.semof[key], c)
                eng.clock[key] = c
                self.nwaits += 1

    @staticmethod
    def _collect(reads, writes):
        deps = {}
        for t in reads:
            if t.w is not None:
                kk, c = t.w
                if c > deps.get(kk, 0):
                    deps[kk] = c
        for t in writes:
            if t.w is not None:
                kk, c = t.w
                if c > deps.get(kk, 0):
                    deps[kk] = c
            for kk, c in t.r.items():
                if c > deps.get(kk, 0):
                    deps[kk] = c
        return deps

    def op(self, eng, fn, reads=(), writes=(), sig=True):
        ex = [t for t in reads if t.excl]
        if ex:
            reads = [t for t in reads if not t.excl]
            writes = list(writes) + ex
        deps = self._collect(reads, writes)
        self._need(eng, deps)
        ins = fn(eng.be)
        if sig:
            eng.cnt += 1
            ins.then_inc(eng.sem, 1)
            c = eng.cnt
        else:
            c = eng.cnt + 1
        self.nins += 1
        me = (eng.name, c)
        for t in writes:
            t.w = me
            t.r = {}
        for t in reads:
            if t.r.get(eng.name, 0) < c:
                t.r[eng.name] = c
        return ins

    def dma(self, eng, out, in_, reads=(), writes=()):
        half = len(self.dsems) // 2
        if eng.name == "pool":
            d = self.dsems[self.dnext]
            self.dnext = (self.dnext + 1) % half
        else:
            d = self.dsems[half + self.dnext2]
            self.dnext2 = (self.dnext2 + 1) % (len(self.dsems) - half)
        deps = self._collect(reads, writes)
        if d[2] > 0:
            deps[d[0]] = max(deps.get(d[0], 0), d[2])
        self._need(eng, deps)
        ins = eng.be.dma_start(out=out, in_=in_)
        d[2] += 16
        ins.then_inc(d[1], 16)
        self.nins += 1
        me = (d[0], d[2])
        for t in writes:
            t.w = me
            t.r = {}
        for t in reads:
            t.r[d[0]] = d[2]
        return ins

    def barrier(self):
        allv = {e.name: e.cnt for e in self.es.values() if e.cnt > 0}
        for d in self.dsems:
            if d[2] > 0:
                allv[d[0]] = d[2]
        for e in self.es.values():
            for key, c in allv.items():
                if key == e.name:
                    continue
                if c > e.clock.get(key, 0):
                    e.be.wait_ge(self.semof[key], c)
                    e.clock[key] = c
                    self.nwaits += 1

    def finish(self):
        self.barrier()


class Cfg:
    def __init__(self, NPS=4, S=2048, NSS=4, TS=64, PAST=4096, DEPTH=2, mixers=("mla", "dsa", "sb", "diff"),
                 ffn=True, run_sample=True):
        self.NPS, self.S, self.NSS, self.TS, self.PAST, self.DEPTH = NPS, S, NSS, TS, PAST, DEPTH
        self.mixers = mixers
        self.ffn = ffn
        self.run_sample = run_sample


def rope_tables(pos, d):
    half = d // 2
    freqs = (np.float32(THETA) ** (-np.arange(half, dtype=np.float32) * np.float32(2.0) / np.float32(d))).astype(np.float32)
    ang = pos.astype(np.float32)[:, None] * freqs[None, :]
    return np.cos(ang).astype(np.float32), np.sin(ang).astype(np.float32)


def host_consts(cfg):
    c = {}
    c["ident"] = np.eye(128, dtype=np.float32)
    c["bigi"] = np.tile(np.eye(128, dtype=np.float32) * NEGB, (1, 4))
    q = np.arange(128)[:, None]
    kk = np.arange(128)[None, :]
    c["m_chunk"] = np.where((kk // CHUNK) <= (q // CHUNK), 0.0, -1.0).astype(np.float32)
    c["m_strict"] = np.where(kk < q, 0.0, -1.0).astype(np.float32)
    c["tri"] = (q >= kk).astype(np.float32)
    c["ones"] = np.ones((128, 128), np.float32)
    def tabs(pos):
        c64, s64 = rope_tables(pos, 64)
        c32, s32 = rope_tables(pos, 32)
        return np.concatenate([c64, c64, -s64, s64, c32, c32, -s32, s32], axis=1).astype(np.float32)
    S = cfg.S
    tab = tabs(np.arange(S))
    c["rope_p"] = np.ascontiguousarray(tab.reshape(S // 128, 128, 192).transpose(1, 0, 2))
    tab2 = np.zeros((128, 1, 192), np.float32)
    tab2[:cfg.TS, 0] = tabs(cfg.PAST + np.arange(cfg.TS))
    c["rope_s"] = tab2
    c["zero"] = np.zeros((1, 512), np.float32)
    c["pow2"] = np.tile((0.5 ** np.arange(1, 33, dtype=np.float32))[None, :], (128, 1)).astype(np.float32)
    return c


class Builder:
    def __init__(self, cfg):
        self.cfg = cfg
        self.nc = bass.Bass("TRN2", target_bir_lowering=False)
        self.din = {}
        self.dout = {}

    def declare(self):
        cfg, nc = self.cfg, self.nc
        L = cfg.DEPTH

        def I(name, shape):
            self.din[name] = nc.dram_tensor(name, list(shape), F32, kind="ExternalInput").ap()

        def O(name, shape):
            self.dout[name] = nc.dram_tensor(name, list(shape), F32, kind="ExternalOutput").ap()

        I("xp", (cfg.NPS, cfg.S, D))
        I("xs", (cfg.NSS, cfg.TS, D))
        I("c_mla", (L, cfg.NSS, cfg.PAST, 160))
        I("c_dsa", (L, cfg.NSS, cfg.PAST, 192))
        I("c_sb", (L, cfg.NSS, cfg.PAST, 512))
        I("c_diff", (L, cfg.NSS, cfg.PAST, 512))
        for nm in ("n1", "nm", "n2"):
            I(nm, (L, 128, 8))
        I("fnorm", (D,))
        for nm in ("wg1", "wu1", "wg2", "wu2"):
            I(nm, (L, D, DFF))
        for nm in ("wd1", "wd2"):
            I(nm, (L, DFF, D))
        I("w_in", (L, D, IN_COLS))
        I("qn", (L, 256))
        I("wqb", (L, 256, 384))
        I("kvn", (L, 128))
        I("wkvb", (L, 128, 512))
        I("lam", (L, 128))
        I("dn", (L, 64))
        I("wo", (L, D, D))
        for nm, a in host_consts(cfg).items():
            I("k_" + nm, a.shape)
        O("yp", (cfg.NPS, cfg.S, D))
        O("ys", (cfg.NSS, cfg.TS, D))
        O("mla_p", (L, cfg.NPS, cfg.S, 160))
        O("dsa_p", (L, cfg.NPS, cfg.S, 192))
        O("sb_p", (L, cfg.NPS, cfg.S, 512))
        O("diff_p", (L, cfg.NPS, cfg.S, 512))
        O("mla_s", (L, cfg.NSS, cfg.TS, 160))
        O("dsa_s", (L, cfg.NSS, cfg.TS, 192))
        O("sb_s", (L, cfg.NSS, cfg.TS, 512))
        O("diff_s", (L, cfg.NSS, cfg.TS, 512))

    def sb(self, stack, name, shape, dtype):
        self.uid = getattr(self, "uid", 0) + 1
        return stack.enter_context(self.nc.sbuf_tensor("%s_%d" % (name, self.uid), list(shape), dtype))

    def PE(self, fn, r=(), w=(), sig=True):
        return self.k.op(self.k.pe, fn, r, w, sig)

    def ACT(self, fn, r=(), w=()):
        return self.k.op(self.k.act, fn, r, w)

    def DVE(self, fn, r=(), w=()):
        return self.k.op(self.k.dve, fn, r, w)

    def POOL(self, fn, r=(), w=()):
        return self.k.op(self.k.pool, fn, r, w)

    def psum(self):
        i = self.ps_next
        self.ps_next = (self.ps_next + 1) % len(self.ps_rot)
        return self.ps_rot[i]

    def ring(self, lst):
        idx = lst.idx
        lst.idx = (idx + 1) % len(lst)
        return lst[idx]

    def mkring(self, stack, name, shape, dtype, n):
        r = Ring((self.sb(stack, "%s%d" % (name, i), shape, dtype), T()) for i in range(n))
        r.idx = 0
        return r

    def build(self):
        cfg, nc = self.cfg, self.nc
        self.declare()
        with ExitStack() as gs:
            self.k = K(nc, gs)
            k = self.k
            self.rings = {}
            self.banks = []
            for i in range(8):
                pt = gs.enter_context(nc.psum_tensor("psb%d" % i, [128, 512], F32))
                self.banks.append((pt, T(excl=True)))
            self.ps_rot = self.banks[0:5]
            self.ps_next = 0
            self.ps_acc = self.banks[5:8]
            self.load_consts(gs)
            for n in range(cfg.NPS):
                self.run_job("p", n)
                k.barrier()
            if cfg.NSS > 0 and cfg.run_sample:
                self.run_job("s", 0)
                k.barrier()
            k.finish()
            self.stats = (k.nins, k.nwaits)
        return nc

    def load_consts(self, gs):
        cfg, k = self.cfg, self.k
        din = self.din
        C = {}
        TC = T()
        self.TC = TC

        def cst(name, shape, dtype, src):
            t = self.sb(gs, "c_" + name, shape, dtype)
            k.dma(k.pool, t[:], src, writes=[TC])
            C[name] = t
            return t

        cst("ident", (128, 128), BF16, din["k_ident"][:, :])
        cst("bigi", (128, 512), BF16, din["k_bigi"][:, :])
        cst("m_chunk", (128, 128), BF16, din["k_m_chunk"][:, :])
        cst("m_strict", (128, 128), BF16, din["k_m_strict"][:, :])
        cst("tri", (128, 128), BF16, din["k_tri"][:, :])
        cst("ones", (128, 128), BF16, din["k_ones"][:, :])
        cst("pow2", (128, 32), F32, din["k_pow2"][:, :])
        cst("zero", (1, 512), BF16, din["k_zero"][:, :])
        L = cfg.DEPTH
        for nm in ("n1", "nm", "n2"):
            cst(nm, (128, L, 8), F32, din[nm].rearrange("l p c -> p l c"))
        cst("qn", (128, L, 256), F32, din["qn"].partition_broadcast(128))
        cst("kvn", (128, L, 128), F32, din["kvn"].partition_broadcast(128))
        cst("lam", (128, L, 128), F32, din["lam"].partition_broadcast(128))
        cst("dn", (128, L, 64), F32, din["dn"].partition_broadcast(128))
        self.C = C

    def run_job(self, kind, n):
        cfg, k, nc = self.cfg, self.k, self.nc
        if kind == "p":
            NT, TP = cfg.S // 128, 128
        else:
            NT, TP = cfg.NSS, cfg.TS
        self.kind, self.NT, self.TP, self.jn = kind, NT, TP, n
        ntok = NT * TP
        self.ntok = ntok
        tpg = 512 // TP
        self.groups = [(g0, min(tpg, NT - g0)) for g0 in range(0, NT, tpg)]
        with ExitStack() as js:
            self.X = self.sb(js, "X", (128, NT, D), F32)
            self.TX = [T() for _ in range(NT)]
            self.XNT = self.sb(js, "XNT", (128, 8, ntok), BF16)
            self.TXNT = [T() for _ in self.groups]
            self.small = self.mkring(js, "small", (128, 8), F32, 6)
            for t in range(NT):
                if kind == "p":
                    src = self.din["xp"][n, t * 128:(t + 1) * 128, :]
                else:
                    src = self.din["xs"][t, :, :]
                k.dma(k.sp, self.X[:TP, t, :], src, writes=[self.TX[t]])
            for l in range(cfg.DEPTH):
                if cfg.ffn:
                    self.norm_transpose(self.C["n1"][:, l, :])
                    self.ffn(l, "1")
                    k.barrier()
                if cfg.mixers:
                    self.norm_transpose(self.C["nm"][:, l, :])
                    self.mixer(l)
                    k.barrier()
                if cfg.ffn:
                    self.norm_transpose(self.C["n2"][:, l, :])
                    self.ffn(l, "2")
                    k.barrier()
            self.final_norm()

    def rstd_of(self, ss_ap, TP, Tss, n_feat):
        sm, Tsm = self.ring(self.small)
        self.DVE(lambda e: e.tensor_scalar(out=sm[:TP, 0:1], in0=ss_ap, scalar1=1.0 / n_feat, scalar2=EPS,
                                           op0=ALU.mult, op1=ALU.add), [Tss], [Tsm])
        self.ACT(lambda e: e.activation(out=sm[:TP, 1:2], in_=sm[:TP, 0:1], func=AF.Ln), [Tsm], [Tsm])
        self.ACT(lambda e: e.activation(out=sm[:TP, 2:3], in_=sm[:TP, 1:2], func=AF.Exp, scale=-0.5), [Tsm], [Tsm])
        return sm[:TP, 2:3], Tsm

    def norm_transpose(self, G):
        TP, NT = self.TP, self.NT
        C = self.C
        tpg = 512 // TP
        ns_ = ExitStack()
        xsr = self.mkring(ns_, "xs", (128, D), BF16, 2)
        junk = self.mkring(ns_, "junk", (128, D), BF16, 2)
        for t in range(NT):
            g = t // tpg
            sm, Tsm = self.ring(self.small)
            jk, Tjk = self.ring(junk)
            self.ACT(lambda e: e.activation(out=jk[:TP, :], in_=self.X[:TP, t, :], func=AF.Square,
                                            accum_out=sm[:TP, 3:4]), [self.TX[t]], [Tjk, Tsm])
            rs, Trs = self.rstd_of(sm[:TP, 3:4], TP, Tsm, D)
            xs, Txs = self.ring(xsr)
            self.DVE(lambda e: e.tensor_scalar(out=xs[:TP, :], in0=self.X[:TP, t, :], scalar1=rs, scalar2=None,
                                               op0=ALU.mult), [self.TX[t], Trs], [Txs])
            pb, Tpb = self.psum()
            pv = pb[:].bitcast(BF16).rearrange("p (c t) -> p c t", c=8)
            for c in range(8):
                self.PE(lambda e: e.transpose(pv[:, c, :TP], xs[:TP, c * 128:(c + 1) * 128], C["ident"][:TP, :TP]),
                        [Txs, self.TC], [Tpb], sig=(c == 7))
            self.DVE(lambda e: e.tensor_tensor(out=self.XNT[:, :, t * TP:(t + 1) * TP], in0=pv[:, :, :TP],
                                               in1=G.unsqueeze(2).to_broadcast([128, 8, TP]), op=ALU.mult),
                     [Tpb, self.TC], [self.TXNT[g]])
        self.k.barrier()
        ns_.close()

    def ffn(self, l, which):
        cfg, k = self.cfg, self.k
        TP, NT = self.TP, self.NT
        wg = self.din["wg" + which][l].rearrange("(k p) f -> p k f", p=128)
        wu = self.din["wu" + which][l].rearrange("(k p) f -> p k f", p=128)
        wd = self.din["wd" + which][l].rearrange("(c p) d -> p c d", p=128)
        NCH = DFF // 128
        CG = 3
        cgs = [(c0, min(CG, NCH - c0)) for c0 in range(0, NCH, CG)]
        with ExitStack() as fs:
            WG = self.mkring(fs, "WG", (128, 8, CG * 128), BF16, 3)
            WU = self.mkring(fs, "WU", (128, 8, CG * 128), BF16, 3)
            WD = self.mkring(fs, "WD", (128, CG, D), BF16, 2)
            HT = self.mkring(fs, "HT", (128, CG, self.ntok), BF16, 2)
            SG = self.mkring(fs, "SG", (128, 512), BF16, 3)

            def gate_up(gi):
                c0, nch = cgs[gi]
                wgt, Twg = WG[gi % 3]
                wut, Twu = WU[gi % 3]
                wdt, Twd = WD[gi % 2]
                ht, Tht = HT[gi % 2]
                k.dma(k.pool, wgt[:, :, :nch * 128], wg[:, :, c0 * 128:(c0 + nch) * 128], writes=[Twg])
                k.dma(k.pool, wut[:, :, :nch * 128], wu[:, :, c0 * 128:(c0 + nch) * 128], writes=[Twu])
                k.dma(k.pool, wdt[:, :nch, :], wd[:, c0:c0 + nch, :], writes=[Twd])
                for g, (t0, nt) in enumerate(self.groups):
                    col0, ncol = t0 * TP, nt * TP
                    for ci in range(nch):
                        pg, Tpg = self.psum()
                        for kk in range(8):
                            self.PE(lambda e: e.matmul(pg[:, :ncol], wgt[:, kk, ci * 128:(ci + 1) * 128],
                                                       self.XNT[:, kk, col0:col0 + ncol], start=(kk == 0), stop=(kk == 7)),
                                    [Twg, self.TXNT[g]], [Tpg], sig=(kk == 7))
                        pu, Tpu = self.psum()
                        for kk in range(8):
                            self.PE(lambda e: e.matmul(pu[:, :ncol], wut[:, kk, ci * 128:(ci + 1) * 128],
                                                       self.XNT[:, kk, col0:col0 + ncol], start=(kk == 0), stop=(kk == 7)),
                                    [Twu, self.TXNT[g]], [Tpu], sig=(kk == 7))
                        sg, Tsg = self.ring(SG)
                        self.ACT(lambda e: e.activation(out=sg[:, :ncol], in_=pg[:, :ncol], func=AF.Silu), [Tpg], [Tsg])
                        self.DVE(lambda e: e.tensor_tensor(out=ht[:, ci, col0:col0 + ncol], in0=sg[:, :ncol],
                                                           in1=pu[:, :ncol], op=ALU.mult), [Tsg, Tpu], [Tht])

            def down(gi):
                c0, nch = cgs[gi]
                wdt, Twd = WD[gi % 2]
                ht, Tht = HT[gi % 2]
                for t in range(NT):
                    for h in range(2):
                        pd, Tpd = self.psum()
                        for ci in range(nch):
                            self.PE(lambda e: e.matmul(pd[:TP, :], ht[:, ci, t * TP:(t + 1) * TP],
                                                       wdt[:, ci, h * 512:(h + 1) * 512], start=(ci == 0), stop=(ci == nch - 1)),
                                    [Tht, Twd], [Tpd], sig=(ci == nch - 1))
                        xs = self.X[:TP, t, h * 512:(h + 1) * 512]
                        self.DVE(lambda e: e.scalar_tensor_tensor(out=xs, in0=pd[:TP, :], scalar=0.5, in1=xs,
                                                                  op0=ALU.mult, op1=ALU.add), [Tpd, self.TX[t]], [self.TX[t]])

            ng = len(cgs)
            gate_up(0)
            for gi in range(ng):
                if gi + 1 < ng:
                    gate_up(gi + 1)
                down(gi)

    def final_norm(self):
        k = self.k
        TP, NT = self.TP, self.NT
        with ExitStack() as fs:
            YT = self.mkring(fs, "YT", (128, D), F32, 2)
            junk = self.mkring(fs, "junk", (128, D), BF16, 2)
            GF = self.sb(fs, "gf", (128, D), F32)
            TGF = T()
            k.dma(k.pool, GF[:], self.din["fnorm"].partition_broadcast(128), writes=[TGF])
            for t in range(NT):
                sm, Tsm = self.ring(self.small)
                jk, Tjk = self.ring(junk)
                self.ACT(lambda e: e.activation(out=jk[:TP, :], in_=self.X[:TP, t, :], func=AF.Square,
                                                accum_out=sm[:TP, 3:4]), [self.TX[t]], [Tjk, Tsm])
                rs, Trs = self.rstd_of(sm[:TP, 3:4], TP, Tsm, D)
                yt, Tyt = self.ring(YT)
                self.DVE(lambda e: e.scalar_tensor_tensor(out=yt[:TP, :], in0=self.X[:TP, t, :], scalar=rs,
                                                          in1=GF[:TP, :], op0=ALU.mult, op1=ALU.mult),
                         [self.TX[t], Trs, TGF], [Tyt])
                if self.kind == "p":
                    dst = self.dout["yp"][self.jn, t * 128:(t + 1) * 128, :]
                else:
                    dst = self.dout["ys"][t, :, :]
                k.dma(k.sp, dst, yt[:TP, :], reads=[Tyt])
            k.barrier()

    MIX = {
        "mla": (0, 416, [(0, 416)], 0, 4, 4),
        "dsa": (416, 968, [(0, 384), (384, 512), (896, 72)], 256, 2, 12),
        "sb": (1384, 768, [(0, 512), (512, 256)], 512, 4, 4),
        "diff": (2152, 768, [(0, 512), (512, 256)], 768, 4, 4),
    }

    def mixer(self, l):
        for m in self.cfg.mixers:
            with ExitStack() as ms:
                self.mixer_pass(l, m, ms)
            self.k.barrier()

    def acc_bank(self):
        i = getattr(self, "acc_next", 0)
        self.acc_next = (i + 1) % len(self.ps_acc)
        return self.ps_acc[i]

    def mixer_pass(self, l, m, ms):
        cfg, k = self.cfg, self.k
        TP, NT, kind = self.TP, self.NT, self.kind
        c0, ncols, banks, orow0, nks, nqs = self.MIX[m]
        self.m, self.l = m, l
        if kind == "p":
            NK = cfg.S
        else:
            NK = cfg.PAST + cfg.TS
        self.NK = NK
        NKT = (NK + 127) // 128
        self.NKT = NKT
        self.nsel = min(256, NK // 4)
        rsrc = self.din["k_rope_p"] if kind == "p" else self.din["k_rope_s"]
        ntab = (cfg.S // 128) if kind == "p" else 1
        self.tab = None
        if m == "dsa":
            self.tab = self.sb(ms, "tab", (128, ntab, 128), F32)
            k.dma(k.pool, self.tab[:], rsrc[:, :, 0:128], writes=[self.TC])
        elif m in ("mla", "diff"):
            self.tab = self.sb(ms, "tab", (128, ntab, 64), F32)
            k.dma(k.pool, self.tab[:], rsrc[:, :, 128:192], writes=[self.TC])
        self.WIN = self.sb(ms, "WIN", (128, 8, ncols), BF16)
        self.TWIN = T()
        k.dma(k.pool, self.WIN[:], self.din["w_in"][l].rearrange("(k p) c -> p k c", p=128)[:, :, c0:c0 + ncols],
              writes=[self.TWIN])
        self.WO = self.sb(ms, "WO", (128, 2, D), BF16)
        self.TWO = T()
        k.dma(k.pool, self.WO[:], self.din["wo"][l, orow0:orow0 + 256, :].rearrange("(k p) d -> p k d", p=128),
              writes=[self.TWO])
        self.KT = self.sb(ms, "KT", (128, nks, NK), BF16)
        self.TK = [T() for _ in range(NKT)]
        nv = 1 if m == "dsa" else 4
        self.V = self.sb(ms, "V", (128, NKT, nv, 65), BF16)
        self.TV = [T() for _ in range(NKT)]
        self.QT = self.sb(ms, "QT", (128, nqs, 512), BF16)
        self.TQ = T()
        ntl = NT if kind == "s" else 4
        self.O = self.sb(ms, "O", (128, ntl, 256), BF16)
        self.TO = [T() for _ in range(ntl)]
        self.OT = self.sb(ms, "OT", (128, 2, 512), BF16)
        self.TOT = T()
        self.ROWS = self.mkring(ms, "ROWS", (128, 512), F32, 2)
        self.STG = self.mkring(ms, "STG", (128, 1024), BF16, 2)
        self.RT = self.mkring(ms, "RT", (128, 512), F32, 4)
        self.PT = self.mkring(ms, "PT", (128, 512), BF16, 4)
        self.NRM = self.mkring(ms, "NRM", (128, 8), F32, 4)
        TVall = self.TV
        self.DVE(lambda e: e.memset(self.V[:, :, :, 64:65], 1.0), [], TVall)
        if m == "mla":
            self.WQB = self.sb(ms, "WQB", (128, 2, 384), BF16)
            self.WKVB = self.sb(ms, "WKVB", (128, 512), BF16)
            self.TWM = T()
            k.dma(k.pool, self.WQB[:], self.din["wqb"][l].rearrange("(k p) c -> p k c", p=128), writes=[self.TWM])
            k.dma(k.pool, self.WKVB[:], self.din["wkvb"][l], writes=[self.TWM])
            self.CQT = self.mkring(ms, "CQT", (128, 2, 128), BF16, 2)
            self.LT = self.mkring(ms, "LT", (128, 128), BF16, 2)
        if m == "dsa":
            self.IB = self.sb(ms, "IB", (128, NK), F32)
            self.TIB = T()
            self.MBS = [(self.sb(ms, "MB%d" % i, (128, NK), BF16), T()) for i in range(2 if kind == "p" else 1)]
            self.JK = self.sb(ms, "JK", (128, NK), BF16)
            self.TJK = T()
            self.WAB = self.sb(ms, "WAB", (128, ntl, 8), F32)
            self.WSG = self.sb(ms, "WSG", (128, ntl, 8), F32)
            self.TWAB = [T() for _ in range(ntl)]
            self.RB = self.mkring(ms, "RB", (128, 512), F32, 2)
            self.BIS = self.mkring(ms, "BIS", (128, 48), F32, 2)
        if m == "sb":
            W = 256 if kind == "p" else 4 * TP
            self.SBW = W
            self.SP = self.sb(ms, "SP", (128, NKT, W), BF16)
            self.ES = self.sb(ms, "ES", (128, NKT, W), BF16)
            self.TSP = [T() for _ in range(NKT)]
        if m == "diff":
            self.OD = self.mkring(ms, "OD", (128, 8, 64), F32, 2)
            self.LAMF = self.sb(ms, "LAMF", (128, 8), F32)
            self.TLAM = T()
            self.diff_lambda(l)
        import os
        DBG = int(os.environ.get("MIXSTOP", "9"))
        if DBG <= 1:
            return
        if kind == "p":
            for g, (t0, nt) in enumerate(self.groups):
                for t in range(t0, t0 + nt):
                    self.proj_tile(t, qslot=t - t0, kt=t, kcol=t * 128, oslot=t - t0)
                if DBG <= 2:
                    continue
                self.attend_group(t0, nt)
                if DBG <= 3:
                    continue
                self.out_proj(list(range(t0, t0 + nt)), list(range(nt)))
        else:
            for s in range(NT):
                self.load_past(s)
                self.proj_tile(s, qslot=0, kt=NKT - 1, kcol=cfg.PAST, oslot=s)
                self.attend_sample(s)
            self.out_proj(list(range(NT)), list(range(NT)))

    def transposes_to(self, srcs, dst_ap, rows, nparts, reads, writes, dt=BF16):
        C = self.C
        pb, Tpb = self.psum()
        n = len(srcs)
        pv = pb[:].bitcast(BF16)[:, 0:n * 128].rearrange("p (c t) -> p c t", c=n)
        for i, sap in enumerate(srcs):
            self.PE(lambda e: e.transpose(pv[:nparts, i, :rows], sap, C["ident"][:rows, :rows]), list(reads) + [self.TC], [Tpb],
                    sig=(i == n - 1))
        self.evac_copy(dst_ap, pv[:nparts, :, :rows], [Tpb], writes)

    def evac_copy(self, dst, src, reads, writes):
        i = getattr(self, "_ec", 0)
        self._ec = i + 1
        import os
        if i % 2 == 0 or os.environ.get("ALLDVE") == "1":
            self.DVE(lambda e: e.tensor_copy(out=dst, in_=src), reads, writes)
        else:
            self.ACT(lambda e: e.activation(out=dst, in_=src, func=AF.Copy), reads, writes)

    def rope(self, src, dst, H, d, tslot, TP, reads, writes):
        half = d // 2
        toff = 0
        tab = self.tab
        CC = tab[:TP, tslot, toff:toff + d]
        SS = tab[:TP, tslot, toff + d:toff + 2 * d]
        a, Ta = self.ring(self.RT)
        b, Tb = self.ring(self.RT)
        av = a[:TP, 0:H * d].rearrange("p (h d) -> p h d", h=H)
        bv = b[:TP, 0:H * d].rearrange("p (h d) -> p h d", h=H)
        self.DVE(lambda e: e.tensor_tensor(out=av, in0=src, in1=CC.unsqueeze(1).to_broadcast([TP, H, d]), op=ALU.mult),
                 list(reads) + [self.TC], [Ta])
        self.DVE(lambda e: e.tensor_tensor(out=bv[:, :, 0:half], in0=src[:, :, half:d],
                                           in1=SS[:, 0:half].unsqueeze(1).to_broadcast([TP, H, half]), op=ALU.mult),
                 list(reads) + [self.TC], [Tb])
        self.DVE(lambda e: e.tensor_tensor(out=bv[:, :, half:d], in0=src[:, :, 0:half],
                                           in1=SS[:, half:d].unsqueeze(1).to_broadcast([TP, H, half]), op=ALU.mult),
                 list(reads) + [self.TC], [Tb])
        self.DVE(lambda e: e.tensor_tensor(out=dst, in0=av, in1=bv, op=ALU.add), [Ta, Tb], writes)

    def project(self, t):
        TP = self.TP
        g = t // (512 // TP)
        outs = []
        for (b0, bn) in self.MIX[self.m][2]:
            pb, Tpb = self.psum()
            for kk in range(8):
                self.PE(lambda e: e.matmul(pb[:TP, :bn], self.XNT[:, kk, t * TP:(t + 1) * TP], self.WIN[:, kk, b0:b0 + bn],
                                           start=(kk == 0), stop=(kk == 7)), [self.TXNT[g], self.TWIN], [Tpb], sig=(kk == 7))
            outs.append((pb, Tpb))
        return outs

    def rows_out(self, rows, Trows, t, width):
        k = self.k
        name = {"mla": "mla", "dsa": "dsa", "sb": "sb", "diff": "diff"}[self.m]
        if self.kind == "p":
            dst = self.dout[name + "_p"][self.l, self.jn, t * 128:(t + 1) * 128, :]
        else:
            dst = self.dout[name + "_s"][self.l, t, :, :]
        k.dma(k.sp, dst, rows[:self.TP, 0:width], reads=[Trows])

    def proj_tile(self, t, qslot, kt, kcol, oslot):
        getattr(self, "proj_" + self.m)(t, qslot, kt, kcol, oslot)

    def mla_keys(self, lat, krope, nk, kt, kcol, reads):
        lt, Tlt = self.ring(self.LT)
        self.transposes_to([lat], lt[:, 0:nk].unsqueeze(1), nk, 128, reads, [Tlt])
        pkv, Tpkv = self.psum()
        self.PE(lambda e: e.matmul(pkv[:nk, :], lt[:, :nk], self.WKVB[:, :], start=True, stop=True), [Tlt, self.TWM], [Tpkv])
        st, Tst = self.ring(self.STG)
        kst = st[:nk, 0:384].rearrange("p (h d) -> p h d", h=4)
        pv = pkv[:nk, :].rearrange("p (h d) -> p h d", h=4)
        self.evac_copy(kst[:, :, 0:64], pv[:, :, 0:64], [Tpkv], [Tst])
        self.evac_copy(self.V[:nk, kt, :, 0:64], pv[:, :, 64:128], [Tpkv], [self.TV[kt]])
        self.evac_copy(kst[:, :, 64:96], krope.unsqueeze(1).to_broadcast([nk, 4, 32]), reads, [Tst])
        self.transposes_to([kst[:, h, :] for h in range(4)], self.KT[0:96, 0:4, kcol:kcol + nk], nk, 96, [Tst], [self.TK[kt]])

    def proj_mla(self, t, qslot, kt, kcol, oslot):
        TP, l, C = self.TP, self.l, self.C
        (pj, Tpj), = self.project(t)
        rows, Trows = self.ring(self.ROWS)
        st, Tst = self.ring(self.STG)
        jk, Tjk = self.ring(self.RT)
        nr, Tnr = self.ring(self.NRM)
        self.ACT(lambda e: e.activation(out=jk[:TP, 0:256], in_=pj[:TP, 0:256], func=AF.Square, accum_out=nr[:TP, 0:1]),
                 [Tpj], [Tjk, Tnr])
        self.ACT(lambda e: e.activation(out=jk[:TP, 256:384], in_=pj[:TP, 256:384], func=AF.Square, accum_out=nr[:TP, 1:2]),
                 [Tpj], [Tjk, Tnr])
        rq, Trq = self.rstd_of(nr[:TP, 0:1], TP, Tnr, 256)
        rk, Trk = self.rstd_of(nr[:TP, 1:2], TP, Tnr, 128)
        self.DVE(lambda e: e.scalar_tensor_tensor(out=st[:TP, 0:256], in0=pj[:TP, 0:256], scalar=rq, in1=C["qn"][:TP, l, :],
                                                  op0=ALU.mult, op1=ALU.mult), [Tpj, Trq, self.TC], [Tst])
        self.DVE(lambda e: e.scalar_tensor_tensor(out=rows[:TP, 0:128], in0=pj[:TP, 256:384], scalar=rk, in1=C["kvn"][:TP, l, :],
                                                  op0=ALU.mult, op1=ALU.mult), [Tpj, Trk, self.TC], [Trows])
        self.evac_copy(st[:TP, 256:384], rows[:TP, 0:128], [Trows], [Tst])
        self.rope(pj[:TP, 384:416].unsqueeze(1), rows[:TP, 128:160].unsqueeze(1), 1, 32, self.tslot(t), TP, [Tpj], [Trows])
        self.evac_copy(st[:TP, 384:416], rows[:TP, 128:160], [Trows], [Tst])
        import os
        ML = int(os.environ.get("MLASTOP", "9"))
        if ML <= 1:
            return
        self.rows_out(rows, Trows, t, 160)
        if ML <= 2:
            return
        cq, Tcq = self.ring(self.CQT)
        self.transposes_to([st[:TP, 0:128], st[:TP, 128:256]], cq[:, :, :TP], TP, 128, [Tst], [Tcq])
        pq, Tpq = self.psum()
        for kc in range(2):
            self.PE(lambda e: e.matmul(pq[:TP, 0:384], cq[:, kc, :TP], self.WQB[:, kc, :], start=(kc == 0), stop=(kc == 1)),
                    [Tcq, self.TWM], [Tpq], sig=(kc == 1))
        if ML <= 4:
            return
        sq, Tsq = self.ring(self.STG)
        pqv = pq[:TP, 0:384].rearrange("p (h d) -> p h d", h=4)
        sqv = sq[:TP, 0:384].rearrange("p (h d) -> p h d", h=4)
        self.evac_copy(sqv[:, :, 0:64], pqv[:, :, 0:64], [Tpq], [Tsq])
        if ML <= 5:
            return
        self.rope(pqv[:, :, 64:96], sqv[:, :, 64:96], 4, 32, self.tslot(t), TP, [Tpq], [Tsq])
        if ML <= 6:
            return
        self.transposes_to([sqv[:, h, :] for h in range(4)], self.QT[0:96, 0:4, qslot * TP:(qslot + 1) * TP], TP, 96, [Tsq], [self.TQ])
        if ML <= 7:
            return
        self.mla_keys(st[:TP, 256:384], st[:TP, 384:416], TP, kt, kcol, [Tst])

    def tslot(self, t):
        return t if self.kind == "p" else 0

    def proj_dsa(self, t, qslot, kt, kcol, oslot):
        TP = self.TP
        (pa, Tpa), (pb, Tpb), (pc, Tpc) = self.project(t)
        rows, Trows = self.ring(self.ROWS)
        st, Tst = self.ring(self.STG)
        ts = self.tslot(t)
        self.rope(pa[:TP, 0:256].rearrange("p (h d) -> p h d", h=4), st[:TP, 0:256].rearrange("p (h d) -> p h d", h=4),
                  4, 64, ts, TP, [Tpa], [Tst])
        self.rope(pa[:TP, 256:320].unsqueeze(1), rows[:TP, 0:64].unsqueeze(1), 1, 64, ts, TP, [Tpa], [Trows])
        self.evac_copy(rows[:TP, 64:128], pa[:TP, 320:384], [Tpa], [Trows])
        self.evac_copy(self.V[:TP, kt, 0, 0:64], pa[:TP, 320:384], [Tpa], [self.TV[kt]])
        self.rope(pc[:TP, 0:64].unsqueeze(1), rows[:TP, 128:192].unsqueeze(1), 1, 64, ts, TP, [Tpc], [Trows])
        self.ACT(lambda e: e.activation(out=self.WAB[:TP, oslot, :], in_=pc[:TP, 64:72], func=AF.Abs, scale=8.0 ** -0.5),
                 [Tpc], [self.TWAB[oslot]])
        self.DVE(lambda e: e.tensor_scalar(out=self.WSG[:TP, oslot, :], in0=pc[:TP, 64:72], scalar1=0.0, scalar2=2.0,
                                           op0=ALU.is_ge, op1=ALU.mult), [Tpc], [self.TWAB[oslot]])
        self.DVE(lambda e: e.tensor_scalar(out=self.WSG[:TP, oslot, :], in0=self.WSG[:TP, oslot, :], scalar1=-1.0, scalar2=None,
                                           op0=ALU.add), [self.TWAB[oslot]], [self.TWAB[oslot]])
        self.rope(pb[:TP, 0:512].rearrange("p (h d) -> p h d", h=8), st[:TP, 256:768].rearrange("p (h d) -> p h d", h=8),
                  8, 64, ts, TP, [Tpb], [Tst])
        self.evac_copy(st[:TP, 768:832], rows[:TP, 0:64], [Trows], [Tst])
        self.evac_copy(st[:TP, 832:896], rows[:TP, 128:192], [Trows], [Tst])
        self.rows_out(rows, Trows, t, 192)
        qs = slice(qslot * TP, (qslot + 1) * TP)
        self.transposes_to([st[:TP, h * 64:(h + 1) * 64] for h in range(4)], self.QT[0:64, 0:4, qs], TP, 64, [Tst], [self.TQ])
        self.transposes_to([st[:TP, 256 + h * 64:256 + (h + 1) * 64] for h in range(8)], self.QT[0:64, 4:12, qs], TP, 64, [Tst], [self.TQ])
        self.transposes_to([st[:TP, 768:832], st[:TP, 832:896]], self.KT[0:64, 0:2, kcol:kcol + TP], TP, 64, [Tst], [self.TK[kt]])

    def proj_sb(self, t, qslot, kt, kcol, oslot):
        TP = self.TP
        (pa, Tpa), (pb, Tpb) = self.project(t)
        rows, Trows = self.ring(self.ROWS)
        st, Tst = self.ring(self.STG)
        rv = rows[:TP, :].rearrange("p (h d) -> p h d", h=4)
        self.evac_copy(st[:TP, 0:256], pa[:TP, 0:256], [Tpa], [Tst])
        self.evac_copy(rv[:, :, 0:64], pa[:TP, 256:512].rearrange("p (h d) -> p h d", h=4), [Tpa], [Trows])
        self.evac_copy(st[:TP, 256:512].rearrange("p (h d) -> p h d", h=4), pa[:TP, 256:512].rearrange("p (h d) -> p h d", h=4), [Tpa], [Tst])
        self.evac_copy(rv[:, :, 64:128], pb[:TP, 0:256].rearrange("p (h d) -> p h d", h=4), [Tpb], [Trows])
        self.evac_copy(self.V[:TP, kt, :, 0:64], pb[:TP, 0:256].rearrange("p (h d) -> p h d", h=4), [Tpb], [self.TV[kt]])
        self.rows_out(rows, Trows, t, 512)
        qs = slice(qslot * TP, (qslot + 1) * TP)
        self.transposes_to([st[:TP, h * 64:(h + 1) * 64] for h in range(4)], self.QT[0:64, 0:4, qs], TP, 64, [Tst], [self.TQ])
        self.transposes_to([st[:TP, 256 + h * 64:256 + (h + 1) * 64] for h in range(4)], self.KT[0:64, 0:4, kcol:kcol + TP], TP, 64, [Tst], [self.TK[kt]])

    def diff_lambda(self, l):
        lam = self.C["lam"]
        lam_init = 0.8 - 0.6 * math.exp(-0.3 * l)
        self.lam_init = lam_init
        LF, TL = self.LAMF, self.TLAM
        jk, Tjk = self.ring(self.RT)
        self.DVE(lambda e: e.scalar_tensor_tensor(out=jk[:, 0:32], in0=lam[:, l, 0:32], scalar=1.0, in1=lam[:, l, 32:64],
                                                  op0=ALU.mult, op1=ALU.mult, accum_out=LF[:, 0:1]), [self.TC], [Tjk, TL])
        self.DVE(lambda e: e.scalar_tensor_tensor(out=jk[:, 32:64], in0=lam[:, l, 64:96], scalar=1.0, in1=lam[:, l, 96:128],
                                                  op0=ALU.mult, op1=ALU.mult, accum_out=LF[:, 1:2]), [self.TC], [Tjk, TL])
        self.ACT(lambda e: e.activation(out=LF[:, 2:4], in_=LF[:, 0:2], func=AF.Exp), [TL], [TL])
        self.DVE(lambda e: e.scalar_tensor_tensor(out=LF[:, 4:5], in0=LF[:, 3:4], scalar=-lam_init, in1=LF[:, 2:3],
                                                  op0=ALU.add, op1=ALU.subtract), [TL], [TL])

    def proj_diff(self, t, qslot, kt, kcol, oslot):
        TP = self.TP
        (pa, Tpa), (pb, Tpb) = self.project(t)
        rows, Trows = self.ring(self.ROWS)
        st, Tst = self.ring(self.STG)
        ts = self.tslot(t)
        rv = rows[:TP, :].rearrange("p (h d) -> p h d", h=4)
        self.rope(pa[:TP, 0:256].rearrange("p (h d) -> p h d", h=8), st[:TP, 0:256].rearrange("p (h d) -> p h d", h=8),
                  8, 32, ts, TP, [Tpa], [Tst])
        kf, Tkf = self.ring(self.RT)
        self.rope(pa[:TP, 256:512].rearrange("p (h d) -> p h d", h=8), kf[:TP, 0:256].rearrange("p (h d) -> p h d", h=8),
                  8, 32, ts, TP, [Tpa], [Tkf])
        self.evac_copy(rv[:, :, 0:64], kf[:TP, 0:256].rearrange("p (h d) -> p h d", h=4), [Tkf], [Trows])
        self.evac_copy(st[:TP, 256:512], kf[:TP, 0:256], [Tkf], [Tst])
        self.evac_copy(rv[:, :, 64:128], pb[:TP, 0:256].rearrange("p (h d) -> p h d", h=4), [Tpb], [Trows])
        self.evac_copy(self.V[:TP, kt, :, 0:64], pb[:TP, 0:256].rearrange("p (h d) -> p h d", h=4), [Tpb], [self.TV[kt]])
        self.rows_out(rows, Trows, t, 512)
        qs = slice(qslot * TP, (qslot + 1) * TP)
        self.transposes_to([st[:TP, h * 64:(h + 1) * 64] for h in range(4)], self.QT[0:64, 0:4, qs], TP, 64, [Tst], [self.TQ])
        self.transposes_to([st[:TP, 256 + h * 64:256 + (h + 1) * 64] for h in range(4)], self.KT[0:64, 0:4, kcol:kcol + TP], TP, 64, [Tst], [self.TK[kt]])

    def load_past(self, s):
        cfg, k, m, l = self.cfg, self.k, self.m, self.l
        PAST = cfg.PAST
        npt = PAST // 128
        cache = self.din["c_" + m][l, s]
        cv = cache.rearrange("(t p) c -> p t c", p=128)
        with ExitStack() as cs:
            if m == "mla":
                CST = self.mkring(cs, "CST", (128, 4, 160), BF16, 2)
            elif m == "dsa":
                CST = self.mkring(cs, "CST", (128, 4, 2, 64), BF16, 2)
            else:
                CST = self.mkring(cs, "CST", (128, 4, 4, 64), BF16, 2)
            for b0 in range(0, npt, 4):
                nb = min(4, npt - b0)
                c, Tc = self.ring(CST)
                if m == "mla":
                    k.dma(k.pool, c[:, 0:nb, :], cv[:, b0:b0 + nb, :], writes=[Tc])
                    for i in range(nb):
                        self.mla_keys(c[:, i, 0:128], c[:, i, 128:160], 128, b0 + i, (b0 + i) * 128, [Tc])
                elif m == "dsa":
                    k.dma(k.pool, c[:, 0:nb, 0, :], cv[:, b0:b0 + nb, 0:64], writes=[Tc])
                    k.dma(k.pool, c[:, 0:nb, 1, :], cv[:, b0:b0 + nb, 128:192], writes=[Tc])
                    k.dma(k.pool, self.V[:, b0:b0 + nb, 0, 0:64], cv[:, b0:b0 + nb, 64:128], writes=self.TV[b0:b0 + nb])
                    for st_ in range(2):
                        self.transposes_to([c[:, i, st_, :] for i in range(nb)],
                                           self.KT[0:64, st_, b0 * 128:(b0 + nb) * 128].rearrange("p (c t) -> p c t", c=nb),
                                           128, 64, [Tc], self.TK[b0:b0 + nb])
                else:
                    c4 = cv.rearrange("p t (h d) -> p t h d", h=4)
                    for i in range(nb):
                        k.dma(k.pool, c[:, i, :, :], c4[:, b0 + i, :, 0:64], writes=[Tc])
                        k.dma(k.pool, self.V[:, b0 + i, :, 0:64], c4[:, b0 + i, :, 64:128], writes=[self.TV[b0 + i]])
                    for h in range(4):
                        self.transposes_to([c[:, i, h, :] for i in range(nb)],
                                           self.KT[0:64, h, b0 * 128:(b0 + nb) * 128].rearrange("p (c t) -> p c t", c=nb),
                                           128, 64, [Tc], self.TK[b0:b0 + nb])
            self.k.barrier()

    def kt_info(self, kt):
        kcol = kt * 128
        return kcol, min(128, self.NK - kcol)

    def attend_group(self, t0, nt):
        m = self.m
        kts = []
        for kt in range(t0 + nt):
            j0 = max(0, kt - t0)
            kts.append((kt, j0, (j0 if kt >= t0 else None)))
        if m == "mla":
            for hp in range(0, 4, 2):
                self.softmax_multi([dict(streams=[dict(ks=h, qs=h, vs=h, p0=0, dk=96)], nsub=nt, TP=128, kts=kts, scale=96 ** -0.5,
                                         dest=self.O[:128, 0:nt, h * 64:(h + 1) * 64], Tdest=self.TO[0:nt], mask=self.C["m_chunk"],
                                         qoff=0, shared_k=False, mb=None) for h in (hp, hp + 1)])
        elif m == "diff":
            for h in range(4):
                od, Tod = self.ring(self.OD)
                self.softmax_multi([dict(streams=[dict(ks=h, qs=h, vs=h, p0=32 * mm, dk=32)], nsub=nt, TP=128, kts=kts, scale=32 ** -0.5,
                                         dest=od[:128, mm * 4:mm * 4 + nt, :], Tdest=[Tod], mask=self.C["m_chunk"],
                                         qoff=0, shared_k=False, mb=None) for mm in range(2)])
                self.diff_combine(od[:128, 0:nt, :], od[:128, 4:4 + nt, :], Tod, self.O[:128, 0:nt, h * 64:(h + 1) * 64],
                                  self.TO[0:nt], 128, nt)
        elif m == "sb":
            for hf in range(0, nt, 2):
                ns2 = min(2, nt - hf)
                kts2 = []
                for kt in range(t0 + hf + ns2):
                    j0 = max(0, kt - (t0 + hf))
                    kts2.append((kt, j0, (j0 if kt >= t0 + hf else None)))
                for h in range(4):
                    self.sb_unit([h], ns2, 128, kts2, dest=self.O[:128, hf:hf + ns2, h * 64:(h + 1) * 64],
                                 Tdest=self.TO[hf:hf + ns2], qoff=hf * 128)
        elif m == "dsa":
            def att(j):
                t = t0 + j
                ktj = [(kt, 0, None) for kt in range(t + 1)]
                self.softmax_unit([dict(ks=0, qs=h, vs=0, p0=0, dk=64) for h in range(4)], 1, 128, ktj, 64 ** -0.5,
                                  dest=self.O[:128, j, :].rearrange("p (h d) -> p h d", h=4), Tdest=[self.TO[j]],
                                  mask=None, qoff=j * 128, shared_k=True, mb=self.MBS[(t0 + j) % 2])
            self.dsa_select(0, 0, 128, (t0 + 1) * 128, prompt=True, mbuf=self.MBS[t0 % 2])
            for j in range(nt):
                if j + 1 < nt:
                    self.dsa_select(j + 1, j + 1, 128, (t0 + j + 2) * 128, prompt=True, mbuf=self.MBS[(t0 + j + 1) % 2])
                att(j)

    def attend_sample(self, s):
        m, TP = self.m, self.TP
        kts = [(kt, 0, None) for kt in range(self.NKT)]
        if m == "mla":
            self.softmax_unit([dict(ks=h, qs=h, vs=h, p0=0, dk=96) for h in range(4)], 1, TP, kts, 96 ** -0.5,
                              dest=self.O[:TP, s, :].rearrange("p (h d) -> p h d", h=4), Tdest=[self.TO[s]], mask=None)
        elif m == "diff":
            od, Tod = self.ring(self.OD)
            for mm in range(2):
                self.softmax_unit([dict(ks=h, qs=h, vs=h, p0=32 * mm, dk=32) for h in range(4)], 1, TP, kts, 32 ** -0.5,
                                  dest=od[:TP, mm * 4:mm * 4 + 4, :], Tdest=[Tod], mask=None)
            self.diff_combine(od[:TP, 0:4, :], od[:TP, 4:8, :], Tod, self.O[:TP, s, :].rearrange("p (h d) -> p h d", h=4),
                              [self.TO[s]], TP, 4)
        elif m == "sb":
            kts2 = [(kt, 0, (0 if kt == self.NKT - 1 else None)) for kt in range(self.NKT)]
            self.sb_unit([0, 1, 2, 3], 1, TP, kts2, dest=self.O[:TP, s, :].rearrange("p (h d) -> p h d", h=4), Tdest=[self.TO[s]])
        elif m == "dsa":
            self.dsa_select(0, s, TP, self.NK, prompt=False, mbuf=self.MBS[0])
            self.softmax_unit([dict(ks=0, qs=h, vs=0, p0=0, dk=64) for h in range(4)], 1, TP, kts, 64 ** -0.5,
                              dest=self.O[:TP, s, :].rearrange("p (h d) -> p h d", h=4), Tdest=[self.TO[s]],
                              mask=None, shared_k=True, mb=self.MBS[0])

    def zero_acc(self, acc, Tacc, rows, ncols):
        Z = self.C["zero"]
        self.PE(lambda e: e.matmul(acc[:rows, 0:ncols], Z[0:1, 0:rows], Z[0:1, 0:ncols], start=True, stop=False, skip_group_check=True),
                [self.TC], [Tacc], sig=False)

    def softmax_unit(self, streams, nsub, TP, kts, scale, dest, Tdest, mask, qoff=0, shared_k=False, mb=None):
        self.softmax_multi([dict(streams=streams, nsub=nsub, TP=TP, kts=kts, scale=scale, dest=dest, Tdest=Tdest,
                                 mask=mask, qoff=qoff, shared_k=shared_k, mb=mb)])

    def softmax_multi(self, units):
        for u in units:
            ns = len(u["streams"])
            u["ns"] = ns
            u["W"] = u["nsub"] * u["TP"]
            u["n"] = ns * u["nsub"]
            u["acc"], u["Tacc"] = self.acc_bank()
            self.zero_acc(u["acc"], u["Tacc"], u["TP"], u["n"] * 65)
            u["sc"] = {}
            u["pt"] = {}

        def stageA(u, i):
            kt, j0, dj = u["kts"][i]
            TP, W, ns, qoff, streams = u["TP"], u["W"], u["ns"], u["qoff"], u["streams"]
            kcol, nk = self.kt_info(kt)
            sc, Tsc = self.psum()
            u["sc"][i] = (sc, Tsc)
            c0 = j0 * TP
            if u["shared_k"]:
                s0 = streams[0]
                mb = u["mb"]
                self.PE(lambda e: e.matmul(sc[:nk, 0:ns * W].rearrange("p (s w) -> p s w", s=ns),
                                           self.KT[0:64, s0["ks"], kcol:kcol + nk],
                                           self.QT[0:64, 0:ns, qoff:qoff + W], start=True, stop=(mb is None)),
                        [self.TK[kt], self.TQ], [Tsc], sig=(mb is None))
                if mb is not None:
                    bi = self.C["bigi"][:TP, :].rearrange("p (s w) -> p s w", s=4)[:, 0:ns, 0:TP]
                    self.PE(lambda e: e.matmul(sc[:nk, 0:ns * W].rearrange("p (s w) -> p s w", s=ns),
                                               mb[0][:TP, kcol:kcol + nk], bi, start=False, stop=True),
                            [mb[1], self.TC], [Tsc])
            else:
                mask = u["mask"]
                for si, s_ in enumerate(streams):
                    p0, dk = s_["p0"], s_["dk"]
                    hasmask = (dj is not None and mask is not None)
                    last = (si == ns - 1)
                    self.PE(lambda e: e.matmul(sc[:nk, si * W + c0:(si + 1) * W], self.KT[p0:p0 + dk, s_["ks"], kcol:kcol + nk],
                                               self.QT[p0:p0 + dk, s_["qs"], qoff + c0:qoff + W], start=True, stop=(not hasmask)),
                            [self.TK[kt], self.TQ], [Tsc], sig=(last and not hasmask))
                    if hasmask:
                        d0 = si * W + dj * TP
                        self.PE(lambda e: e.matmul(sc[:nk, d0:d0 + TP], mask[:TP, :nk], self.C["bigi"][:TP, 0:TP],
                                                   start=False, stop=True), [self.TC], [Tsc], sig=last)

        def stageB(u, i):
            kt, j0, dj = u["kts"][i]
            TP, W, ns = u["TP"], u["W"], u["ns"]
            kcol, nk = self.kt_info(kt)
            sc, Tsc = u["sc"].pop(i)
            pt, Tpt = self.ring(self.PT)
            u["pt"][i] = (pt, Tpt)
            c0 = j0 * TP
            lo = c0 if ns == 1 else 0
            self.ACT(lambda e: e.activation(out=pt[:nk, lo:ns * W], in_=sc[:nk, lo:ns * W], func=AF.Exp, scale=u["scale"]),
                     [Tsc], [Tpt])

        def stageC(u, i):
            kt, j0, dj = u["kts"][i]
            TP, W, nsub = u["TP"], u["W"], u["nsub"]
            kcol, nk = self.kt_info(kt)
            pt, Tpt = u["pt"].pop(i)
            acc, Tacc = u["acc"], u["Tacc"]
            nst = len(u["streams"])
            for si, s_ in enumerate(u["streams"]):
                for j in range(j0, nsub):
                    a0 = (si * nsub + j) * 65
                    self.PE(lambda e: e.matmul(acc[:TP, a0:a0 + 65], pt[:nk, si * W + j * TP:si * W + (j + 1) * TP],
                                               self.V[:nk, kt, s_["vs"], :], start=False, stop=True, skip_group_check=True),
                            [Tpt, self.TV[kt]], [Tacc], sig=(si == nst - 1 and j == nsub - 1))

        maxlen = max(len(u["kts"]) for u in units)
        for u in units:
            stageA(u, 0)
        for i in range(maxlen):
            for u in units:
                if i + 1 < len(u["kts"]):
                    stageA(u, i + 1)
            for u in units:
                if i < len(u["kts"]):
                    stageB(u, i)
            for u in units:
                if i < len(u["kts"]):
                    stageC(u, i)
        for u in units:
            TP, n = u["TP"], u["n"]
            accv = u["acc"][:TP, 0:n * 65].rearrange("p (n d) -> p n d", n=n)
            nr, Tnr = self.ring(self.NRM)
            rcp = nr[:TP, 0:n]
            self.DVE(lambda e: e.reciprocal(out=rcp, in_=accv[:, :, 64]), [u["Tacc"]], [Tnr])
            self.DVE(lambda e: e.tensor_tensor(out=u["dest"], in0=accv[:, :, 0:64], in1=rcp.unsqueeze(2).to_broadcast([TP, n, 64]),
                                               op=ALU.mult), [u["Tacc"], Tnr], u["Tdest"])

    def diff_combine(self, o1, o2, Tod, dest, Tdest, TP, n):
        l = self.l
        a, Ta = self.ring(self.RT)
        b, Tb = self.ring(self.RT)
        av = a[:TP, 0:n * 64].rearrange("p (n d) -> p n d", n=n)
        bv = b[:TP, 0:n * 64].rearrange("p (n d) -> p n d", n=n)
        self.DVE(lambda e: e.scalar_tensor_tensor(out=av, in0=o2, scalar=self.LAMF[:TP, 4:5], in1=o1, op0=ALU.mult, op1=ALU.add),
                 [Tod, self.TLAM], [Ta])
        self.DVE(lambda e: e.tensor_tensor(out=bv, in0=av, in1=av, op=ALU.mult), [Ta], [Tb])
        nr, Tnr = self.ring(self.NRM)
        self.DVE(lambda e: e.tensor_reduce(out=nr[:TP, 0:n], in_=bv, axis=AX.X, op=ALU.add), [Tb], [Tnr])
        self.DVE(lambda e: e.tensor_scalar(out=nr[:TP, 0:n], in0=nr[:TP, 0:n], scalar1=1.0 / 64, scalar2=EPS, op0=ALU.mult, op1=ALU.add),
                 [Tnr], [Tnr])
        self.ACT(lambda e: e.activation(out=nr[:TP, 0:n], in_=nr[:TP, 0:n], func=AF.Ln), [Tnr], [Tnr])
        self.ACT(lambda e: e.activation(out=nr[:TP, 0:n], in_=nr[:TP, 0:n], func=AF.Exp, scale=-0.5), [Tnr], [Tnr])
        self.DVE(lambda e: e.tensor_tensor(out=bv, in0=av, in1=nr[:TP, 0:n].unsqueeze(2).to_broadcast([TP, n, 64]), op=ALU.mult),
                 [Ta, Tnr], [Tb])
        self.DVE(lambda e: e.scalar_tensor_tensor(out=dest, in0=bv, scalar=1.0 - self.lam_init,
                                                  in1=self.C["dn"][:TP, l, :].unsqueeze(1).to_broadcast([TP, n, 64]),
                                                  op0=ALU.mult, op1=ALU.mult), [Tb, self.TC], Tdest)

    def sb_unit(self, heads, nsub, TP, kts, dest, Tdest, qoff=0):
        ns = len(heads)
        W = nsub * TP
        NC = ns * W
        scale = 64 ** -0.5
        C = self.C
        SP, ES = self.SP, self.ES
        for (kt, j0, dj) in kts:
            kcol, nk = self.kt_info(kt)
            sc, Tsc = self.psum()
            c0 = j0 * TP
            for i, h in enumerate(heads):
                hasmask = dj is not None
                last = (i == ns - 1)
                self.PE(lambda e: e.matmul(sc[:nk, i * W + c0:(i + 1) * W], self.KT[0:64, h, kcol:kcol + nk],
                                           self.QT[0:64, h, qoff + c0:qoff + W], start=True, stop=(not hasmask)), [self.TK[kt], self.TQ], [Tsc],
                        sig=(last and not hasmask))
                if hasmask:
                    d0 = i * W + dj * TP
                    self.PE(lambda e: e.matmul(sc[:nk, d0:d0 + TP], C["m_strict"][:TP, :nk], C["bigi"][:TP, 0:TP],
                                               start=False, stop=True), [self.TC], [Tsc], sig=last)
            lo = c0 if ns == 1 else 0
            self.ACT(lambda e: e.activation(out=ES[:nk, kt, lo:NC], in_=sc[:nk, lo:NC], func=AF.Exp, scale=scale), [Tsc], [self.TSP[kt]])
            self.ACT(lambda e: e.activation(out=SP[:nk, kt, lo:NC], in_=ES[:nk, kt, lo:NC], func=AF.Ln, bias=1.0),
                     [self.TSP[kt]], [self.TSP[kt]])
        acc, Tacc = self.acc_bank()
        n = ns * nsub
        self.zero_acc(acc, Tacc, TP, n * 64)
        accv = acc[:TP, 0:n * 64].rearrange("p (n d) -> p n d", n=n)
        for idx, (kt, j0, dj) in enumerate(kts):
            kcol, nk = self.kt_info(kt)
            c0 = j0 * TP
            lo = c0 if ns == 1 else 0
            ci, Tci = self.psum()
            later = kts[idx:]
            for li, (kt2, j02, dj2) in enumerate(later):
                kcol2, nk2 = self.kt_info(kt2)
                lo2 = (j02 * TP) if ns == 1 else 0
                lhs = C["tri"][:nk2, :nk] if li == 0 else C["ones"][:nk2, :nk]
                self.PE(lambda e: e.matmul(ci[:nk, lo2:NC], lhs, SP[:nk2, kt2, lo2:NC], start=(li == 0), stop=(li == len(later) - 1)),
                        [self.TSP[kt2], self.TC], [Tci], sig=(li == len(later) - 1))
            pt, Tpt = self.ring(self.PT)
            self.ACT(lambda e: e.activation(out=pt[:nk, lo:NC], in_=ci[:nk, lo:NC], func=AF.Exp, scale=-1.0), [Tci], [Tpt])
            self.DVE(lambda e: e.tensor_tensor(out=pt[:nk, lo:NC], in0=pt[:nk, lo:NC], in1=ES[:nk, kt, lo:NC], op=ALU.mult),
                     [Tpt, self.TSP[kt]], [Tpt])
            for i, h in enumerate(heads):
                for j in range(j0, nsub):
                    a0 = (i * nsub + j) * 64
                    self.PE(lambda e: e.matmul(acc[:TP, a0:a0 + 64], pt[:nk, i * W + j * TP:i * W + (j + 1) * TP],
                                               self.V[:nk, kt, h, 0:64], start=False, stop=True, skip_group_check=True), [Tpt, self.TV[kt]], [Tacc],
                            sig=(i == ns - 1 and j == nsub - 1))
        self.evac_copy(dest, accv, [Tacc], Tdest)

    def dsa_select(self, qslot, wslot, TP, nkv, prompt, mbuf):
        IB, JK = self.IB, self.JK
        MB, TMB = mbuf
        qs = slice(qslot * TP, (qslot + 1) * TP)
        nsel = self.nsel
        for b0 in range(0, nkv, 512):
            nb = min(512, nkv - b0)
            for h in range(8):
                pi, Tpi = self.psum()
                kt_lo, kt_hi = b0 // 128, (b0 + nb + 127) // 128
                self.PE(lambda e: e.matmul(pi[:TP, 0:nb], self.QT[0:64, 4 + h, qs], self.KT[0:64, 1, b0:b0 + nb], start=True, stop=True),
                        [self.TQ] + self.TK[kt_lo:kt_hi], [Tpi])
                rb, Trb = self.ring(self.RB)
                self.ACT(lambda e: e.activation(out=rb[:TP, 0:nb], in_=pi[:TP, 0:nb], func=AF.Relu, scale=self.WAB[:TP, wslot, h:h + 1]),
                         [Tpi, self.TWAB[wslot]], [Trb])
                if h == 0:
                    self.DVE(lambda e: e.tensor_scalar(out=IB[:TP, b0:b0 + nb], in0=rb[:TP, 0:nb], scalar1=self.WSG[:TP, wslot, 0:1],
                                                       scalar2=None, op0=ALU.mult), [Trb, self.TWAB[wslot]], [self.TIB])
                else:
                    self.DVE(lambda e: e.scalar_tensor_tensor(out=IB[:TP, b0:b0 + nb], in0=rb[:TP, 0:nb], scalar=self.WSG[:TP, wslot, h:h + 1],
                                                              in1=IB[:TP, b0:b0 + nb], op0=ALU.mult, op1=ALU.add),
                             [Trb, self.TWAB[wslot], self.TIB], [self.TIB])
        bs, Tbs = self.ring(self.BIS)
        NIT = 20
        need_search = (nkv > nsel)
        if need_search:
            self.DVE(lambda e: e.tensor_reduce(out=bs[:TP, 0:1], in_=IB[:TP, 0:nkv], axis=AX.X, op=ALU.max), [self.TIB], [Tbs])
            self.DVE(lambda e: e.tensor_reduce(out=bs[:TP, 1:2], in_=IB[:TP, 0:nkv], axis=AX.X, op=ALU.min), [self.TIB], [Tbs])
        if prompt:
            self.DVE(lambda e: e.memset(IB[0:64, nkv - 64:nkv], -3.0e38), [self.TIB], [self.TIB])
        if not need_search:
            self.DVE(lambda e: e.memset(bs[:TP, 4:5], -1.0e38), [], [Tbs])
            thr = bs[:TP, 4:5]
        else:
            self.DVE(lambda e: e.tensor_tensor(out=bs[:TP, 2:3], in0=bs[:TP, 0:1], in1=bs[:TP, 1:2], op=ALU.subtract), [Tbs], [Tbs])
            self.DVE(lambda e: e.tensor_scalar(out=bs[:TP, 2:3], in0=bs[:TP, 2:3], scalar1=1.0 + 2.0 ** -10, scalar2=1e-30,
                                               op0=ALU.mult, op1=ALU.add), [Tbs], [Tbs])
            steps = bs[:TP, 8:8 + NIT + 1]
            self.DVE(lambda e: e.tensor_scalar(out=steps, in0=self.C["pow2"][:TP, 0:NIT + 1], scalar1=bs[:TP, 2:3], scalar2=None,
                                               op0=ALU.mult), [Tbs, self.TC], [Tbs])
            self.DVE(lambda e: e.tensor_tensor(out=bs[:TP, 4:5], in0=bs[:TP, 1:2], in1=bs[:TP, 8:9], op=ALU.add), [Tbs], [Tbs])
            cand = bs[:TP, 4:5]
            for it in range(1, NIT + 1):
                self.DVE(lambda e: e.tensor_scalar(out=JK[:TP, 0:nkv], in0=IB[:TP, 0:nkv], scalar1=cand, scalar2=0.0,
                                                   op0=ALU.is_ge, op1=ALU.add, accum_out=bs[:TP, 5:6]), [self.TIB, Tbs], [self.TJK, Tbs])
                self.DVE(lambda e: e.tensor_scalar(out=bs[:TP, 6:7], in0=bs[:TP, 5:6], scalar1=float(nsel) - 0.5,
                                                   scalar2=bs[:TP, 8 + it - 1:8 + it], op0=ALU.is_ge, op1=ALU.mult), [Tbs], [Tbs])
                nxt = it if it < NIT else it - 1
                self.DVE(lambda e: e.scalar_tensor_tensor(out=cand, in0=cand, scalar=bs[:TP, 8 + nxt:8 + nxt + 1], in1=bs[:TP, 6:7],
                                                          op0=ALU.subtract, op1=ALU.add), [Tbs], [Tbs])
            thr = cand
        self.DVE(lambda e: e.tensor_scalar(out=MB[:TP, 0:nkv], in0=IB[:TP, 0:nkv], scalar1=thr, scalar2=1.0,
                                           op0=ALU.is_ge, op1=ALU.subtract), [self.TIB, Tbs], [TMB])

    def out_proj(self, tiles, oslots):
        TP = self.TP
        for t, os_ in zip(tiles, oslots):
            col = os_ * TP
            self.transposes_to([self.O[:TP, os_, 0:128], self.O[:TP, os_, 128:256]], self.OT[:, :, col:col + TP], TP, 128,
                               [self.TO[os_]], [self.TOT])
            for h in range(2):
                po, Tpo = self.psum()
                for kc in range(2):
                    self.PE(lambda e: e.matmul(po[:TP, :], self.OT[:, kc, col:col + TP], self.WO[:, kc, h * 512:(h + 1) * 512],
                                               start=(kc == 0), stop=(kc == 1)), [self.TOT, self.TWO], [Tpo], sig=(kc == 1))
                xs = self.X[:TP, t, h * 512:(h + 1) * 512]
                self.DVE(lambda e: e.tensor_tensor(out=xs, in0=xs, in1=po[:TP, :], op=ALU.add), [Tpo, self.TX[t]], [self.TX[t]])


def shard_inputs(cfg, inputs, n_cores):
    consts = host_consts(cfg)
    per = []
    f = lambda a: np.ascontiguousarray(np.asarray(a, dtype=np.float32))
    L = cfg.DEPTH
    shared = {
        "fnorm": f(inputs["final_norm"]),
        "w_in": f(inputs["w_in"]), "qn": f(inputs["mla_q_norm"]), "wqb": f(inputs["mla_w_qb"]),
        "kvn": f(inputs["mla_kv_norm"]), "wkvb": f(inputs["mla_w_kvb"]),
        "lam": f(np.asarray(inputs["diff_lambda"]).reshape(L, 128)), "dn": f(inputs["diff_norm"]),
        "wo": f(inputs["w_out"]),
        "wg1": f(inputs["w_ff1_gate"]), "wu1": f(inputs["w_ff1_up"]), "wd1": f(inputs["w_ff1_down"]),
        "wg2": f(inputs["w_ff2_gate"]), "wu2": f(inputs["w_ff2_up"]), "wd2": f(inputs["w_ff2_down"]),
    }
    for nm, src in (("n1", "norm_ff1"), ("nm", "norm_mix"), ("n2", "norm_ff2")):
        shared[nm] = f(np.asarray(inputs[src]).reshape(L, 8, 128).transpose(0, 2, 1))
    for nm, a in consts.items():
        shared["k_" + nm] = f(a)
    for c in range(n_cores):
        d = dict(shared)
        ps = slice(c * cfg.NPS, (c + 1) * cfg.NPS)
        ss = slice(c * cfg.NSS, (c + 1) * cfg.NSS)
        d["xp"] = f(inputs["x_prompt"][ps])
        d["xs"] = f(inputs["x_sample"][ss])
        d["c_mla"] = f(inputs["cache_mla"][:, ss])
        d["c_dsa"] = f(inputs["cache_dsa"][:, ss])
        d["c_sb"] = f(np.asarray(inputs["cache_sb"])[:, ss].reshape(L, cfg.NSS, cfg.PAST, 512))
        d["c_diff"] = f(np.asarray(inputs["cache_diff"])[:, ss].reshape(L, cfg.NSS, cfg.PAST, 512))
        per.append(d)
    return per


def gather_outputs(cfg, results, n_cores):
    L = cfg.DEPTH
    cat = lambda name, ax: np.concatenate([np.asarray(r[name]) for r in results], axis=ax)
    yp = cat("yp", 0)
    ys = cat("ys", 0)
    outs = [yp, ys]
    B = cfg.NPS * n_cores
    Bs = cfg.NSS * n_cores
    outs.append(cat("mla_p", 1))
    outs.append(cat("dsa_p", 1))
    outs.append(cat("sb_p", 1).reshape(L, B, cfg.S, 4, 128))
    outs.append(cat("diff_p", 1).reshape(L, B, cfg.S, 4, 128))
    outs.append(cat("mla_s", 1))
    outs.append(cat("dsa_s", 1))
    outs.append(cat("sb_s", 1).reshape(L, Bs, cfg.TS, 4, 128))
    outs.append(cat("diff_s", 1).reshape(L, Bs, cfg.TS, 4, 128))
    return tuple(np.ascontiguousarray(o.astype(np.float32, copy=False)) for o in outs)


def run(cfg, inputs, n_cores=8):
    b = Builder(cfg)
    nc = b.build()
    in_maps = shard_inputs(cfg, inputs, n_cores)
    res = run_bass_kernel_spmd(nc, in_maps, core_ids=list(range(n_cores)))
    return gather_outputs(cfg, res.results, n_cores)


def kernel(**inputs):
    cfg = Cfg()
    return run(cfg, inputs, 8)
```

```python
import math
from contextlib import ExitStack
import numpy as np
import concourse.bass as bass
import concourse.mybir as mybir
from concourse.bass_utils import run_bass_kernel_spmd

F32 = mybir.dt.float32
BF16 = mybir.dt.bfloat16
AF = mybir.ActivationFunctionType
ALU = mybir.AluOpType
AX = mybir.AxisListType

D = 1024
DFF = 2816
HD = 64
CHUNK = 64
EPS = 1e-6
THETA = 10000.0
IN_COLS = 2920
NEGB = 30000.0
C_CQ, C_CKV, C_KR = 0, 256, 384
C_QB, C_KB, C_VB, C_QIB, C_KIB, C_WIB = 416, 672, 736, 800, 1312, 1376
C_QC, C_KC, C_VC = 1384, 1640, 1896
C_QD, C_KD, C_VD = 2152, 2408, 2664


class T:
    __slots__ = ("w", "r", "excl")

    def __init__(self, excl=False):
        self.w = None
        self.r = {}
        self.excl = excl


class Ring(list):
    pass


class Eng:
    def __init__(self, name, be, sem, is_pe=False):
        self.name = name
        self.be = be
        self.sem = sem
        self.cnt = 0
        self.clock = {}
        self.is_pe = is_pe


class K:
    def __init__(self, nc, stack, n_dma_sems=48):
        self.nc = nc
        self.es = {}
        for name, be in (("pe", nc.tensor), ("act", nc.scalar), ("dve", nc.vector),
                         ("pool", nc.gpsimd), ("sp", nc.sync)):
            sem = stack.enter_context(nc.semaphore("s_" + name))
            self.es[name] = Eng(name, be, sem, is_pe=(name == "pe"))
        self.pe, self.act, self.dve, self.pool, self.sp = (
            self.es[n] for n in ("pe", "act", "dve", "pool", "sp"))
        self.dsems = []
        for i in range(n_dma_sems):
            sem = stack.enter_context(nc.semaphore("s_dma%d" % i))
            self.dsems.append([("dma", i), sem, 0])
        self.dnext = 0
        self.dnext2 = 0
        self.semof = {e.name: e.sem for e in self.es.values()}
        for d in self.dsems:
            self.semof[d[0]] = d[1]
        self.nins = 0
        self.nwaits = 0

    def _need(self, eng, deps):
        for key, c in deps.items():
            if key == eng.name and eng.is_pe:
                continue
            if c > eng.clock.get(key, 0):
                if key in self.es:
                    assert c <= self.es[key].cnt, "dependency on an unsignalled instruction"
                eng.be.wait_ge(self.semof[key], c)
                eng.clock[key] = c
                self.nwaits += 1

    @staticmethod
    def _collect(reads, writes):
        deps = {}
        for t in reads:
            if t.w is not None:
                kk, c = t.w
                if c > deps.get(kk, 0):
                    deps[kk] = c
        for t in writes:
            if t.w is not None:
                kk, c = t.w
                if c > deps.get(kk, 0):
                    deps[kk] = c
            for kk, c in t.r.items():
                if c > deps.get(kk, 0):
                    deps[kk] = c
        return deps

    def op(self, eng, fn, reads=(), writes=(), sig=True):
        ex = [t for t in reads if t.excl]
        if ex:
            reads = [t for t in reads if not t.excl]
            writes = list(writes) + ex
        deps = self._collect(reads, writes)
        self._need(eng, deps)
        ins = fn(eng.be)
        if sig:
            eng.cnt += 1
            ins.then_inc(eng.sem, 1)
            c = eng.cnt
        else:
            c = eng.cnt + 1
        self.nins += 1
        me = (eng.name, c)
        for t in writes:
            t.w = me
            t.r = {}
        for t in reads:
            if t.r.get(eng.name, 0) < c:
                t.r[eng.name] = c
        return ins

    def dma(self, eng, out, in_, reads=(), writes=()):
        half = len(self.dsems) // 2
        if eng.name == "pool":
            d = self.dsems[self.dnext]
            self.dnext = (self.dnext + 1) % half
        else:
            d = self.dsems[half + self.dnext2]
            self.dnext2 = (self.dnext2 + 1) % (len(self.dsems) - half)
        deps = self._collect(reads, writes)
        if d[2] > 0:
            deps[d[0]] = max(deps.get(d[0], 0), d[2])
        self._need(eng, deps)
        ins = eng.be.dma_start(out=out, in_=in_)
        d[2] += 16
        ins.then_inc(d[1], 16)
        self.nins += 1
        me = (d[0], d[2])
        for t in writes:
            t.w = me
            t.r = {}
        for t in reads:
            t.r[d[0]] = d[2]
        return ins

    def barrier(self):
        allv = {e.name: e.cnt for e in self.es.values() if e.cnt > 0}
        for d in self.dsems:
            if d[2] > 0:
                allv[d[0]] = d[2]
        for e in self.es.values():
            for key, c in allv.items():
                if key == e.name:
                    continue
                if c > e.clock.get(key, 0):
                    e.be.wait_ge(self.semof[key], c)
                    e.clock[key] = c
                    self.nwaits += 1

    def finish(self):
        self.barrier()


class Cfg:
    def __init__(self, NPS=4, S=2048, NSS=4, TS=64, PAST=4096, DEPTH=2, mixers=("mla", "dsa", "sb", "diff"),
                 ffn=True, run_sample=True):
        self.NPS, self.S, self.NSS, self.TS, self.PAST, self.DEPTH = NPS, S, NSS, TS, PAST, DEPTH
        self.mixers = mixers
        self.ffn = ffn
        self.run_sample = run_sample


def rope_tables(pos, d):
    half = d // 2
    freqs = (np.float32(THETA) ** (-np.arange(half, dtype=np.float32) * np.float32(2.0) / np.float32(d))).astype(np.float32)
    ang = pos.astype(np.float32)[:, None] * freqs[None, :]
    return np.cos(ang).astype(np.float32), np.sin(ang).astype(np.float32)


def host_consts(cfg):
    c = {}
    c["ident"] = np.eye(128, dtype=np.float32)
    c["bigi"] = np.tile(np.eye(128, dtype=np.float32) * NEGB, (1, 4))
    q = np.arange(128)[:, None]
    kk = np.arange(128)[None, :]
    c["m_chunk"] = np.where((kk // CHUNK) <= (q // CHUNK), 0.0, -1.0).astype(np.float32)
    c["m_strict"] = np.where(kk < q, 0.0, -1.0).astype(np.float32)
    c["tri"] = (q >= kk).astype(np.float32)
    c["ones"] = np.ones((128, 128), np.float32)
    def tabs(pos):
        c64, s64 = rope_tables(pos, 64)
        c32, s32 = rope_tables(pos, 32)
        return np.concatenate([c64, c64, -s64, s64, c32, c32, -s32, s32], axis=1).astype(np.float32)
    S = cfg.S
    tab = tabs(np.arange(S))
    c["rope_p"] = np.ascontiguousarray(tab.reshape(S // 128, 128, 192).transpose(1, 0, 2))
    tab2 = np.zeros((128, 1, 192), np.float32)
    tab2[:cfg.TS, 0] = tabs(cfg.PAST + np.arange(cfg.TS))
    c["rope_s"] = tab2
    c["zero"] = np.zeros((1, 512), np.float32)
    c["pow2"] = np.tile((0.5 ** np.arange(1, 33, dtype=np.float32))[None, :], (128, 1)).astype(np.float32)
    return c


class Builder:
    def __init__(self, cfg):
        self.cfg = cfg
        self.nc = bass.Bass("TRN2", target_bir_lowering=False)
        self.din = {}
        self.dout = {}

    def declare(self):
        cfg, nc = self.cfg, self.nc
        L = cfg.DEPTH

        def I(name, shape):
            self.din[name] = nc.dram_tensor(name, list(shape), F32, kind="ExternalInput").ap()

        def O(name, shape):
            self.dout[name] = nc.dram_tensor(name, list(shape), F32, kind="ExternalOutput").ap()

        I("xp", (cfg.NPS, cfg.S, D))
        I("xs", (cfg.NSS, cfg.TS, D))
        I("c_mla", (L, cfg.NSS, cfg.PAST, 160))
        I("c_dsa", (L, cfg.NSS, cfg.PAST, 192))
        I("c_sb", (L, cfg.NSS, cfg.PAST, 512))
        I("c_diff", (L, cfg.NSS, cfg.PAST, 512))
        for nm in ("n1", "nm", "n2"):
            I(nm, (L, 128, 8))
        I("fnorm", (D,))
        for nm in ("wg1", "wu1", "wg2", "wu2"):
            I(nm, (L, D, DFF))
        for nm in ("wd1", "wd2"):
            I(nm, (L, DFF, D))
        I("w_in", (L, D, IN_COLS))
        I("qn", (L, 256))
        I("wqb", (L, 256, 384))
        I("kvn", (L, 128))
        I("wkvb", (L, 128, 512))
        I("lam", (L, 128))
        I("dn", (L, 64))
        I("wo", (L, D, D))
        for nm, a in host_consts(cfg).items():
            I("k_" + nm, a.shape)
        O("yp", (cfg.NPS, cfg.S, D))
        O("ys", (cfg.NSS, cfg.TS, D))
        O("mla_p", (L, cfg.NPS, cfg.S, 160))
        O("dsa_p", (L, cfg.NPS, cfg.S, 192))
        O("sb_p", (L, cfg.NPS, cfg.S, 512))
        O("diff_p", (L, cfg.NPS, cfg.S, 512))
        O("mla_s", (L, cfg.NSS, cfg.TS, 160))
        O("dsa_s", (L, cfg.NSS, cfg.TS, 192))
        O("sb_s", (L, cfg.NSS, cfg.TS, 512))
        O("diff_s", (L, cfg.NSS, cfg.TS, 512))

    def sb(self, stack, name, shape, dtype):
        self.uid = getattr(self, "uid", 0) + 1
        return stack.enter_context(self.nc.sbuf_tensor("%s_%d" % (name, self.uid), list(shape), dtype))

    def PE(self, fn, r=(), w=(), sig=True):
        return self.k.op(self.k.pe, fn, r, w, sig)

    def ACT(self, fn, r=(), w=()):
        return self.k.op(self.k.act, fn, r, w)

    def DVE(self, fn, r=(), w=()):
        return self.k.op(self.k.dve, fn, r, w)

    def POOL(self, fn, r=(), w=()):
        return self.k.op(self.k.pool, fn, r, w)

    def psum(self):
        i = self.ps_next
        self.ps_next = (self.ps_next + 1) % len(self.ps_rot)
        return self.ps_rot[i]

    def ring(self, lst):
        idx = lst.idx
        lst.idx = (idx + 1) % len(lst)
        return lst[idx]

    def mkring(self, stack, name, shape, dtype, n):
        r = Ring((self.sb(stack, "%s%d" % (name, i), shape, dtype), T()) for i in range(n))
        r.idx = 0
        return r

    def build(self):
        cfg, nc = self.cfg, self.nc
        self.declare()
        with ExitStack() as gs:
            self.k = K(nc, gs)
            k = self.k
            self.rings = {}
            self.banks = []
            for i in range(8):
                pt = gs.enter_context(nc.psum_tensor("psb%d" % i, [128, 512], F32))
                self.banks.append((pt, T(excl=True)))
            self.ps_rot = self.banks[0:5]
            self.ps_next = 0
            self.ps_acc = self.banks[5:8]
            self.load_consts(gs)
            for n in range(cfg.NPS):
                self.run_job("p", n)
                k.barrier()
            if cfg.NSS > 0 and cfg.run_sample:
                self.run_job("s", 0)
                k.barrier()
            k.finish()
            self.stats = (k.nins, k.nwaits)
        return nc

    def load_consts(self, gs):
        cfg, k = self.cfg, self.k
        din = self.din
        C = {}
        TC = T()
        self.TC = TC

        def cst(name, shape, dtype, src):
            t = self.sb(gs, "c_" + name, shape, dtype)
            k.dma(k.pool, t[:], src, writes=[TC])
            C[name] = t
            return t

        cst("ident", (128, 128), BF16, din["k_ident"][:, :])
        cst("bigi", (128, 512), BF16, din["k_bigi"][:, :])
        cst("m_chunk", (128, 128), BF16, din["k_m_chunk"][:, :])
        cst("m_strict", (128, 128), BF16, din["k_m_strict"][:, :])
        cst("tri", (128, 128), BF16, din["k_tri"][:, :])
        cst("ones", (128, 128), BF16, din["k_ones"][:, :])
        cst("pow2", (128, 32), F32, din["k_pow2"][:, :])
        cst("zero", (1, 512), BF16, din["k_zero"][:, :])
        L = cfg.DEPTH
        for nm in ("n1", "nm", "n2"):
            cst(nm, (128, L, 8), F32, din[nm].rearrange("l p c -> p l c"))
        cst("qn", (128, L, 256), F32, din["qn"].partition_broadcast(128))
        cst("kvn", (128, L, 128), F32, din["kvn"].partition_broadcast(128))
        cst("lam", (128, L, 128), F32, din["lam"].partition_broadcast(128))
        cst("dn", (128, L, 64), F32, din["dn"].partition_broadcast(128))
        self.C = C

    def run_job(self, kind, n):
        cfg, k, nc = self.cfg, self.k, self.nc
        if kind == "p":
            NT, TP = cfg.S // 128, 128
        else:
            NT, TP = cfg.NSS, cfg.TS
        self.kind, self.NT, self.TP, self.jn = kind, NT, TP, n
        ntok = NT * TP
        self.ntok = ntok
        tpg = 512 // TP
        self.groups = [(g0, min(tpg, NT - g0)) for g0 in range(0, NT, tpg)]
        with ExitStack() as js:
            self.X = self.sb(js, "X", (128, NT, D), F32)
            self.TX = [T() for _ in range(NT)]
            self.XNT = self.sb(js, "XNT", (128, 8, ntok), BF16)
            self.TXNT = [T() for _ in self.groups]
            self.small = self.mkring(js, "small", (128, 8), F32, 6)
            for t in range(NT):
                if kind == "p":
                    src = self.din["xp"][n, t * 128:(t + 1) * 128, :]
                else:
                    src = self.din["xs"][t, :, :]
                k.dma(k.sp, self.X[:TP, t, :], src, writes=[self.TX[t]])
            for l in range(cfg.DEPTH):
                if cfg.ffn:
                    self.norm_transpose(self.C["n1"][:, l, :])
                    self.ffn(l, "1")
                    k.barrier()
                if cfg.mixers:
                    self.norm_transpose(self.C["nm"][:, l, :])
                    self.mixer(l)
                    k.barrier()
                if cfg.ffn:
                    self.norm_transpose(self.C["n2"][:, l, :])
                    self.ffn(l, "2")
                    k.barrier()
            self.final_norm()

    def rstd_of(self, ss_ap, TP, Tss, n_feat):
        sm, Tsm = self.ring(self.small)
        self.DVE(lambda e: e.tensor_scalar(out=sm[:TP, 0:1], in0=ss_ap, scalar1=1.0 / n_feat, scalar2=EPS,
                                           op0=ALU.mult, op1=ALU.add), [Tss], [Tsm])
        self.ACT(lambda e: e.activation(out=sm[:TP, 1:2], in_=sm[:TP, 0:1], func=AF.Ln), [Tsm], [Tsm])
        self.ACT(lambda e: e.activation(out=sm[:TP, 2:3], in_=sm[:TP, 1:2], func=AF.Exp, scale=-0.5), [Tsm], [Tsm])
        return sm[:TP, 2:3], Tsm

    def norm_transpose(self, G):
        TP, NT = self.TP, self.NT
        C = self.C
        tpg = 512 // TP
        ns_ = ExitStack()
        xsr = self.mkring(ns_, "xs", (128, D), BF16, 2)
        junk = self.mkring(ns_, "junk", (128, D), BF16, 2)
        for t in range(NT):
            g = t // tpg
            sm, Tsm = self.ring(self.small)
            jk, Tjk = self.ring(junk)
            self.ACT(lambda e: e.activation(out=jk[:TP, :], in_=self.X[:TP, t, :], func=AF.Square,
                                            accum_out=sm[:TP, 3:4]), [self.TX[t]], [Tjk, Tsm])
            rs, Trs = self.rstd_of(sm[:TP, 3:4], TP, Tsm, D)
            xs, Txs = self.ring(xsr)
            self.DVE(lambda e: e.tensor_scalar(out=xs[:TP, :], in0=self.X[:TP, t, :], scalar1=rs, scalar2=None,
                                               op0=ALU.mult), [self.TX[t], Trs], [Txs])
            pb, Tpb = self.psum()
            pv = pb[:].bitcast(BF16).rearrange("p (c t) -> p c t", c=8)
            for c in range(8):
                self.PE(lambda e: e.transpose(pv[:, c, :TP], xs[:TP, c * 128:(c + 1) * 128], C["ident"][:TP, :TP]),
                        [Txs, self.TC], [Tpb], sig=(c == 7))
            self.DVE(lambda e: e.tensor_tensor(out=self.XNT[:, :, t * TP:(t + 1) * TP], in0=pv[:, :, :TP],
                                               in1=G.unsqueeze(2).to_broadcast([128, 8, TP]), op=ALU.mult),
                     [Tpb, self.TC], [self.TXNT[g]])
        self.k.barrier()
        ns_.close()

    def ffn(self, l, which):
        cfg, k = self.cfg, self.k
        TP, NT = self.TP, self.NT
        wg = self.din["wg" + which][l].rearrange("(k p) f -> p k f", p=128)
        wu = self.din["wu" + which][l].rearrange("(k p) f -> p k f", p=128)
        wd = self.din["wd" + which][l].rearrange("(c p) d -> p c d", p=128)
        NCH = DFF // 128
        CG = 3
        cgs = [(c0, min(CG, NCH - c0)) for c0 in range(0, NCH, CG)]
        with ExitStack() as fs:
            WG = self.mkring(fs, "WG", (128, 8, CG * 128), BF16, 3)
            WU = self.mkring(fs, "WU", (128, 8, CG * 128), BF16, 3)
            WD = self.mkring(fs, "WD", (128, CG, D), BF16, 2)
            HT = self.mkring(fs, "HT", (128, CG, self.ntok), BF16, 2)
            SG = self.mkring(fs, "SG", (128, 512), BF16, 3)

            def gate_up(gi):
                c0, nch = cgs[gi]
                wgt, Twg = WG[gi % 3]
                wut, Twu = WU[gi % 3]
                wdt, Twd = WD[gi % 2]
                ht, Tht = HT[gi % 2]
                k.dma(k.pool, wgt[:, :, :nch * 128], wg[:, :, c0 * 128:(c0 + nch) * 128], writes=[Twg])
                k.dma(k.pool, wut[:, :, :nch * 128], wu[:, :, c0 * 128:(c0 + nch) * 128], writes=[Twu])
                k.dma(k.pool, wdt[:, :nch, :], wd[:, c0:c0 + nch, :], writes=[Twd])
                for g, (t0, nt) in enumerate(self.groups):
                    col0, ncol = t0 * TP, nt * TP
                    for ci in range(nch):
                        pg, Tpg = self.psum()
                        for kk in range(8):
                            self.PE(lambda e: e.matmul(pg[:, :ncol], wgt[:, kk, ci * 128:(ci + 1) * 128],
                                                       self.XNT[:, kk, col0:col0 + ncol], start=(kk == 0), stop=(kk == 7)),
                                    [Twg, self.TXNT[g]], [Tpg], sig=(kk == 7))
                        pu, Tpu = self.psum()
                        for kk in range(8):
                            self.PE(lambda e: e.matmul(pu[:, :ncol], wut[:, kk, ci * 128:(ci + 1) * 128],
                                                       self.XNT[:, kk, col0:col0 + ncol], start=(kk == 0), stop=(kk == 7)),
                                    [Twu, self.TXNT[g]], [Tpu], sig=(kk == 7))
                        sg, Tsg = self.ring(SG)
                        self.ACT(lambda e: e.activation(out=sg[:, :ncol], in_=pg[:, :ncol], func=AF.Silu), [Tpg], [Tsg])
                        self.DVE(lambda e: e.tensor_tensor(out=ht[:, ci, col0:col0 + ncol], in0=sg[:, :ncol],
                                                           in1=pu[:, :ncol], op=ALU.mult), [Tsg, Tpu], [Tht])

            def down(gi):
                c0, nch = cgs[gi]
                wdt, Twd = WD[gi % 2]
                ht, Tht = HT[gi % 2]
                for t in range(NT):
                    for h in range(2):
                        pd, Tpd = self.psum()
                        for ci in range(nch):
                            self.PE(lambda e: e.matmul(pd[:TP, :], ht[:, ci, t * TP:(t + 1) * TP],
                                                       wdt[:, ci, h * 512:(h + 1) * 512], start=(ci == 0), stop=(ci == nch - 1)),
                                    [Tht, Twd], [Tpd], sig=(ci == nch - 1))
                        xs = self.X[:TP, t, h * 512:(h + 1) * 512]
                        self.DVE(lambda e: e.scalar_tensor_tensor(out=xs, in0=pd[:TP, :], scalar=0.5, in1=xs,
                                                                  op0=ALU.mult, op1=ALU.add), [Tpd, self.TX[t]], [self.TX[t]])

            ng = len(cgs)
            gate_up(0)
            for gi in range(ng):
                if gi + 1 < ng:
                    gate_up(gi + 1)
                down(gi)

    def final_norm(self):
        k = self.k
        TP, NT = self.TP, self.NT
        with ExitStack() as fs:
            YT = self.mkring(fs, "YT", (128, D), F32, 2)
            junk = self.mkring(fs, "junk", (128, D), BF16, 2)
            GF = self.sb(fs, "gf", (128, D), F32)
            TGF = T()
            k.dma(k.pool, GF[:], self.din["fnorm"].partition_broadcast(128), writes=[TGF])
            for t in range(NT):
                sm, Tsm = self.ring(self.small)
                jk, Tjk = self.ring(junk)
                self.ACT(lambda e: e.activation(out=jk[:TP, :], in_=self.X[:TP, t, :], func=AF.Square,
                                                accum_out=sm[:TP, 3:4]), [self.TX[t]], [Tjk, Tsm])
                rs, Trs = self.rstd_of(sm[:TP, 3:4], TP, Tsm, D)
                yt, Tyt = self.ring(YT)
                self.DVE(lambda e: e.scalar_tensor_tensor(out=yt[:TP, :], in0=self.X[:TP, t, :], scalar=rs,
                                                          in1=GF[:TP, :], op0=ALU.mult, op1=ALU.mult),
                         [self.TX[t], Trs, TGF], [Tyt])
                if self.kind == "p":
                    dst = self.dout["yp"][self.jn, t * 128:(t + 1) * 128, :]
                else:
                    dst = self.dout["ys"][t, :, :]
                k.dma(k.sp, dst, yt[:TP, :], reads=[Tyt])
            k.barrier()

    MIX = {
        "mla": (0, 416, [(0, 416)], 0, 4, 4),
        "dsa": (416, 968, [(0, 384), (384, 512), (896, 72)], 256, 2, 12),
        "sb": (1384, 768, [(0, 512), (512, 256)], 512, 4, 4),
        "diff": (2152, 768, [(0, 512), (512, 256)], 768, 4, 4),
    }

    def mixer(self, l):
        for m in self.cfg.mixers:
            with ExitStack() as ms:
                self.mixer_pass(l, m, ms)
            self.k.barrier()

    def acc_bank(self):
        i = getattr(self, "acc_next", 0)
        self.acc_next = (i + 1) % len(self.ps_acc)
        return self.ps_acc[i]

    def mixer_pass(self, l, m, ms):
        cfg, k = self.cfg, self.k
        TP, NT, kind = self.TP, self.NT, self.kind
        c0, ncols, banks, orow0, nks, nqs = self.MIX[m]
        self.m, self.l = m, l
        if kind == "p":
            NK = cfg.S
        else:
            NK = cfg.PAST + cfg.TS
        self.NK = NK
        NKT = (NK + 127) // 128
        self.NKT = NKT
        self.nsel = min(256, NK // 4)
        rsrc = self.din["k_rope_p"] if kind == "p" else self.din["k_rope_s"]
        ntab = (cfg.S // 128) if kind == "p" else 1
        self.tab = None
        if m == "dsa":
            self.tab = self.sb(ms, "tab", (128, ntab, 128), F32)
            k.dma(k.pool, self.tab[:], rsrc[:, :, 0:128], writes=[self.TC])
        elif m in ("mla", "diff"):
            self.tab = self.sb(ms, "tab", (128, ntab, 64), F32)
            k.dma(k.pool, self.tab[:], rsrc[:, :, 128:192], writes=[self.TC])
        self.WIN = self.sb(ms, "WIN", (128, 8, ncols), BF16)
        self.TWIN = T()
        k.dma(k.pool, self.WIN[:], self.din["w_in"][l].rearrange("(k p) c -> p k c", p=128)[:, :, c0:c0 + ncols],
              writes=[self.TWIN])
        self.WO = self.sb(ms, "WO", (128, 2, D), BF16)
        self.TWO = T()
        k.dma(k.pool, self.WO[:], self.din["wo"][l, orow0:orow0 + 256, :].rearrange("(k p) d -> p k d", p=128),
              writes=[self.TWO])
        self.KT = self.sb(ms, "KT", (128, nks, NK), BF16)
        self.TK = [T() for _ in range(NKT)]
        nv = 1 if m == "dsa" else 4
        self.V = self.sb(ms, "V", (128, NKT, nv, 65), BF16)
        self.TV = [T() for _ in range(NKT)]
        self.QT = self.sb(ms, "QT", (128, nqs, 512), BF16)
        self.TQ = T()
        ntl = NT if kind == "s" else 4
        self.O = self.sb(ms, "O", (128, ntl, 256), BF16)
        self.TO = [T() for _ in range(ntl)]
        self.OT = self.sb(ms, "OT", (128, 2, 512), BF16)
        self.TOT = T()
        self.ROWS = self.mkring(ms, "ROWS", (128, 512), F32, 2)
        self.STG = self.mkring(ms, "STG", (128, 1024), BF16, 2)
        self.RT = self.mkring(ms, "RT", (128, 512), F32, 4)
        self.PT = self.mkring(ms, "PT", (128, 512), BF16, 4)
        self.NRM = self.mkring(ms, "NRM", (128, 8), F32, 4)
        TVall = self.TV
        self.DVE(lambda e: e.memset(self.V[:, :, :, 64:65], 1.0), [], TVall)
        if m == "mla":
            self.WQB = self.sb(ms, "WQB", (128, 2, 384), BF16)
            self.WKVB = self.sb(ms, "WKVB", (128, 512), BF16)
            self.TWM = T()
            k.dma(k.pool, self.WQB[:], self.din["wqb"][l].rearrange("(k p) c -> p k c", p=128), writes=[self.TWM])
            k.dma(k.pool, self.WKVB[:], self.din["wkvb"][l], writes=[self.TWM])
            self.CQT = self.mkring(ms, "CQT", (128, 2, 128), BF16, 2)
            self.LT = self.mkring(ms, "LT", (128, 128), BF16, 2)
        if m == "dsa":
            self.IB = self.sb(ms, "IB", (128, NK), F32)
            self.TIB = T()
            self.MBS = [(self.sb(ms, "MB%d" % i, (128, NK), BF16), T()) for i in range(2 if kind == "p" else 1)]
            self.JK = self.sb(ms, "JK", (128, NK), BF16)
            self.TJK = T()
            self.WAB = self.sb(ms, "WAB", (128, ntl, 8), F32)
            self.WSG = self.sb(ms, "WSG", (128, ntl, 8), F32)
            self.TWAB = [T() for _ in range(ntl)]
            self.RB = self.mkring(ms, "RB", (128, 512), F32, 2)
            self.BIS = self.mkring(ms, "BIS", (128, 48), F32, 2)
        if m == "sb":
            W = 256 if kind == "p" else 4 * TP
            self.SBW = W
            self.SP = self.sb(ms, "SP", (128, NKT, W), BF16)
            self.ES = self.sb(ms, "ES", (128, NKT, W), BF16)
            self.TSP = [T() for _ in range(NKT)]
        if m == "diff":
            self.OD = self.mkring(ms, "OD", (128, 8, 64), F32, 2)
            self.LAMF = self.sb(ms, "LAMF", (128, 8), F32)
            self.TLAM = T()
            self.diff_lambda(l)
        if kind == "s":
            if m == "mla":
                self.CST = self.mkring(ms, "CST", (128, 4, 160), BF16, 3)
            elif m == "dsa":
                self.CST = self.mkring(ms, "CST", (128, 4, 2, 64), BF16, 3)
            else:
                self.CST = self.mkring(ms, "CST", (128, 4, 4, 64), BF16, 3)
        import os
        DBG = int(os.environ.get("MIXSTOP", "9"))
        if DBG <= 1:
            return
        if kind == "p":
            for g, (t0, nt) in enumerate(self.groups):
                for t in range(t0, t0 + nt):
                    self.proj_tile(t, qslot=t - t0, kt=t, kcol=t * 128, oslot=t - t0)
                if DBG <= 2:
                    continue
                self.attend_group(t0, nt)
                if DBG <= 3:
                    continue
                self.out_proj(list(range(t0, t0 + nt)), list(range(nt)))
        else:
            for s in range(NT):
                self.load_past(s)
                self.proj_tile(s, qslot=0, kt=NKT - 1, kcol=cfg.PAST, oslot=s)
                self.attend_sample(s)
            self.out_proj(list(range(NT)), list(range(NT)))

    def transposes_to(self, srcs, dst_ap, rows, nparts, reads, writes, dt=BF16):
        C = self.C
        pb, Tpb = self.psum()
        n = len(srcs)
        pv = pb[:].bitcast(BF16)[:, 0:n * 128].rearrange("p (c t) -> p c t", c=n)
        for i, sap in enumerate(srcs):
            self.PE(lambda e: e.transpose(pv[:nparts, i, :rows], sap, C["ident"][:rows, :rows]), list(reads) + [self.TC], [Tpb],
                    sig=(i == n - 1))
        self.evac_copy(dst_ap, pv[:nparts, :, :rows], [Tpb], writes)

    def evac_copy(self, dst, src, reads, writes):
        i = getattr(self, "_ec", 0)
        self._ec = i + 1
        import os
        if i % 2 == 0 or os.environ.get("ALLDVE") == "1":
            self.DVE(lambda e: e.tensor_copy(out=dst, in_=src), reads, writes)
        else:
            self.ACT(lambda e: e.activation(out=dst, in_=src, func=AF.Copy), reads, writes)

    def rope(self, src, dst, H, d, tslot, TP, reads, writes):
        half = d // 2
        toff = 0
        tab = self.tab
        CC = tab[:TP, tslot, toff:toff + d]
        SS = tab[:TP, tslot, toff + d:toff + 2 * d]
        a, Ta = self.ring(self.RT)
        b, Tb = self.ring(self.RT)
        av = a[:TP, 0:H * d].rearrange("p (h d) -> p h d", h=H)
        bv = b[:TP, 0:H * d].rearrange("p (h d) -> p h d", h=H)
        self.DVE(lambda e: e.tensor_tensor(out=av, in0=src, in1=CC.unsqueeze(1).to_broadcast([TP, H, d]), op=ALU.mult),
                 list(reads) + [self.TC], [Ta])
        self.DVE(lambda e: e.tensor_tensor(out=bv[:, :, 0:half], in0=src[:, :, half:d],
                                           in1=SS[:, 0:half].unsqueeze(1).to_broadcast([TP, H, half]), op=ALU.mult),
                 list(reads) + [self.TC], [Tb])
        self.DVE(lambda e: e.tensor_tensor(out=bv[:, :, half:d], in0=src[:, :, 0:half],
                                           in1=SS[:, half:d].unsqueeze(1).to_broadcast([TP, H, half]), op=ALU.mult),
                 list(reads) + [self.TC], [Tb])
        self.DVE(lambda e: e.tensor_tensor(out=dst, in0=av, in1=bv, op=ALU.add), [Ta, Tb], writes)

    def project(self, t):
        TP = self.TP
        g = t // (512 // TP)
        outs = []
        for (b0, bn) in self.MIX[self.m][2]:
            pb, Tpb = self.psum()
            for kk in range(8):
                self.PE(lambda e: e.matmul(pb[:TP, :bn], self.XNT[:, kk, t * TP:(t + 1) * TP], self.WIN[:, kk, b0:b0 + bn],
                                           start=(kk == 0), stop=(kk == 7)), [self.TXNT[g], self.TWIN], [Tpb], sig=(kk == 7))
            outs.append((pb, Tpb))
        return outs

    def rows_out(self, rows, Trows, t, width):
        k = self.k
        name = {"mla": "mla", "dsa": "dsa", "sb": "sb", "diff": "diff"}[self.m]
        if self.kind == "p":
            dst = self.dout[name + "_p"][self.l, self.jn, t * 128:(t + 1) * 128, :]
        else:
            dst = self.dout[name + "_s"][self.l, t, :, :]
        k.dma(k.sp, dst, rows[:self.TP, 0:width], reads=[Trows])

    def proj_tile(self, t, qslot, kt, kcol, oslot):
        getattr(self, "proj_" + self.m)(t, qslot, kt, kcol, oslot)

    def mla_keys(self, lat, krope, nk, kt, kcol, reads):
        lt, Tlt = self.ring(self.LT)
        self.transposes_to([lat], lt[:, 0:nk].unsqueeze(1), nk, 128, reads, [Tlt])
        pkv, Tpkv = self.psum()
        self.PE(lambda e: e.matmul(pkv[:nk, :], lt[:, :nk], self.WKVB[:, :], start=True, stop=True), [Tlt, self.TWM], [Tpkv])
        st, Tst = self.ring(self.STG)
        kst = st[:nk, 0:384].rearrange("p (h d) -> p h d", h=4)
        pv = pkv[:nk, :].rearrange("p (h d) -> p h d", h=4)
        self.evac_copy(kst[:, :, 0:64], pv[:, :, 0:64], [Tpkv], [Tst])
        self.evac_copy(self.V[:nk, kt, :, 0:64], pv[:, :, 64:128], [Tpkv], [self.TV[kt]])
        self.evac_copy(kst[:, :, 64:96], krope.unsqueeze(1).to_broadcast([nk, 4, 32]), reads, [Tst])
        self.transposes_to([kst[:, h, :] for h in range(4)], self.KT[0:96, 0:4, kcol:kcol + nk], nk, 96, [Tst], [self.TK[kt]])

    def proj_mla(self, t, qslot, kt, kcol, oslot):
        TP, l, C = self.TP, self.l, self.C
        (pj, Tpj), = self.project(t)
        rows, Trows = self.ring(self.ROWS)
        st, Tst = self.ring(self.STG)
        jk, Tjk = self.ring(self.RT)
        nr, Tnr = self.ring(self.NRM)
        self.ACT(lambda e: e.activation(out=jk[:TP, 0:256], in_=pj[:TP, 0:256], func=AF.Square, accum_out=nr[:TP, 0:1]),
                 [Tpj], [Tjk, Tnr])
        self.ACT(lambda e: e.activation(out=jk[:TP, 256:384], in_=pj[:TP, 256:384], func=AF.Square, accum_out=nr[:TP, 1:2]),
                 [Tpj], [Tjk, Tnr])
        rq, Trq = self.rstd_of(nr[:TP, 0:1], TP, Tnr, 256)
        rk, Trk = self.rstd_of(nr[:TP, 1:2], TP, Tnr, 128)
        self.DVE(lambda e: e.scalar_tensor_tensor(out=st[:TP, 0:256], in0=pj[:TP, 0:256], scalar=rq, in1=C["qn"][:TP, l, :],
                                                  op0=ALU.mult, op1=ALU.mult), [Tpj, Trq, self.TC], [Tst])
        self.DVE(lambda e: e.scalar_tensor_tensor(out=rows[:TP, 0:128], in0=pj[:TP, 256:384], scalar=rk, in1=C["kvn"][:TP, l, :],
                                                  op0=ALU.mult, op1=ALU.mult), [Tpj, Trk, self.TC], [Trows])
        self.evac_copy(st[:TP, 256:384], rows[:TP, 0:128], [Trows], [Tst])
        self.rope(pj[:TP, 384:416].unsqueeze(1), rows[:TP, 128:160].unsqueeze(1), 1, 32, self.tslot(t), TP, [Tpj], [Trows])
        self.evac_copy(st[:TP, 384:416], rows[:TP, 128:160], [Trows], [Tst])
        import os
        ML = int(os.environ.get("MLASTOP", "9"))
        if ML <= 1:
            return
        self.rows_out(rows, Trows, t, 160)
        if ML <= 2:
            return
        cq, Tcq = self.ring(self.CQT)
        self.transposes_to([st[:TP, 0:128], st[:TP, 128:256]], cq[:, :, :TP], TP, 128, [Tst], [Tcq])
        pq, Tpq = self.psum()
        for kc in range(2):
            self.PE(lambda e: e.matmul(pq[:TP, 0:384], cq[:, kc, :TP], self.WQB[:, kc, :], start=(kc == 0), stop=(kc == 1)),
                    [Tcq, self.TWM], [Tpq], sig=(kc == 1))
        if ML <= 4:
            return
        sq, Tsq = self.ring(self.STG)
        pqv = pq[:TP, 0:384].rearrange("p (h d) -> p h d", h=4)
        sqv = sq[:TP, 0:384].rearrange("p (h d) -> p h d", h=4)
        self.evac_copy(sqv[:, :, 0:64], pqv[:, :, 0:64], [Tpq], [Tsq])
        if ML <= 5:
            return
        self.rope(pqv[:, :, 64:96], sqv[:, :, 64:96], 4, 32, self.tslot(t), TP, [Tpq], [Tsq])
        if ML <= 6:
            return
        self.transposes_to([sqv[:, h, :] for h in range(4)], self.QT[0:96, 0:4, qslot * TP:(qslot + 1) * TP], TP, 96, [Tsq], [self.TQ])
        if ML <= 7:
            return
        self.mla_keys(st[:TP, 256:384], st[:TP, 384:416], TP, kt, kcol, [Tst])

    def tslot(self, t):
        return t if self.kind == "p" else 0

    def proj_dsa(self, t, qslot, kt, kcol, oslot):
        TP = self.TP
        (pa, Tpa), (pb, Tpb), (pc, Tpc) = self.project(t)
        rows, Trows = self.ring(self.ROWS)
        st, Tst = self.ring(self.STG)
        ts = self.tslot(t)
        self.rope(pa[:TP, 0:256].rearrange("p (h d) -> p h d", h=4), st[:TP, 0:256].rearrange("p (h d) -> p h d", h=4),
                  4, 64, ts, TP, [Tpa], [Tst])
        self.rope(pa[:TP, 256:320].unsqueeze(1), rows[:TP, 0:64].unsqueeze(1), 1, 64, ts, TP, [Tpa], [Trows])
        self.evac_copy(rows[:TP, 64:128], pa[:TP, 320:384], [Tpa], [Trows])
        self.evac_copy(self.V[:TP, kt, 0, 0:64], pa[:TP, 320:384], [Tpa], [self.TV[kt]])
        self.rope(pc[:TP, 0:64].unsqueeze(1), rows[:TP, 128:192].unsqueeze(1), 1, 64, ts, TP, [Tpc], [Trows])
        self.ACT(lambda e: e.activation(out=self.WAB[:TP, oslot, :], in_=pc[:TP, 64:72], func=AF.Abs, scale=8.0 ** -0.5),
                 [Tpc], [self.TWAB[oslot]])
        self.DVE(lambda e: e.tensor_scalar(out=self.WSG[:TP, oslot, :], in0=pc[:TP, 64:72], scalar1=0.0, scalar2=2.0,
                                           op0=ALU.is_ge, op1=ALU.mult), [Tpc], [self.TWAB[oslot]])
        self.DVE(lambda e: e.tensor_scalar(out=self.WSG[:TP, oslot, :], in0=self.WSG[:TP, oslot, :], scalar1=-1.0, scalar2=None,
                                           op0=ALU.add), [self.TWAB[oslot]], [self.TWAB[oslot]])
        self.rope(pb[:TP, 0:512].rearrange("p (h d) -> p h d", h=8), st[:TP, 256:768].rearrange("p (h d) -> p h d", h=8),
                  8, 64, ts, TP, [Tpb], [Tst])
        self.evac_copy(st[:TP, 768:832], rows[:TP, 0:64], [Trows], [Tst])
        self.evac_copy(st[:TP, 832:896], rows[:TP, 128:192], [Trows], [Tst])
        self.rows_out(rows, Trows, t, 192)
        qs = slice(qslot * TP, (qslot + 1) * TP)
        self.transposes_to([st[:TP, h * 64:(h + 1) * 64] for h in range(4)], self.QT[0:64, 0:4, qs], TP, 64, [Tst], [self.TQ])
        self.transposes_to([st[:TP, 256 + h * 64:256 + (h + 1) * 64] for h in range(8)], self.QT[0:64, 4:12, qs], TP, 64, [Tst], [self.TQ])
        self.transposes_to([st[:TP, 768:832], st[:TP, 832:896]], self.KT[0:64, 0:2, kcol:kcol + TP], TP, 64, [Tst], [self.TK[kt]])

    def proj_sb(self, t, qslot, kt, kcol, oslot):
        TP = self.TP
        (pa, Tpa), (pb, Tpb) = self.project(t)
        rows, Trows = self.ring(self.ROWS)
        st, Tst = self.ring(self.STG)
        rv = rows[:TP, :].rearrange("p (h d) -> p h d", h=4)
        self.evac_copy(st[:TP, 0:256], pa[:TP, 0:256], [Tpa], [Tst])
        self.evac_copy(rv[:, :, 0:64], pa[:TP, 256:512].rearrange("p (h d) -> p h d", h=4), [Tpa], [Trows])
        self.evac_copy(st[:TP, 256:512].rearrange("p (h d) -> p h d", h=4), pa[:TP, 256:512].rearrange("p (h d) -> p h d", h=4), [Tpa], [Tst])
        self.evac_copy(rv[:, :, 64:128], pb[:TP, 0:256].rearrange("p (h d) -> p h d", h=4), [Tpb], [Trows])
        self.evac_copy(self.V[:TP, kt, :, 0:64], pb[:TP, 0:256].rearrange("p (h d) -> p h d", h=4), [Tpb], [self.TV[kt]])
        self.rows_out(rows, Trows, t, 512)
        qs = slice(qslot * TP, (qslot + 1) * TP)
        self.transposes_to([st[:TP, h * 64:(h + 1) * 64] for h in range(4)], self.QT[0:64, 0:4, qs], TP, 64, [Tst], [self.TQ])
        self.transposes_to([st[:TP, 256 + h * 64:256 + (h + 1) * 64] for h in range(4)], self.KT[0:64, 0:4, kcol:kcol + TP], TP, 64, [Tst], [self.TK[kt]])

    def diff_lambda(self, l):
        lam = self.C["lam"]
        lam_init = 0.8 - 0.6 * math.exp(-0.3 * l)
        self.lam_init = lam_init
        LF, TL = self.LAMF, self.TLAM
        jk, Tjk = self.ring(self.RT)
        self.DVE(lambda e: e.scalar_tensor_tensor(out=jk[:, 0:32], in0=lam[:, l, 0:32], scalar=1.0, in1=lam[:, l, 32:64],
                                                  op0=ALU.mult, op1=ALU.mult, accum_out=LF[:, 0:1]), [self.TC], [Tjk, TL])
        self.DVE(lambda e: e.scalar_tensor_tensor(out=jk[:, 32:64], in0=lam[:, l, 64:96], scalar=1.0, in1=lam[:, l, 96:128],
                                                  op0=ALU.mult, op1=ALU.mult, accum_out=LF[:, 1:2]), [self.TC], [Tjk, TL])
        self.ACT(lambda e: e.activation(out=LF[:, 2:4], in_=LF[:, 0:2], func=AF.Exp), [TL], [TL])
        self.DVE(lambda e: e.scalar_tensor_tensor(out=LF[:, 4:5], in0=LF[:, 3:4], scalar=-lam_init, in1=LF[:, 2:3],
                                                  op0=ALU.add, op1=ALU.subtract), [TL], [TL])

    def proj_diff(self, t, qslot, kt, kcol, oslot):
        TP = self.TP
        (pa, Tpa), (pb, Tpb) = self.project(t)
        rows, Trows = self.ring(self.ROWS)
        st, Tst = self.ring(self.STG)
        ts = self.tslot(t)
        rv = rows[:TP, :].rearrange("p (h d) -> p h d", h=4)
        self.rope(pa[:TP, 0:256].rearrange("p (h d) -> p h d", h=8), st[:TP, 0:256].rearrange("p (h d) -> p h d", h=8),
                  8, 32, ts, TP, [Tpa], [Tst])
        kf, Tkf = self.ring(self.RT)
        self.rope(pa[:TP, 256:512].rearrange("p (h d) -> p h d", h=8), kf[:TP, 0:256].rearrange("p (h d) -> p h d", h=8),
                  8, 32, ts, TP, [Tpa], [Tkf])
        self.evac_copy(rv[:, :, 0:64], kf[:TP, 0:256].rearrange("p (h d) -> p h d", h=4), [Tkf], [Trows])
        self.evac_copy(st[:TP, 256:512], kf[:TP, 0:256], [Tkf], [Tst])
        self.evac_copy(rv[:, :, 64:128], pb[:TP, 0:256].rearrange("p (h d) -> p h d", h=4), [Tpb], [Trows])
        self.evac_copy(self.V[:TP, kt, :, 0:64], pb[:TP, 0:256].rearrange("p (h d) -> p h d", h=4), [Tpb], [self.TV[kt]])
        self.rows_out(rows, Trows, t, 512)
        qs = slice(qslot * TP, (qslot + 1) * TP)
        self.transposes_to([st[:TP, h * 64:(h + 1) * 64] for h in range(4)], self.QT[0:64, 0:4, qs], TP, 64, [Tst], [self.TQ])
        self.transposes_to([st[:TP, 256 + h * 64:256 + (h + 1) * 64] for h in range(4)], self.KT[0:64, 0:4, kcol:kcol + TP], TP, 64, [Tst], [self.TK[kt]])

    def load_past(self, s):
        cfg, k, m, l = self.cfg, self.k, self.m, self.l
        PAST = cfg.PAST
        npt = PAST // 128
        cache = self.din["c_" + m][l, s]
        cv = cache.rearrange("(t p) c -> p t c", p=128)
        if True:
            CST = self.CST
            for b0 in range(0, npt, 4):
                nb = min(4, npt - b0)
                c, Tc = self.ring(CST)
                if m == "mla":
                    k.dma(k.pool, c[:, 0:nb, :], cv[:, b0:b0 + nb, :], writes=[Tc])
                    for i in range(nb):
                        self.mla_keys(c[:, i, 0:128], c[:, i, 128:160], 128, b0 + i, (b0 + i) * 128, [Tc])
                elif m == "dsa":
                    k.dma(k.pool, c[:, 0:nb, 0, :], cv[:, b0:b0 + nb, 0:64], writes=[Tc])
                    k.dma(k.pool, c[:, 0:nb, 1, :], cv[:, b0:b0 + nb, 128:192], writes=[Tc])
                    k.dma(k.pool, self.V[:, b0:b0 + nb, 0, 0:64], cv[:, b0:b0 + nb, 64:128], writes=self.TV[b0:b0 + nb])
                    for st_ in range(2):
                        self.transposes_to([c[:, i, st_, :] for i in range(nb)],
                                           self.KT[0:64, st_, b0 * 128:(b0 + nb) * 128].rearrange("p (c t) -> p c t", c=nb),
                                           128, 64, [Tc], self.TK[b0:b0 + nb])
                else:
                    c4 = cv.rearrange("p t (h d) -> p t h d", h=4)
                    for i in range(nb):
                        k.dma(k.pool, c[:, i, :, :], c4[:, b0 + i, :, 0:64], writes=[Tc])
                        k.dma(k.pool, self.V[:, b0 + i, :, 0:64], c4[:, b0 + i, :, 64:128], writes=[self.TV[b0 + i]])
                    for h in range(4):
                        self.transposes_to([c[:, i, h, :] for i in range(nb)],
                                           self.KT[0:64, h, b0 * 128:(b0 + nb) * 128].rearrange("p (c t) -> p c t", c=nb),
                                           128, 64, [Tc], self.TK[b0:b0 + nb])

    def kt_info(self, kt):
        kcol = kt * 128
        return kcol, min(128, self.NK - kcol)

    def attend_group(self, t0, nt):
        m = self.m
        kts = []
        for kt in range(t0 + nt):
            j0 = max(0, kt - t0)
            kts.append((kt, j0, (j0 if kt >= t0 else None)))
        if m == "mla":
            for hp in range(0, 4, 2):
                self.softmax_multi([dict(streams=[dict(ks=h, qs=h, vs=h, p0=0, dk=96)], nsub=nt, TP=128, kts=kts, scale=96 ** -0.5,
                                         dest=self.O[:128, 0:nt, h * 64:(h + 1) * 64], Tdest=self.TO[0:nt], mask=self.C["m_chunk"],
                                         qoff=0, shared_k=False, mb=None) for h in (hp, hp + 1)])
        elif m == "diff":
            for h in range(4):
                od, Tod = self.ring(self.OD)
                self.softmax_multi([dict(streams=[dict(ks=h, qs=h, vs=h, p0=32 * mm, dk=32)], nsub=nt, TP=128, kts=kts, scale=32 ** -0.5,
                                         dest=od[:128, mm * 4:mm * 4 + nt, :], Tdest=[Tod], mask=self.C["m_chunk"],
                                         qoff=0, shared_k=False, mb=None) for mm in range(2)])
                self.diff_combine(od[:128, 0:nt, :], od[:128, 4:4 + nt, :], Tod, self.O[:128, 0:nt, h * 64:(h + 1) * 64],
                                  self.TO[0:nt], 128, nt)
        elif m == "sb":
            for hf in range(0, nt, 2):
                ns2 = min(2, nt - hf)
                kts2 = []
                for kt in range(t0 + hf + ns2):
                    j0 = max(0, kt - (t0 + hf))
                    kts2.append((kt, j0, (j0 if kt >= t0 + hf else None)))
                for h in range(4):
                    self.sb_unit([h], ns2, 128, kts2, dest=self.O[:128, hf:hf + ns2, h * 64:(h + 1) * 64],
                                 Tdest=self.TO[hf:hf + ns2], qoff=hf * 128)
        elif m == "dsa":
            def att(j):
                t = t0 + j
                ktj = [(kt, 0, None) for kt in range(t + 1)]
                self.softmax_unit([dict(ks=0, qs=h, vs=0, p0=0, dk=64) for h in range(4)], 1, 128, ktj, 64 ** -0.5,
                                  dest=self.O[:128, j, :].rearrange("p (h d) -> p h d", h=4), Tdest=[self.TO[j]],
                                  mask=None, qoff=j * 128, shared_k=True, mb=self.MBS[(t0 + j) % 2])
            self.dsa_select(0, 0, 128, (t0 + 1) * 128, prompt=True, mbuf=self.MBS[t0 % 2])
            for j in range(nt):
                if j + 1 < nt:
                    self.dsa_select(j + 1, j + 1, 128, (t0 + j + 2) * 128, prompt=True, mbuf=self.MBS[(t0 + j + 1) % 2])
                att(j)

    def attend_sample(self, s):
        m, TP = self.m, self.TP
        kts = [(kt, 0, None) for kt in range(self.NKT)]
        if m == "mla":
            self.softmax_unit([dict(ks=h, qs=h, vs=h, p0=0, dk=96) for h in range(4)], 1, TP, kts, 96 ** -0.5,
                              dest=self.O[:TP, s, :].rearrange("p (h d) -> p h d", h=4), Tdest=[self.TO[s]], mask=None)
        elif m == "diff":
            od, Tod = self.ring(self.OD)
            for mm in range(2):
                self.softmax_unit([dict(ks=h, qs=h, vs=h, p0=32 * mm, dk=32) for h in range(4)], 1, TP, kts, 32 ** -0.5,
                                  dest=od[:TP, mm * 4:mm * 4 + 4, :], Tdest=[Tod], mask=None)
            self.diff_combine(od[:TP, 0:4, :], od[:TP, 4:8, :], Tod, self.O[:TP, s, :].rearrange("p (h d) -> p h d", h=4),
                              [self.TO[s]], TP, 4)
        elif m == "sb":
            kts2 = [(kt, 0, (0 if kt == self.NKT - 1 else None)) for kt in range(self.NKT)]
            self.sb_unit([0, 1, 2, 3], 1, TP, kts2, dest=self.O[:TP, s, :].rearrange("p (h d) -> p h d", h=4), Tdest=[self.TO[s]])
        elif m == "dsa":
            self.dsa_select(0, s, TP, self.NK, prompt=False, mbuf=self.MBS[0])
            self.softmax_unit([dict(ks=0, qs=h, vs=0, p0=0, dk=64) for h in range(4)], 1, TP, kts, 64 ** -0.5,
                              dest=self.O[:TP, s, :].rearrange("p (h d) -> p h d", h=4), Tdest=[self.TO[s]],
                              mask=None, shared_k=True, mb=self.MBS[0])

    def zero_acc(self, acc, Tacc, rows, ncols):
        Z = self.C["zero"]
        self.PE(lambda e: e.matmul(acc[:rows, 0:ncols], Z[0:1, 0:rows], Z[0:1, 0:ncols], start=True, stop=False, skip_group_check=True),
                [self.TC], [Tacc], sig=False)

    def softmax_unit(self, streams, nsub, TP, kts, scale, dest, Tdest, mask, qoff=0, shared_k=False, mb=None):
        self.softmax_multi([dict(streams=streams, nsub=nsub, TP=TP, kts=kts, scale=scale, dest=dest, Tdest=Tdest,
                                 mask=mask, qoff=qoff, shared_k=shared_k, mb=mb)])

    def softmax_multi(self, units):
        for u in units:
            ns = len(u["streams"])
            u["ns"] = ns
            u["W"] = u["nsub"] * u["TP"]
            u["n"] = ns * u["nsub"]
            u["acc"], u["Tacc"] = self.acc_bank()
            self.zero_acc(u["acc"], u["Tacc"], u["TP"], u["n"] * 65)
            u["sc"] = {}
            u["pt"] = {}

        def stageA(u, i):
            kt, j0, dj = u["kts"][i]
            TP, W, ns, qoff, streams = u["TP"], u["W"], u["ns"], u["qoff"], u["streams"]
            kcol, nk = self.kt_info(kt)
            sc, Tsc = self.psum()
            u["sc"][i] = (sc, Tsc)
            c0 = j0 * TP
            if u["shared_k"]:
                s0 = streams[0]
                mb = u["mb"]
                self.PE(lambda e: e.matmul(sc[:nk, 0:ns * W].rearrange("p (s w) -> p s w", s=ns),
                                           self.KT[0:64, s0["ks"], kcol:kcol + nk],
                                           self.QT[0:64, 0:ns, qoff:qoff + W], start=True, stop=(mb is None)),
                        [self.TK[kt], self.TQ], [Tsc], sig=(mb is None))
                if mb is not None:
                    bi = self.C["bigi"][:TP, :].rearrange("p (s w) -> p s w", s=4)[:, 0:ns, 0:TP]
                    self.PE(lambda e: e.matmul(sc[:nk, 0:ns * W].rearrange("p (s w) -> p s w", s=ns),
                                               mb[0][:TP, kcol:kcol + nk], bi, start=False, stop=True),
                            [mb[1], self.TC], [Tsc])
            else:
                mask = u["mask"]
                for si, s_ in enumerate(streams):
                    p0, dk = s_["p0"], s_["dk"]
                    hasmask = (dj is not None and mask is not None)
                    last = (si == ns - 1)
                    self.PE(lambda e: e.matmul(sc[:nk, si * W + c0:(si + 1) * W], self.KT[p0:p0 + dk, s_["ks"], kcol:kcol + nk],
                                               self.QT[p0:p0 + dk, s_["qs"], qoff + c0:qoff + W], start=True, stop=(not hasmask)),
                            [self.TK[kt], self.TQ], [Tsc], sig=(last and not hasmask))
                    if hasmask:
                        d0 = si * W + dj * TP
                        self.PE(lambda e: e.matmul(sc[:nk, d0:d0 + TP], mask[:TP, :nk], self.C["bigi"][:TP, 0:TP],
                                                   start=False, stop=True), [self.TC], [Tsc], sig=last)

        def stageB(u, i):
            kt, j0, dj = u["kts"][i]
            TP, W, ns = u["TP"], u["W"], u["ns"]
            kcol, nk = self.kt_info(kt)
            sc, Tsc = u["sc"].pop(i)
            pt, Tpt = self.ring(self.PT)
            u["pt"][i] = (pt, Tpt)
            c0 = j0 * TP
            lo = c0 if ns == 1 else 0
            self.ACT(lambda e: e.activation(out=pt[:nk, lo:ns * W], in_=sc[:nk, lo:ns * W], func=AF.Exp, scale=u["scale"]),
                     [Tsc], [Tpt])

        def stageC(u, i):
            kt, j0, dj = u["kts"][i]
            TP, W, nsub = u["TP"], u["W"], u["nsub"]
            kcol, nk = self.kt_info(kt)
            pt, Tpt = u["pt"].pop(i)
            acc, Tacc = u["acc"], u["Tacc"]
            nst = len(u["streams"])
            for si, s_ in enumerate(u["streams"]):
                for j in range(j0, nsub):
                    a0 = (si * nsub + j) * 65
                    self.PE(lambda e: e.matmul(acc[:TP, a0:a0 + 65], pt[:nk, si * W + j * TP:si * W + (j + 1) * TP],
                                               self.V[:nk, kt, s_["vs"], :], start=False, stop=True, skip_group_check=True),
                            [Tpt, self.TV[kt]], [Tacc], sig=(si == nst - 1 and j == nsub - 1))

        maxlen = max(len(u["kts"]) for u in units)
        depth = 2 if len(units) == 1 else 1
        for u in units:
            for d_ in range(depth):
                if d_ < len(u["kts"]):
                    stageA(u, d_)
        for i in range(maxlen):
            for u in units:
                if i + depth < len(u["kts"]):
                    stageA(u, i + depth)
            for u in units:
                if i < len(u["kts"]):
                    stageB(u, i)
            for u in units:
                if i < len(u["kts"]):
                    stageC(u, i)
        for u in units:
            TP, n = u["TP"], u["n"]
            accv = u["acc"][:TP, 0:n * 65].rearrange("p (n d) -> p n d", n=n)
            nr, Tnr = self.ring(self.NRM)
            rcp = nr[:TP, 0:n]
            self.DVE(lambda e: e.reciprocal(out=rcp, in_=accv[:, :, 64]), [u["Tacc"]], [Tnr])
            self.DVE(lambda e: e.tensor_tensor(out=u["dest"], in0=accv[:, :, 0:64], in1=rcp.unsqueeze(2).to_broadcast([TP, n, 64]),
                                               op=ALU.mult), [u["Tacc"], Tnr], u["Tdest"])

    def diff_combine(self, o1, o2, Tod, dest, Tdest, TP, n):
        l = self.l
        a, Ta = self.ring(self.RT)
        b, Tb = self.ring(self.RT)
        av = a[:TP, 0:n * 64].rearrange("p (n d) -> p n d", n=n)
        bv = b[:TP, 0:n * 64].rearrange("p (n d) -> p n d", n=n)
        self.DVE(lambda e: e.scalar_tensor_tensor(out=av, in0=o2, scalar=self.LAMF[:TP, 4:5], in1=o1, op0=ALU.mult, op1=ALU.add),
                 [Tod, self.TLAM], [Ta])
        self.DVE(lambda e: e.tensor_tensor(out=bv, in0=av, in1=av, op=ALU.mult), [Ta], [Tb])
        nr, Tnr = self.ring(self.NRM)
        self.DVE(lambda e: e.tensor_reduce(out=nr[:TP, 0:n], in_=bv, axis=AX.X, op=ALU.add), [Tb], [Tnr])
        self.DVE(lambda e: e.tensor_scalar(out=nr[:TP, 0:n], in0=nr[:TP, 0:n], scalar1=1.0 / 64, scalar2=EPS, op0=ALU.mult, op1=ALU.add),
                 [Tnr], [Tnr])
        self.ACT(lambda e: e.activation(out=nr[:TP, 0:n], in_=nr[:TP, 0:n], func=AF.Ln), [Tnr], [Tnr])
        self.ACT(lambda e: e.activation(out=nr[:TP, 0:n], in_=nr[:TP, 0:n], func=AF.Exp, scale=-0.5), [Tnr], [Tnr])
        self.DVE(lambda e: e.tensor_tensor(out=bv, in0=av, in1=nr[:TP, 0:n].unsqueeze(2).to_broadcast([TP, n, 64]), op=ALU.mult),
                 [Ta, Tnr], [Tb])
        self.DVE(lambda e: e.scalar_tensor_tensor(out=dest, in0=bv, scalar=1.0 - self.lam_init,
                                                  in1=self.C["dn"][:TP, l, :].unsqueeze(1).to_broadcast([TP, n, 64]),
                                                  op0=ALU.mult, op1=ALU.mult), [Tb, self.TC], Tdest)

    def sb_unit(self, heads, nsub, TP, kts, dest, Tdest, qoff=0):
        ns = len(heads)
        W = nsub * TP
        NC = ns * W
        scale = 64 ** -0.5
        C = self.C
        SP, ES = self.SP, self.ES
        for (kt, j0, dj) in kts:
            kcol, nk = self.kt_info(kt)
            sc, Tsc = self.psum()
            c0 = j0 * TP
            for i, h in enumerate(heads):
                hasmask = dj is not None
                last = (i == ns - 1)
                self.PE(lambda e: e.matmul(sc[:nk, i * W + c0:(i + 1) * W], self.KT[0:64, h, kcol:kcol + nk],
                                           self.QT[0:64, h, qoff + c0:qoff + W], start=True, stop=(not hasmask)), [self.TK[kt], self.TQ], [Tsc],
                        sig=(last and not hasmask))
                if hasmask:
                    d0 = i * W + dj * TP
                    self.PE(lambda e: e.matmul(sc[:nk, d0:d0 + TP], C["m_strict"][:TP, :nk], C["bigi"][:TP, 0:TP],
                                               start=False, stop=True), [self.TC], [Tsc], sig=last)
            lo = c0 if ns == 1 else 0
            self.ACT(lambda e: e.activation(out=ES[:nk, kt, lo:NC], in_=sc[:nk, lo:NC], func=AF.Exp, scale=scale), [Tsc], [self.TSP[kt]])
            self.ACT(lambda e: e.activation(out=SP[:nk, kt, lo:NC], in_=ES[:nk, kt, lo:NC], func=AF.Ln, bias=1.0),
                     [self.TSP[kt]], [self.TSP[kt]])
        acc, Tacc = self.acc_bank()
        n = ns * nsub
        self.zero_acc(acc, Tacc, TP, n * 64)
        accv = acc[:TP, 0:n * 64].rearrange("p (n d) -> p n d", n=n)
        def cinc(idx):
            kt, j0, dj = kts[idx]
            kcol, nk = self.kt_info(kt)
            ci, Tci = self.psum()
            later = kts[idx:]
            for li, (kt2, j02, dj2) in enumerate(later):
                kcol2, nk2 = self.kt_info(kt2)
                lo2 = (j02 * TP) if ns == 1 else 0
                lhs = C["tri"][:nk2, :nk] if li == 0 else C["ones"][:nk2, :nk]
                self.PE(lambda e: e.matmul(ci[:nk, lo2:NC], lhs, SP[:nk2, kt2, lo2:NC], start=(li == 0), stop=(li == len(later) - 1)),
                        [self.TSP[kt2], self.TC], [Tci], sig=(li == len(later) - 1))
            return ci, Tci

        nxt = cinc(0)
        for idx, (kt, j0, dj) in enumerate(kts):
            kcol, nk = self.kt_info(kt)
            c0 = j0 * TP
            lo = c0 if ns == 1 else 0
            ci, Tci = nxt
            if idx + 1 < len(kts):
                nxt = cinc(idx + 1)
            pt, Tpt = self.ring(self.PT)
            self.ACT(lambda e: e.activation(out=pt[:nk, lo:NC], in_=ci[:nk, lo:NC], func=AF.Exp, scale=-1.0), [Tci], [Tpt])
            self.DVE(lambda e: e.tensor_tensor(out=pt[:nk, lo:NC], in0=pt[:nk, lo:NC], in1=ES[:nk, kt, lo:NC], op=ALU.mult),
                     [Tpt, self.TSP[kt]], [Tpt])
            for i, h in enumerate(heads):
                for j in range(j0, nsub):
                    a0 = (i * nsub + j) * 64
                    self.PE(lambda e: e.matmul(acc[:TP, a0:a0 + 64], pt[:nk, i * W + j * TP:i * W + (j + 1) * TP],
                                               self.V[:nk, kt, h, 0:64], start=False, stop=True, skip_group_check=True), [Tpt, self.TV[kt]], [Tacc],
                            sig=(i == ns - 1 and j == nsub - 1))
        self.evac_copy(dest, accv, [Tacc], Tdest)

    def dsa_select(self, qslot, wslot, TP, nkv, prompt, mbuf):
        IB, JK = self.IB, self.JK
        MB, TMB = mbuf
        qs = slice(qslot * TP, (qslot + 1) * TP)
        nsel = self.nsel
        for b0 in range(0, nkv, 512):
            nb = min(512, nkv - b0)
            for h in range(8):
                pi, Tpi = self.psum()
                kt_lo, kt_hi = b0 // 128, (b0 + nb + 127) // 128
                self.PE(lambda e: e.matmul(pi[:TP, 0:nb], self.QT[0:64, 4 + h, qs], self.KT[0:64, 1, b0:b0 + nb], start=True, stop=True),
                        [self.TQ] + self.TK[kt_lo:kt_hi], [Tpi])
                rb, Trb = self.ring(self.RB)
                self.ACT(lambda e: e.activation(out=rb[:TP, 0:nb], in_=pi[:TP, 0:nb], func=AF.Relu, scale=self.WAB[:TP, wslot, h:h + 1]),
                         [Tpi, self.TWAB[wslot]], [Trb])
                if h == 0:
                    self.DVE(lambda e: e.tensor_scalar(out=IB[:TP, b0:b0 + nb], in0=rb[:TP, 0:nb], scalar1=self.WSG[:TP, wslot, 0:1],
                                                       scalar2=None, op0=ALU.mult), [Trb, self.TWAB[wslot]], [self.TIB])
                else:
                    self.DVE(lambda e: e.scalar_tensor_tensor(out=IB[:TP, b0:b0 + nb], in0=rb[:TP, 0:nb], scalar=self.WSG[:TP, wslot, h:h + 1],
                                                              in1=IB[:TP, b0:b0 + nb], op0=ALU.mult, op1=ALU.add),
                             [Trb, self.TWAB[wslot], self.TIB], [self.TIB])
        bs, Tbs = self.ring(self.BIS)
        NIT = 20
        need_search = (nkv > nsel)
        if need_search:
            self.DVE(lambda e: e.tensor_reduce(out=bs[:TP, 0:1], in_=IB[:TP, 0:nkv], axis=AX.X, op=ALU.max), [self.TIB], [Tbs])
            self.DVE(lambda e: e.tensor_reduce(out=bs[:TP, 1:2], in_=IB[:TP, 0:nkv], axis=AX.X, op=ALU.min), [self.TIB], [Tbs])
        if prompt:
            self.DVE(lambda e: e.memset(IB[0:64, nkv - 64:nkv], -3.0e38), [self.TIB], [self.TIB])
        if not need_search:
            self.DVE(lambda e: e.memset(bs[:TP, 4:5], -1.0e38), [], [Tbs])
            thr = bs[:TP, 4:5]
        else:
            self.DVE(lambda e: e.tensor_tensor(out=bs[:TP, 2:3], in0=bs[:TP, 0:1], in1=bs[:TP, 1:2], op=ALU.subtract), [Tbs], [Tbs])
            self.DVE(lambda e: e.tensor_scalar(out=bs[:TP, 2:3], in0=bs[:TP, 2:3], scalar1=1.0 + 2.0 ** -10, scalar2=1e-30,
                                               op0=ALU.mult, op1=ALU.add), [Tbs], [Tbs])
            steps = bs[:TP, 8:8 + NIT + 1]
            self.DVE(lambda e: e.tensor_scalar(out=steps, in0=self.C["pow2"][:TP, 0:NIT + 1], scalar1=bs[:TP, 2:3], scalar2=None,
                                               op0=ALU.mult), [Tbs, self.TC], [Tbs])
            self.DVE(lambda e: e.tensor_tensor(out=bs[:TP, 4:5], in0=bs[:TP, 1:2], in1=bs[:TP, 8:9], op=ALU.add), [Tbs], [Tbs])
            cand = bs[:TP, 4:5]
            for it in range(1, NIT + 1):
                self.DVE(lambda e: e.tensor_scalar(out=JK[:TP, 0:nkv], in0=IB[:TP, 0:nkv], scalar1=cand, scalar2=0.0,
                                                   op0=ALU.is_ge, op1=ALU.add, accum_out=bs[:TP, 5:6]), [self.TIB, Tbs], [self.TJK, Tbs])
                self.DVE(lambda e: e.tensor_scalar(out=bs[:TP, 6:7], in0=bs[:TP, 5:6], scalar1=float(nsel) - 0.5,
                                                   scalar2=bs[:TP, 8 + it - 1:8 + it], op0=ALU.is_ge, op1=ALU.mult), [Tbs], [Tbs])
                nxt = it if it < NIT else it - 1
                self.DVE(lambda e: e.scalar_tensor_tensor(out=cand, in0=cand, scalar=bs[:TP, 8 + nxt:8 + nxt + 1], in1=bs[:TP, 6:7],
                                                          op0=ALU.subtract, op1=ALU.add), [Tbs], [Tbs])
            thr = cand
        self.DVE(lambda e: e.tensor_scalar(out=MB[:TP, 0:nkv], in0=IB[:TP, 0:nkv], scalar1=thr, scalar2=1.0,
                                           op0=ALU.is_ge, op1=ALU.subtract), [self.TIB, Tbs], [TMB])

    def out_proj(self, tiles, oslots):
        TP = self.TP
        for t, os_ in zip(tiles, oslots):
            col = os_ * TP
            self.transposes_to([self.O[:TP, os_, 0:128], self.O[:TP, os_, 128:256]], self.OT[:, :, col:col + TP], TP, 128,
                               [self.TO[os_]], [self.TOT])
            for h in range(2):
                po, Tpo = self.psum()
                for kc in range(2):
                    self.PE(lambda e: e.matmul(po[:TP, :], self.OT[:, kc, col:col + TP], self.WO[:, kc, h * 512:(h + 1) * 512],
                                               start=(kc == 0), stop=(kc == 1)), [self.TOT, self.TWO], [Tpo], sig=(kc == 1))
                xs = self.X[:TP, t, h * 512:(h + 1) * 512]
                self.DVE(lambda e: e.tensor_tensor(out=xs, in0=xs, in1=po[:TP, :], op=ALU.add), [Tpo, self.TX[t]], [self.TX[t]])


def shard_inputs(cfg, inputs, n_cores):
    consts = host_consts(cfg)
    per = []
    f = lambda a: np.ascontiguousarray(np.asarray(a, dtype=np.float32))
    L = cfg.DEPTH
    shared = {
        "fnorm": f(inputs["final_norm"]),
        "w_in": f(inputs["w_in"]), "qn": f(inputs["mla_q_norm"]), "wqb": f(inputs["mla_w_qb"]),
        "kvn": f(inputs["mla_kv_norm"]), "wkvb": f(inputs["mla_w_kvb"]),
        "lam": f(np.asarray(inputs["diff_lambda"]).reshape(L, 128)), "dn": f(inputs["diff_norm"]),
        "wo": f(inputs["w_out"]),
        "wg1": f(inputs["w_ff1_gate"]), "wu1": f(inputs["w_ff1_up"]), "wd1": f(inputs["w_ff1_down"]),
        "wg2": f(inputs["w_ff2_gate"]), "wu2": f(inputs["w_ff2_up"]), "wd2": f(inputs["w_ff2_down"]),
    }
    for nm, src in (("n1", "norm_ff1"), ("nm", "norm_mix"), ("n2", "norm_ff2")):
        shared[nm] = f(np.asarray(inputs[src]).reshape(L, 8, 128).transpose(0, 2, 1))
    for nm, a in consts.items():
        shared["k_" + nm] = f(a)
    for c in range(n_cores):
        d = dict(shared)
        ps = slice(c * cfg.NPS, (c + 1) * cfg.NPS)
        ss = slice(c * cfg.NSS, (c + 1) * cfg.NSS)
        d["xp"] = f(inputs["x_prompt"][ps])
        d["xs"] = f(inputs["x_sample"][ss])
        d["c_mla"] = f(inputs["cache_mla"][:, ss])
        d["c_dsa"] = f(inputs["cache_dsa"][:, ss])
        d["c_sb"] = f(np.asarray(inputs["cache_sb"])[:, ss].reshape(L, cfg.NSS, cfg.PAST, 512))
        d["c_diff"] = f(np.asarray(inputs["cache_diff"])[:, ss].reshape(L, cfg.NSS, cfg.PAST, 512))
        per.append(d)
    return per


def gather_outputs(cfg, results, n_cores):
    L = cfg.DEPTH
    cat = lambda name, ax: np.concatenate([np.asarray(r[name]) for r in results], axis=ax)
    yp = cat("yp", 0)
    ys = cat("ys", 0)
    outs = [yp, ys]
    B = cfg.NPS * n_cores
    Bs = cfg.NSS * n_cores
    outs.append(cat("mla_p", 1))
    outs.append(cat("dsa_p", 1))
    outs.append(cat("sb_p", 1).reshape(L, B, cfg.S, 4, 128))
    outs.append(cat("diff_p", 1).reshape(L, B, cfg.S, 4, 128))
    outs.append(cat("mla_s", 1))
    outs.append(cat("dsa_s", 1))
    outs.append(cat("sb_s", 1).reshape(L, Bs, cfg.TS, 4, 128))
    outs.append(cat("diff_s", 1).reshape(L, Bs, cfg.TS, 4, 128))
    return tuple(np.ascontiguousarray(o.astype(np.float32, copy=False)) for o in outs)


def run(cfg, inputs, n_cores=8):
    b = Builder(cfg)
    nc = b.build()
    in_maps = shard_inputs(cfg, inputs, n_cores)
    res = run_bass_kernel_spmd(nc, in_maps, core_ids=list(range(n_cores)))
    return gather_outputs(cfg, res.results, n_cores)


def kernel(**inputs):
    cfg = Cfg()
    return run(cfg, inputs, 8)
```
